# Optimizing a Trainium2 kernel written in Bass

```python
import math
import jax, jax.numpy as jnp
from jax import lax
import numpy as np

D_MODEL = 1024
BATCH = 32
SEQ = 256
DEPTH = 4
DEC_BATCH = 8
DEC_SEQ = 4096
PAST_LEN = 512

GRID_W = 64
N_MIXERS = 4
QB = 128
ROPE_BASE = 10000.0
LN_EPS = 1e-5
NEG_INF = -1e30
ALPHA = (2 * DEPTH) ** 0.25
BETA = (8 * DEPTH) ** -0.25
D_FF = 4 * D_MODEL
DIFF_DH = 64
DIFF_HEADS = D_MODEL // (2 * DIFF_DH)
DIFF_LAMBDA_INIT = 0.8 - 0.6 * math.exp(-0.3 * 0)
NA_DH = 64
NA_HEADS = D_MODEL // NA_DH
NA_KH = 8
NA_KW = 16
MLA_HEADS = 16
MLA_NOPE = 64
MLA_ROPE = 32
MLA_VDIM = 64
MLA_Q_RANK = D_MODEL // 2
MLA_KV_RANK = D_MODEL // 4
SWA_DH = 64
SWA_HEADS = D_MODEL // SWA_DH
SWA_KV_HEADS = 4
SWA_WINDOW = 128

kernel_name = 'hybrid_diffusion_interleaved_step'


def layer_norm(x, g, b):
    xf = x.astype(jnp.float32)
    mu = jnp.mean(xf, -1, keepdims=True)
    var = jnp.mean(jnp.square(xf - mu), -1, keepdims=True)
    return ((xf - mu) * lax.rsqrt(var + LN_EPS) * g + b).astype(x.dtype)


def rms_norm(x, g):
    xf = x.astype(jnp.float32)
    return (xf * lax.rsqrt(jnp.mean(xf * xf, -1, keepdims=True) + LN_EPS) * g).astype(x.dtype)


def axial_rope(x):
    T, R = x.shape[1], x.shape[-1]
    n = R // 4
    t = jnp.arange(T)
    inv = ROPE_BASE ** (-jnp.arange(n, dtype=jnp.float32) / n)
    bshape = (T,) + (1,) * (x.ndim - 3) + (n,)

    def rot(xh, pos):
        ang = (pos.astype(jnp.float32)[:, None] * inv).reshape(bshape)
        cos, sin = jnp.cos(ang).astype(x.dtype), jnp.sin(ang).astype(x.dtype)
        x1, x2 = xh[..., :n], xh[..., n:]
        return jnp.concatenate([x1 * cos - x2 * sin, x2 * cos + x1 * sin], -1)

    half = R // 2
    return jnp.concatenate([rot(x[..., :half], t // GRID_W), rot(x[..., half:], t % GRID_W)], -1)


def map_query_blocks(fn, *qs):
    B, T = qs[0].shape[:2]
    nb = T // QB
    blocks = tuple(jnp.moveaxis(a.reshape((B, nb, QB) + a.shape[2:]), 1, 0) for a in qs)
    out = lax.map(lambda args: fn(*args), blocks)
    return jnp.moveaxis(out, 0, 1).reshape((B, T) + out.shape[3:])


def dense_attend(q, k, v, sink=None):
    B, T, Hq, dh = q.shape
    Hk = k.shape[2]
    G = Hq // Hk
    scale = dh ** -0.5

    def block(qb):
        qg = qb.reshape(B, QB, Hk, G, dh)
        logits = jnp.einsum('bqkgd,bskd->bkgqs', qg, k, preferred_element_type=jnp.float32) * scale
        if sink is not None:
            s = jnp.broadcast_to(sink.astype(jnp.float32).reshape(1, Hk, G, 1, 1), logits.shape[:-1] + (1,))
            p = jax.nn.softmax(jnp.concatenate([logits, s], -1), -1)[..., :-1]
        else:
            p = jax.nn.softmax(logits, -1)
        o = jnp.einsum('bkgqs,bskd->bqkgd', p, v)
        return o.reshape(B, QB, Hq, dh).astype(v.dtype)

    return map_query_blocks(block, q)


def modulation(cond, w, b):
    m = jax.nn.silu(cond) @ w + b
    return m.reshape(m.shape[:-1] + (6, D_MODEL))


def modulate(x, shift, scale):
    return x * (1 + scale) + shift


def sq_relu_mlp(h, w1, w2):
    return jnp.square(jax.nn.relu(h @ w1)) @ w2


def diff_qkv(h, w_qkv):
    B, T, _ = h.shape
    q, k, v = jnp.split(h @ w_qkv, 3, -1)
    q = q.reshape(B, T, DIFF_HEADS, 2, DIFF_DH)
    k = k.reshape(B, T, DIFF_HEADS, 2, DIFF_DH)
    v = v.reshape(B, T, DIFF_HEADS, 2 * DIFF_DH)
    return q, k, v


def diff_lambda(lam):
    lf = lam.astype(jnp.float32)
    return jnp.exp(jnp.sum(lf[0] * lf[1])) - jnp.exp(jnp.sum(lf[2] * lf[3])) + DIFF_LAMBDA_INIT


def diff_attend(q, k, v, lam, subln):
    B, T = q.shape[:2]
    scale = DIFF_DH ** -0.5

    def block(qb):
        logits = jnp.einsum('bqhmd,bkhmd->bhmqk', qb, k, preferred_element_type=jnp.float32) * scale
        p = jax.nn.softmax(logits, -1)
        a = p[:, :, 0] - lam * p[:, :, 1]
        return jnp.einsum('bhqk,bkhe->bqhe', a, v).astype(v.dtype)

    o = map_query_blocks(block, q)
    o = rms_norm(o, subln) * (1.0 - DIFF_LAMBDA_INIT)
    return o.reshape(B, T, D_MODEL)


def diff_ctx(h, w_qkv, lam, subln, w_o):
    B, S, _ = h.shape
    q, k, v = diff_qkv(h, w_qkv)
    o = diff_attend(q, k, v, diff_lambda(lam), subln)
    return o @ w_o, (k.reshape(B, S, DIFF_HEADS, 2 * DIFF_DH), v)


def diff_lat(h, ck, cv, w_qkv, lam, subln, w_o):
    B, P = ck.shape[:2]
    q, k, v = diff_qkv(h, w_qkv)
    q, k = axial_rope(q), axial_rope(k)
    k_all = jnp.concatenate([ck.reshape(B, P, DIFF_HEADS, 2, DIFF_DH), k], 1)
    v_all = jnp.concatenate([cv, v], 1)
    return diff_attend(q, k_all, v_all, diff_lambda(lam), subln) @ w_o


def na_qkv(h, w_qkv):
    B, T, _ = h.shape
    q, k, v = jnp.split(h @ w_qkv, 3, -1)
    shp = (B, T, NA_HEADS, NA_DH)
    return q.reshape(shp), k.reshape(shp), v.reshape(shp)


def na_ctx(h, w_qkv, rpb, w_o):
    B, S, _ = h.shape
    q, k, v = na_qkv(h, w_qkv)
    o = dense_attend(q, k, v)
    return o.reshape(B, S, D_MODEL) @ w_o, (k, v)


def na_lat(h, ck, cv, w_qkv, rpb, w_o):
    B, T, _ = h.shape
    rows = T // GRID_W
    kh = min(NA_KH, rows)
    q, k, v = na_qkv(h, w_qkv)
    grid = lambda a: a.reshape(B, rows, GRID_W, NA_HEADS, NA_DH)
    qg, kg, vg = grid(q), grid(k), grid(v)
    cols = np.arange(GRID_W)
    c0 = np.clip(cols - NA_KW // 2, 0, GRID_W - NA_KW)
    col_idx = c0[:, None] + np.arange(NA_KW)
    col_rel = col_idx - cols[:, None] + NA_KW - 1
    n_nb = kh * NA_KW
    scale = NA_DH ** -0.5

    def row_block(args):
        r, qr = args
        r0 = jnp.clip(r - kh // 2, 0, rows - kh)
        ks = lax.dynamic_slice_in_dim(kg, r0, kh, axis=1)[:, :, col_idx]
        vs = lax.dynamic_slice_in_dim(vg, r0, kh, axis=1)[:, :, col_idx]
        row_rel = r0 - r + jnp.arange(kh) + NA_KH - 1
        bias = rpb[:, row_rel][:, :, col_rel]
        ln = jnp.einsum('bchd,bicjhd->bhcij', qr, ks, preferred_element_type=jnp.float32) * scale
        ln = ln + jnp.transpose(bias, (0, 2, 1, 3)).astype(jnp.float32)[None]
        lc = jnp.einsum('bchd,bshd->bhcs', qr, ck, preferred_element_type=jnp.float32) * scale
        p = jax.nn.softmax(jnp.concatenate([ln.reshape(B, NA_HEADS, GRID_W, n_nb), lc], -1), -1)
        pn = p[..., :n_nb].reshape(B, NA_HEADS, GRID_W, kh, NA_KW)
        o = jnp.einsum('bhcij,bicjhd->bchd', pn, vs) + jnp.einsum('bhcs,bshd->bchd', p[..., n_nb:], cv)
        return o.astype(h.dtype)

    o = lax.map(row_block, (jnp.arange(rows), jnp.moveaxis(qg, 1, 0)))
    return jnp.moveaxis(o, 0, 1).reshape(B, T, D_MODEL) @ w_o


def mla_project(h, w_a, q_norm, kv_norm, w_uq):
    B, T, _ = h.shape
    a = h @ w_a
    cq = a[..., :MLA_Q_RANK]
    ckv = a[..., MLA_Q_RANK:MLA_Q_RANK + MLA_KV_RANK]
    kpe = a[..., MLA_Q_RANK + MLA_KV_RANK:]
    q = (rms_norm(cq, q_norm) @ w_uq).reshape(B, T, MLA_HEADS, MLA_NOPE + MLA_ROPE)
    return q[..., :MLA_NOPE], q[..., MLA_NOPE:], rms_norm(ckv, kv_norm), kpe


def mla_expand(ckv, w_ukv):
    B, T, _ = ckv.shape
    kv = (ckv @ w_ukv).reshape(B, T, MLA_HEADS, MLA_NOPE + MLA_VDIM)
    return kv[..., :MLA_NOPE], kv[..., MLA_NOPE:]


def mla_attend(qn, qp, kn, kp, v):
    scale = (MLA_NOPE + MLA_ROPE) ** -0.5

    def block(qnb, qpb):
        logits = (jnp.einsum('bqhd,bkhd->bhqk', qnb, kn, preferred_element_type=jnp.float32)
                  + jnp.einsum('bqhr,bkr->bhqk', qpb, kp, preferred_element_type=jnp.float32)) * scale
        p = jax.nn.softmax(logits, -1)
        return jnp.einsum('bhqk,bkhd->bqhd', p, v).astype(v.dtype)

    return map_query_blocks(block, qn, qp)


def mla_ctx(h, w_a, q_norm, kv_norm, w_uq, w_ukv, w_o):
    B, S, _ = h.shape
    qn, qp, ckv, kpe = mla_project(h, w_a, q_norm, kv_norm, w_uq)
    kn, v = mla_expand(ckv, w_ukv)
    o = mla_attend(qn, qp, kn, kpe, v)
    return o.reshape(B, S, MLA_HEADS * MLA_VDIM) @ w_o, (ckv, kpe)


def mla_lat(h, c_ckv, c_kpe, w_a, q_norm, kv_norm, w_uq, w_ukv, w_o):
    B, T, _ = h.shape
    qn, qp, ckv, kpe = mla_project(h, w_a, q_norm, kv_norm, w_uq)
    qp = axial_rope(qp)
    kpe = axial_rope(kpe[:, :, None, :])[:, :, 0]
    kn, v = mla_expand(jnp.concatenate([c_ckv, ckv], 1), w_ukv)
    o = mla_attend(qn, qp, kn, jnp.concatenate([c_kpe, kpe], 1), v)
    return o.reshape(B, T, MLA_HEADS * MLA_VDIM) @ w_o


def swa_qkv(h, w_qkv):
    B, T, _ = h.shape
    a = h @ w_qkv
    nq, nk = SWA_HEADS * SWA_DH, SWA_KV_HEADS * SWA_DH
    q = a[..., :nq].reshape(B, T, SWA_HEADS, SWA_DH)
    k = a[..., nq:nq + nk].reshape(B, T, SWA_KV_HEADS, SWA_DH)
    v = a[..., nq + nk:].reshape(B, T, SWA_KV_HEADS, SWA_DH)
    return q, k, v


def swa_ctx(h, w_qkv, sink, w_o):
    B, S, _ = h.shape
    q, k, v = swa_qkv(h, w_qkv)
    o = dense_attend(q, k, v, sink)
    return o.reshape(B, S, SWA_HEADS * SWA_DH) @ w_o, (k, v)


def swa_lat(h, ck, cv, w_qkv, sink, w_o):
    B, T, _ = h.shape
    P = ck.shape[1]
    nb = T // QB
    G = SWA_HEADS // SWA_KV_HEADS
    scale = SWA_DH ** -0.5
    q, k, v = swa_qkv(h, w_qkv)
    q, k = axial_rope(q), axial_rope(k)

    def band(a):
        ab = a.reshape((B, nb, QB) + a.shape[2:])
        ap = jnp.pad(ab, [(0, 0), (1, 1)] + [(0, 0)] * (ab.ndim - 2))
        return jnp.moveaxis(jnp.concatenate([ap[:, :-2], ap[:, 1:-1], ap[:, 2:]], axis=2), 1, 0)

    qi = np.arange(QB)[:, None]
    kj = np.arange(3 * QB)[None, :]
    kpos = (np.arange(nb)[:, None, None] - 1) * QB + kj[None]
    mask = (np.abs(kj - QB - qi) <= SWA_WINDOW)[None] & (kpos >= 0) & (kpos < T)
    sink_l = sink.astype(jnp.float32).reshape(1, SWA_KV_HEADS, G, 1, 1)

    def block(args):
        qb, kb, vb, mb = args
        qg = qb.reshape(B, QB, SWA_KV_HEADS, G, SWA_DH)
        lw = jnp.einsum('bqkgd,bskd->bkgqs', qg, kb, preferred_element_type=jnp.float32) * scale
        lw = jnp.where(mb, lw, NEG_INF)
        lc = jnp.einsum('bqkgd,bskd->bkgqs', qg, ck, preferred_element_type=jnp.float32) * scale
        ls = jnp.broadcast_to(sink_l, lw.shape[:-1] + (1,))
        p = jax.nn.softmax(jnp.concatenate([lw, lc, ls], -1), -1)
        o = (jnp.einsum('bkgqs,bskd->bqkgd', p[..., :3 * QB], vb)
             + jnp.einsum('bkgqs,bskd->bqkgd', p[..., 3 * QB:3 * QB + P], cv))
        return o.reshape(B, QB, SWA_HEADS, SWA_DH).astype(h.dtype)

    qb = jnp.moveaxis(q.reshape(B, nb, QB, SWA_HEADS, SWA_DH), 1, 0)
    o = lax.map(block, (qb, band(k), band(v), jnp.asarray(mask)))
    return jnp.moveaxis(o, 0, 1).reshape(B, T, SWA_HEADS * SWA_DH) @ w_o


def setup_inputs(seed: int = 0) -> dict:
    key = jax.random.key(seed)
    ks = iter(jax.random.split(key, 48))

    def nrm(shape, scale=1.0):
        return jax.random.normal(next(ks), shape, jnp.float32) * scale

    D = D_MODEL
    inv = D ** -0.5
    return {
        'x_prompt': nrm((BATCH, SEQ, D)),
        'x_sample': nrm((DEC_BATCH, DEC_SEQ, D)),
        'cache_l0_k': nrm((DEC_BATCH, PAST_LEN, DIFF_HEADS, 2 * DIFF_DH)),
        'cache_l0_v': nrm((DEC_BATCH, PAST_LEN, DIFF_HEADS, 2 * DIFF_DH)),
        'cache_l1_k': nrm((DEC_BATCH, PAST_LEN, NA_HEADS, NA_DH)),
        'cache_l1_v': nrm((DEC_BATCH, PAST_LEN, NA_HEADS, NA_DH)),
        'cache_l2_ckv': nrm((DEC_BATCH, PAST_LEN, MLA_KV_RANK)),
        'cache_l2_kpe': nrm((DEC_BATCH, PAST_LEN, MLA_ROPE)),
        'cache_l3_k': nrm((DEC_BATCH, PAST_LEN, SWA_KV_HEADS, SWA_DH)),
        'cache_l3_v': nrm((DEC_BATCH, PAST_LEN, SWA_KV_HEADS, SWA_DH)),
        'c': nrm((DEC_BATCH, D)),
        'c_ctx': nrm((D,)),
        'w_mod': nrm((DEPTH, D, 6 * D), inv),
        'b_mod': nrm((DEPTH, 6 * D), 0.02),
        'ln_g': 1.0 + nrm((DEPTH, 2, D), 0.02),
        'ln_b': nrm((DEPTH, 2, D), 0.02),
        'w_mlp1': nrm((DEPTH, D, D_FF), inv),
        'w_mlp2': nrm((DEPTH, D_FF, D), BETA * D_FF ** -0.5),
        'l0_w_qkv': nrm((D, 3 * D), inv),
        'l0_lam': nrm((4, DIFF_DH), 0.1),
        'l0_subln': 1.0 + nrm((2 * DIFF_DH,), 0.02),
        'l0_w_o': nrm((D, D), BETA * inv),
        'l1_w_qkv': nrm((D, 3 * D), inv),
        'l1_rpb': nrm((NA_HEADS, 2 * NA_KH - 1, 2 * NA_KW - 1), 0.1),
        'l1_w_o': nrm((D, D), BETA * inv),
        'l2_w_a': nrm((D, MLA_Q_RANK + MLA_KV_RANK + MLA_ROPE), inv),
        'l2_q_norm': 1.0 + nrm((MLA_Q_RANK,), 0.02),
        'l2_kv_norm': 1.0 + nrm((MLA_KV_RANK,), 0.02),
        'l2_w_uq': nrm((MLA_Q_RANK, MLA_HEADS * (MLA_NOPE + MLA_ROPE)), MLA_Q_RANK ** -0.5),
        'l2_w_ukv': nrm((MLA_KV_RANK, MLA_HEADS * (MLA_NOPE + MLA_VDIM)), MLA_KV_RANK ** -0.5),
        'l2_w_o': nrm((MLA_HEADS * MLA_VDIM, D), BETA * (MLA_HEADS * MLA_VDIM) ** -0.5),
        'l3_w_qkv': nrm((D, (SWA_HEADS + 2 * SWA_KV_HEADS) * SWA_DH), inv),
        'l3_sink': nrm((SWA_HEADS,), 0.5),
        'l3_w_o': nrm((SWA_HEADS * SWA_DH, D), BETA * (SWA_HEADS * SWA_DH) ** -0.5),
    }


def reference(x_prompt, x_sample, cache_l0_k, cache_l0_v, cache_l1_k, cache_l1_v, cache_l2_ckv,
              cache_l2_kpe, cache_l3_k, cache_l3_v, c, c_ctx, w_mod, b_mod, ln_g, ln_b, w_mlp1, w_mlp2,
              l0_w_qkv, l0_lam, l0_subln, l0_w_o, l1_w_qkv, l1_rpb, l1_w_o, l2_w_a, l2_q_norm,
              l2_kv_norm, l2_w_uq, l2_w_ukv, l2_w_o, l3_w_qkv, l3_sink, l3_w_o):
    ctx_mixers = (diff_ctx, na_ctx, mla_ctx, swa_ctx)
    lat_mixers = (diff_lat, na_lat, mla_lat, swa_lat)
    params = ((l0_w_qkv, l0_lam, l0_subln, l0_w_o),
              (l1_w_qkv, l1_rpb, l1_w_o),
              (l2_w_a, l2_q_norm, l2_kv_norm, l2_w_uq, l2_w_ukv, l2_w_o),
              (l3_w_qkv, l3_sink, l3_w_o))
    caches = ((cache_l0_k, cache_l0_v), (cache_l1_k, cache_l1_v),
              (cache_l2_ckv, cache_l2_kpe), (cache_l3_k, cache_l3_v))
    new_state = []
    xp, xs = x_prompt, x_sample
    for i in range(DEPTH):
        m = i % N_MIXERS
        mp = modulation(c_ctx, w_mod[i], b_mod[i])
        ms = modulation(c, w_mod[i], b_mod[i])[:, :, None, :]
        y, state = ctx_mixers[m](modulate(xp, mp[0], mp[1]), *params[i])
        xp = layer_norm(ALPHA * xp + mp[2] * y, ln_g[i, 0], ln_b[i, 0])
        y = sq_relu_mlp(modulate(xp, mp[3], mp[4]), w_mlp1[i], w_mlp2[i])
        xp = layer_norm(ALPHA * xp + mp[5] * y, ln_g[i, 1], ln_b[i, 1])
        new_state.extend(state)
        y = lat_mixers[m](modulate(xs, ms[:, 0], ms[:, 1]), *caches[i], *params[i])
        xs = layer_norm(ALPHA * xs + ms[:, 2] * y, ln_g[i, 0], ln_b[i, 0])
        y = sq_relu_mlp(modulate(xs, ms[:, 3], ms[:, 4]), w_mlp1[i], w_mlp2[i])
        xs = layer_norm(ALPHA * xs + ms[:, 5] * y, ln_g[i, 1], ln_b[i, 1])
    return (xp, xs, *new_state)
```

```python
import math
from contextlib import ExitStack
import numpy as np
import concourse.bass as bass
import concourse.mybir as mybir
from concourse.bass_utils import run_bass_kernel_spmd

F32 = mybir.dt.float32
BF16 = mybir.dt.bfloat16
AF = mybir.ActivationFunctionType
ALU = mybir.AluOpType
AX = mybir.AxisListType

NLAYERS = 4
D = 1024
NPR = 1024
NLAT = 4096
NTOK = NPR + NLAT
PAST = 512
NKEY = NPR + PAST + NLAT
LKEY0 = NPR
LNEW0 = NPR + PAST
ALPHA = 8.0 ** 0.25
LN_EPS = 1e-5
EPS_LN = LN_EPS / (ALPHA * ALPHA)
LAM_INIT = 0.8 - 0.6 * math.exp(-0.3 * 0)
GRID_W = 64
SEM_ROLL = 30000


class Sem:
    __slots__ = ("h", "total", "dma", "id")

    def __init__(self, h, dma, i):
        self.h = h
        self.total = 0
        self.dma = dma
        self.id = i


class Tk:
    __slots__ = ("name", "w", "r", "dsem", "t", "psum")

    def __init__(self, name, t=None):
        self.psum = False
        self.name = name
        self.w = None
        self.r = {}
        self.dsem = None
        self.t = t

    def __getitem__(self, idx):
        return self.t[idx]


class Eng:
    def __init__(self, kb, name, h, compute=True):
        self.kb = kb
        self.name = name
        self.h = h
        self.sem = kb.newsem(False)
        self.waited = {}

    def wait(self, sem, val):
        if sem.dma:
            val = sem.total
        if val <= 0:
            return
        if self.waited.get(sem.id, 0) >= val:
            return
        if sem is self.sem and self.name == "pe":
            return
        self.h.wait_ge(sem.h, val)
        self.waited[sem.id] = val


class KB:
    def __init__(self, nc):
        self.nc = nc
        self.es = ExitStack()
        self.nsem = 0
        self.sems = []
        self.pe = Eng(self, "pe", nc.tensor)
        self.act = Eng(self, "act", nc.scalar)
        self.dve = Eng(self, "dve", nc.vector)
        self.pool = Eng(self, "pool", nc.gpsimd)
        self.sp = Eng(self, "sp", nc.sync)
        self.engs = [self.pe, self.act, self.dve, self.pool, self.sp]
        self.nins = 0
        self.ntile = 0
        self.free_dsems = []
        self.phase_tiles = []

    def newsem(self, dma):
        if dma and self.free_dsems:
            return self.free_dsems.pop()
        h = self.es.enter_context(self.nc.semaphore("s%d" % self.nsem))
        s = Sem(h, dma, self.nsem)
        self.nsem += 1
        self.sems.append(s)
        return s

    def tile(self, es, name, shape, dtype):
        self.ntile += 1
        t = es.enter_context(self.nc.sbuf_tensor("%s_%d" % (name, self.ntile), shape, dtype))
        tk = Tk(name, t)
        self.phase_tiles.append(tk)
        return tk

    def subs(self, tk, n):
        out = [Tk("%s.%d" % (tk.name, i), tk.t) for i in range(n)]
        self.phase_tiles.extend(out)
        return out

    @staticmethod
    def loaded(tk, subs):
        for s_ in subs:
            s_.w = tk.w
            s_.r = {}

    def psum(self, es, name, shape, dtype=F32):
        t = es.enter_context(self.nc.psum_tensor(name, shape, dtype))
        tk = Tk(name, t)
        tk.psum = True
        return tk

    def dram(self, name, shape, dtype, kind="Internal"):
        t = self.nc.dram_tensor(name, shape, dtype, kind=kind)
        return Tk(name, t.ap())

    def _deps(self, eng, reads, writes, nowaw=False):
        for t in reads:
            if t.w is not None:
                eng.wait(*t.w)
            if t.psum:
                for s, v in t.r.values():
                    if s is not eng.sem:
                        eng.wait(s, v)
        for t in writes:
            if t.w is not None and not nowaw:
                eng.wait(*t.w)
            for s, v in t.r.values():
                eng.wait(s, v)

    def op(self, eng, ins, reads=(), writes=()):
        self._deps(eng, reads, writes)
        i = ins()
        if eng.sem.total >= SEM_ROLL:
            eng.sem = self.newsem(False)
        eng.sem.total += 1
        i.then_inc(eng.sem.h, 1)
        ev = (eng.sem, eng.sem.total)
        sid = eng.sem.id
        for t in reads:
            t.r[sid] = ev
        for t in writes:
            t.w = ev
            t.r = {}
        self.nins += 1
        return i

    def dma(self, eng, out_ap, in_ap, reads=(), writes=(), nowaw=True, **kw):
        dst = writes[0]
        self._deps(eng, reads, writes, nowaw=nowaw)
        if dst.dsem is None:
            dst.dsem = self.newsem(True)
        if dst.dsem.total >= SEM_ROLL * 16:
            eng.wait(dst.dsem, dst.dsem.total)
            dst.dsem = self.newsem(True)
        i = eng.h.dma_start(out=out_ap, in_=in_ap, **kw)
        dst.dsem.total += 16
        i.then_inc(dst.dsem.h, 16)
        ev = (dst.dsem, dst.dsem.total)
        for t in reads:
            t.r[dst.dsem.id] = ev
        dst.w = ev
        self.nins += 1
        return i

    def barrier(self):
        for e in self.engs:
            for s in self.sems:
                if s.total > 0:
                    e.wait(s, s.total)

    def end_phase(self):
        self.barrier()
        for tk in self.phase_tiles:
            if tk.dsem is not None:
                self.free_dsems.append(tk.dsem)
                tk.dsem = None
                tk.w = None
                tk.r = {}
        self.phase_tiles = []


def _rope_tables(R, rows):
    n = R // 4
    t = np.arange(NLAT)
    inv = (10000.0 ** (-np.arange(n, dtype=np.float32) / n)).astype(np.float32)
    cos = np.zeros((128, NLAT), np.float32)
    sin = np.zeros((128, NLAT), np.float32)
    half = R // 2
    for base in rows:
        for d in range(R):
            pos = (t // GRID_W) if d < half else (t % GRID_W)
            f = inv[(d % half) % n]
            ang = pos.astype(np.float32) * f
            cos[base + d] = np.cos(ang).astype(np.float32)
            sin[base + d] = np.sin(ang).astype(np.float32)
    return cos, sin


def _rot_lhsT(R, bases, size):
    n = R // 4
    half = R // 2
    Rm = np.zeros((size, size), np.float32)
    for base in bases:
        for d in range(R):
            if (d % half) < n:
                Rm[base + d, base + d + n] = -1.0
            else:
                Rm[base + d, base + d - n] = 1.0
    return np.ascontiguousarray(Rm.T)


def _na_tables(rpb):
    H = rpb.shape[0]
    a = np.arange(2)[:, None, None, None]
    j = np.arange(64)[None, :, None, None]
    w = np.arange(16)[None, None, :, None]
    c = np.arange(64)[None, None, None, :]
    dlt = a - w + 7 + 0 * j + 0 * c
    ri = np.clip(dlt + 7, 0, 14)
    ci = np.clip(j - c + 15 + 0 * a + 0 * w, 0, 30)
    E = rpb[:, ri, ci].reshape(H, 128, 16 * 64).astype(np.float32)
    c0 = np.clip(c - 8, 0, 64 - 16)
    colok = (j >= c0) & (j < c0 + 16)
    m_int = colok & (dlt >= -4) & (dlt <= 3)
    m_bnd = colok & (np.abs(dlt) <= 7)
    M = np.stack([m_int, m_bnd]).reshape(2, 128, 16 * 64).astype(np.float32)
    return E, M


class Prog:
    def __init__(self, nlayers=NLAYERS, ph="MABCD", dbg=(), layers=None):
        self.layers = layers
        self.nl = nlayers
        self.ph = ph
        self.dbg = dbg
        nc = bass.Bass("TRN2", target_bir_lowering=False)
        self.nc = nc
        self.kb = KB(nc)
        self.build()

    def din(self, name, shape, dtype=F32):
        return self.kb.dram(name, list(shape), dtype, kind="ExternalInput")

    def dout(self, name, shape):
        t = self.kb.dram(name, list(shape), F32, kind="ExternalOutput")
        self.outs.append(t)
        return t

    def build(self):
        nc, kb = self.nc, self.kb
        self.outs = []
        nlw = max(self.nl, 1)
        shapes = {
            "xp": [NPR, D], "xs": [NLAT, D],
            "ck0": [PAST, 1024], "cv0": [PAST, 1024], "ck1": [PAST, 1024], "cv1": [PAST, 1024],
            "cckv": [PAST, 256], "ckpe": [PAST, 32], "ck3": [PAST, 256], "cv3": [PAST, 256],
            "condT": [128, 8, 2],
            "w_mod": [nlw, 1024, 6144], "bmodT": [4, 128, 48], "lngT": [128, 64], "lnbT": [128, 64],
            "w_mlp1": [nlw, 1024, 4096], "w_mlp2": [nlw, 4096, 1024],
            "l0_w_qkv": [1024, 3072], "l0_lam_bc": [128, 256], "l0_subln": [128, 1], "l0_w_o": [1024, 1024],
            "l1_w_qkv": [1024, 3072], "na_E": [16, 128, 1024], "na_M": [2, 128, 1024], "l1_w_o": [1024, 1024],
            "l2_w_a": [1024, 800], "l2_qnormT": [128, 4], "l2_kvnormT": [128, 2], "l2_kvnorm_bc": [128, 256],
            "l2_w_uq": [512, 1536], "l2_w_ukv": [256, 2048], "l2_w_o": [1024, 1024],
            "l3_w_qkv": [1024, 1536], "l3_sink_bc": [128, 16], "l3_w_o": [1024, 1024],
            "ident": [128, 128], "rt64": [128, 128], "rt32": [128, 128],
            "cos64": [128, NLAT], "sin64": [128, NLAT], "cos32": [128, NLAT], "sin32": [128, NLAT],
            "tri": [2, 128, 128],
        }
        prog = self

        class LazyIn(dict):
            def __missing__(self, name):
                t = prog.din(name, shapes[name])
                self[name] = t
                return t

        I = self.I = LazyIn()
        O = self.O = {}
        O["yp"] = self.dout("yp", [NPR, D])
        O["ys"] = self.dout("ys", [NLAT, D])
        O["k0"] = self.dout("k0", [NPR, 1024]); O["v0"] = self.dout("v0", [NPR, 1024])
        O["k1"] = self.dout("k1", [NPR, 1024]); O["v1"] = self.dout("v1", [NPR, 1024])
        O["ckv2"] = self.dout("ckv2", [NPR, 256]); O["kpe2"] = self.dout("kpe2", [NPR, 32])
        O["k3"] = self.dout("k3", [NPR, 256]); O["v3"] = self.dout("v3", [NPR, 256])
        kd = lambda n: ("ExternalOutput" if n in self.dbg else "Internal")
        self.XA = kb.dram("XA", [8, 128, NTOK], F32, kind=kd("XA"))
        self.XB = kb.dram("XB", [8, 128, NTOK], F32, kind=kd("XB"))
        self.QT = kb.dram("QT", [16, 128, NTOK], BF16, kind=kd("QT"))
        self.KT = kb.dram("KT", [8, 128, NKEY], BF16, kind=kd("KT"))
        self.VS = kb.dram("VS", [NKEY, 1024], BF16, kind=kd("VS"))
        self.OT = kb.dram("OT", [8, 128, NTOK], BF16, kind=kd("OT"))

        with ExitStack() as es:
            self.ges = es
            self.bank = [kb.psum(es, "bank%d" % i, [128, 512], F32) for i in range(8)]
            self.ones_f = kb.tile(es, "ones_f", [128, 128], F32)
            self.ones_b = kb.tile(es, "ones_b", [128, 128], BF16)
            self.ident = kb.tile(es, "ident", [128, 128], F32)
            self.epsln = kb.tile(es, "epsln", [128, 1], F32)
            self.eps5 = kb.tile(es, "eps5", [128, 1], F32)
            kb.op(kb.dve, lambda: nc.vector.memset(self.ones_f[:], 1.0), writes=[self.ones_f])
            kb.op(kb.dve, lambda: nc.vector.memset(self.ones_b[:], 1.0), writes=[self.ones_b])
            kb.op(kb.dve, lambda: nc.vector.memset(self.epsln[:], EPS_LN), writes=[self.epsln])
            kb.op(kb.dve, lambda: nc.vector.memset(self.eps5[:], LN_EPS), writes=[self.eps5])
            kb.dma(kb.sp, self.ident[:], I["ident"][:, :], writes=[self.ident])
            self.mv = kb.tile(es, "mv", [128, 4 * 6 * 8 * 2], F32)
            self.lng = kb.tile(es, "lng", [128, 64], F32)
            self.lnb = kb.tile(es, "lnb", [128, 64], F32)
            kb.dma(kb.sp, self.lng[:], I["lngT"][:, :], writes=[self.lng])
            kb.dma(kb.sp, self.lnb[:], I["lnbT"][:, :], writes=[self.lnb])

            self.phase_in()
            if "M" in self.ph:
                self.phase_mod()
            for l in range(self.nl):
                if self.layers is not None and l not in self.layers:
                    continue
                if "A" in self.ph:
                    self.phase_A(l)
                if "B" in self.ph:
                    self.phase_B(l)
                if "C" in self.ph or "D" in self.ph:
                    self.phase_C(l)
            self.phase_out()
            kb.end_phase()
        kb.es.close()

    def mvc(self, l, j, k, c):
        i = ((l * 6 + j) * 8 + k) * 2 + c
        return self.mv[:, i:i + 1]

    def lnc(self, t, l, s, k):
        i = (l * 2 + s) * 8 + k
        return t[:, i:i + 1]

    def phase_in(self):
        nc, kb = self.nc, self.kb
        with ExitStack() as es:
            xin = [kb.tile(es, "xin%d" % i, [128, 4, 1024], F32) for i in range(2)]
            xo = [kb.tile(es, "xo%d" % i, [128, 8, 512], F32) for i in range(2)]
            for c in range(NTOK // 512):
                src = self.I["xp"] if c < 2 else self.I["xs"]
                r0 = c * 512 if c < 2 else (c - 2) * 512
                xi = xin[c % 2]
                kb.dma(kb.sp, xi[:], src[r0:r0 + 512, :].rearrange("(j p) f -> p j f", p=128), writes=[xi])
                xt = xo[c % 2]
                for k in range(8):
                    bk = self.bank[k % 4]
                    for j in range(4):
                        kb.op(kb.pe, lambda bk=bk, j=j, k=k, xi=xi: nc.tensor.transpose(
                            bk[:, j * 128:(j + 1) * 128], xi[:, j, k * 128:(k + 1) * 128], self.ident[:]),
                            reads=[xi, self.ident], writes=[bk])
                    if k % 2 == 0:
                        kb.op(kb.dve, lambda bk=bk, k=k, xt=xt: nc.vector.tensor_copy(out=xt[:, k, :], in_=bk[:, :]),
                              reads=[bk], writes=[xt])
                    else:
                        kb.op(kb.act, lambda bk=bk, k=k, xt=xt: nc.scalar.copy(out=xt[:, k, :], in_=bk[:, :]),
                              reads=[bk], writes=[xt])
                kb.dma(kb.sp, self.XA[:, :, c * 512:(c + 1) * 512].rearrange("k p n -> p k n"), xt[:],
                       reads=[xt], writes=[self.XA])
            kb.end_phase()

    def phase_mod(self):
        nc, kb = self.nc, self.kb
        with ExitStack() as es:
            cond = kb.tile(es, "cond", [128, 8, 2], F32)
            sc = kb.tile(es, "silu", [128, 8, 2], BF16)
            kb.dma(kb.sp, cond[:], self.I["condT"][:, :, :], writes=[cond])
            kb.op(kb.act, lambda: nc.scalar.activation(out=sc[:], in_=cond[:], func=AF.Silu), reads=[cond], writes=[sc])
            wm = [kb.tile(es, "wm%d" % i, [128, 8, 1024], BF16) for i in range(2)]
            bm = kb.tile(es, "bm", [128, 4, 48], F32)
            kb.dma(kb.sp, bm[:], self.I["bmodT"][:, :, :].rearrange("l p n -> p l n"), writes=[bm])
            it = 0
            for l in range(self.nl):
                bk = self.bank[l % 2]
                for j in range(6):
                    w = wm[it % 2]
                    it += 1
                    kb.dma(kb.pool, w[:], self.I["w_mod"][l, :, j * 1024:(j + 1) * 1024].rearrange("(k p) n -> p k n", p=128),
                           writes=[w])
                    for nt in range(8):
                        col = (j * 8 + nt) * 2
                        for k in range(8):
                            kb.op(kb.pe, lambda bk=bk, col=col, w=w, k=k, nt=nt: nc.tensor.matmul(
                                bk[:, col:col + 2], w[:, k, nt * 128:(nt + 1) * 128], sc[:, k, :],
                                start=(k == 0), stop=(k == 7)), reads=[w, sc], writes=[bk])
                base = l * 96
                for c in range(2):
                    kb.op(kb.dve, lambda bk=bk, c=c, base=base, l=l: nc.vector.tensor_tensor(
                        out=self.mv[:, base + c:base + 96:2], in0=bk[:, c:96:2], in1=bm[:, l, :], op=ALU.add),
                        reads=[bk, bm], writes=[self.mv])
                for j in (1, 4):
                    a = base + j * 16
                    kb.op(kb.dve, lambda a=a: nc.vector.tensor_scalar_add(out=self.mv[:, a:a + 16], in0=self.mv[:, a:a + 16], scalar1=1.0),
                          reads=[self.mv], writes=[self.mv])
                for j in (2, 5):
                    a = base + j * 16
                    kb.op(kb.dve, lambda a=a: nc.vector.tensor_scalar_mul(out=self.mv[:, a:a + 16], in0=self.mv[:, a:a + 16], scalar1=1.0 / ALPHA),
                          reads=[self.mv], writes=[self.mv])
            kb.end_phase()

    def ln_epilogue(self, es_tiles, tT, N, l, s, dst, c0, sub=None):
        nc, kb = self.nc, self.kb
        sq, mean, var, s1b, s2b = es_tiles
        if sub is None:
            sub = [tT] * 8
        for k in range(8):
            q = sq[k % 2]
            kb.op(kb.act, lambda q=q, k=k: nc.scalar.activation(out=q[:, 0:N], in_=tT[:, k, :], func=AF.Square),
                  reads=[sub[k]], writes=[q])
            kb.op(kb.pe, lambda k=k: nc.tensor.matmul(s1b[:, 0:N], self.ones_f[:], tT[:, k, :], start=(k == 0), stop=(k == 7)),
                  reads=[sub[k], self.ones_f], writes=[s1b])
            kb.op(kb.pe, lambda q=q, k=k: nc.tensor.matmul(s2b[:, 0:N], self.ones_f[:], q[:, 0:N], start=(k == 0), stop=(k == 7)),
                  reads=[q, self.ones_f], writes=[s2b])
        kb.op(kb.act, lambda: nc.scalar.mul(out=mean[:, 0:N], in_=s1b[:, 0:N], mul=1.0 / D), reads=[s1b], writes=[mean])
        kb.op(kb.dve, lambda: nc.vector.scalar_tensor_tensor(out=var[:, 0:N], in0=mean[:, 0:N], scalar=-1.0, in1=mean[:, 0:N],
                                                             op0=ALU.mult, op1=ALU.mult), reads=[mean], writes=[var])
        kb.op(kb.dve, lambda: nc.vector.scalar_tensor_tensor(out=var[:, 0:N], in0=s2b[:, 0:N], scalar=1.0 / D, in1=var[:, 0:N],
                                                             op0=ALU.mult, op1=ALU.add), reads=[s2b, var], writes=[var])
        kb.op(kb.act, lambda: nc.scalar.activation(out=var[:, 0:N], in_=var[:, 0:N], func=AF.Ln, bias=self.epsln[:, 0:1], scale=1.0),
              reads=[var, self.epsln], writes=[var])
        kb.op(kb.act, lambda: nc.scalar.activation(out=var[:, 0:N], in_=var[:, 0:N], func=AF.Exp, scale=-0.5), reads=[var], writes=[var])
        for k in range(8):
            kb.op(kb.dve, lambda k=k: nc.vector.tensor_tensor(out=tT[:, k, :], in0=tT[:, k, :], in1=mean[:, 0:N], op=ALU.subtract),
                  reads=[sub[k], mean], writes=[sub[k]])
            if k % 4 == 3:
                kb.op(kb.dve, lambda k=k: nc.vector.tensor_tensor(out=tT[:, k, :], in0=tT[:, k, :], in1=var[:, 0:N], op=ALU.mult),
                      reads=[sub[k], var], writes=[sub[k]])
            else:
                kb.op(kb.pool, lambda k=k: nc.gpsimd.tensor_tensor(out=tT[:, k, :], in0=tT[:, k, :], in1=var[:, 0:N], op=ALU.mult),
                      reads=[sub[k], var], writes=[sub[k]])
            kb.op(kb.act, lambda k=k: nc.scalar.activation(out=tT[:, k, :], in_=tT[:, k, :], func=AF.Identity,
                                                           bias=self.lnc(self.lnb, l, s, k), scale=self.lnc(self.lng, l, s, k)),
                  reads=[sub[k], self.lng, self.lnb], writes=[sub[k]])
        rd = [tT] + ([] if sub[0] is tT else list(sub))
        kb.dma(kb.sp, dst[:, :, c0:c0 + N].rearrange("k p n -> p k n"), tT[:], reads=rd, writes=[dst])

    def load_w(self, dst, src_ap, nk, parts=1):
        kb = self.kb
        per = nk // parts if nk >= parts else nk
        k = 0
        while k < nk:
            k1 = min(nk, k + max(per, 1))
            kb.dma(kb.pool, dst[:, k:k1, :], src_ap[k * 128:k1 * 128, :].rearrange("(k p) n -> p k n", p=128), writes=[dst])
            k = k1

    def phase_A(self, l):
        nc, kb, I = self.nc, self.kb, self.I
        m = l % 4
        import os
        SK = os.environ.get("SKIP", "")
        TMB = int(os.environ.get("TMB", "6"))
        with ExitStack() as es:
            xT = [kb.tile(es, "xT%d" % i, [128, 8, 512], F32) for i in range(2)]
            hT = [kb.tile(es, "hT%d" % i, [128, 8, 512], BF16) for i in range(2)]
            stq = [kb.tile(es, "stq%d" % i, [128, 512], BF16) for i in range(4)]
            stv = [kb.tile(es, "stv%d" % i, [128, 1024], BF16) for i in range(2)]
            stf = [kb.tile(es, "stf%d" % i, [128, 1024], F32) for i in range(2)]
            self.cnt = 0
            if m in (0, 3):
                cos = kb.tile(es, "cos", [128, NLAT], F32); sin = kb.tile(es, "sin", [128, NLAT], F32)
                if "s" not in SK:
                    kb.dma(kb.sp, cos[:], I["cos64"][:, :], writes=[cos]); kb.dma(kb.sp, sin[:], I["sin64"][:, :], writes=[sin])
                rt = kb.tile(es, "rt", [128, 128], BF16)
                kb.dma(kb.pool, rt[:], I["rt64"][:, :], writes=[rt])
            elif m == 2:
                cos = kb.tile(es, "cos", [128, NLAT], F32); sin = kb.tile(es, "sin", [128, NLAT], F32)
                kb.dma(kb.sp, cos[:], I["cos32"][:, :], writes=[cos]); kb.dma(kb.sp, sin[:], I["sin32"][:, :], writes=[sin])
                rt = kb.tile(es, "rt", [128, 128], BF16)
                kb.dma(kb.pool, rt[:], I["rt32"][:, :], writes=[rt])
            else:
                cos = sin = rt = None
            rtmp = [kb.tile(es, "rtmp%d" % i, [128, 512], F32) for i in range(4)]
            qb = [kb.tile(es, "qb%d" % i, [128, 512], BF16) for i in range(2)]

            if m == 0:
                W = kb.tile(es, "Wqkv", [128, 8, 3072], BF16)
                if "w" not in SK:
                    self.load_w(W, I["l0_w_qkv"], 8, parts=4)
            elif m == 1:
                W = kb.tile(es, "Wqkv", [128, 8, 3072], BF16)
                self.load_w(W, I["l1_w_qkv"], 8, parts=4)
            elif m == 3:
                W = kb.tile(es, "Wqkv", [128, 8, 1536], BF16)
                self.load_w(W, I["l3_w_qkv"], 8, parts=2)
            else:
                W = kb.tile(es, "Wa", [128, 8, 800], BF16)
                self.load_w(W, I["l2_w_a"], 8, parts=2)
                Wuq = kb.tile(es, "Wuq", [128, 4, 1536], BF16)
                self.load_w(Wuq, I["l2_w_uq"], 4, parts=2)
                qn = kb.tile(es, "qn", [128, 4], F32); kvn = kb.tile(es, "kvn", [128, 2], F32)
                kvbc = kb.tile(es, "kvbc", [128, 256], F32)
                kb.dma(kb.sp, qn[:], I["l2_qnormT"][:, :], writes=[qn])
                kb.dma(kb.sp, kvn[:], I["l2_kvnormT"][:, :], writes=[kvn])
                kb.dma(kb.sp, kvbc[:], I["l2_kvnorm_bc"][:, :], writes=[kvbc])
                cq = kb.tile(es, "cq", [128, 6, 512], F32)
                cqn = kb.tile(es, "cqn", [128, 6, 512], BF16)
                sq2 = [kb.tile(es, "sq2_%d" % i, [128, 512], F32) for i in range(2)]
                rs = [kb.tile(es, "rs%d" % i, [128, 512], F32) for i in range(2)]
                ss1 = kb.tile(es, "ss1", [128, 1], F32)

            if "c" not in SK:
                self.cache_prep(l, es)

            pbank = self.bank
            bi = [0]

            def nextbank(lo, n):
                b = pbank[lo + bi[0] % n]
                bi[0] += 1
                return b

            def rope_store(ps, rows, lc, dst_ap, dstTk, R_rows=None):
                i = self.cnt
                self.cnt += 1
                st = stq[i % 4]
                r0, r1 = rows
                if lc is None:
                    if i % 2 == 0:
                        kb.op(kb.act, lambda: nc.scalar.copy(out=st[r0:r1, :], in_=ps[r0:r1, :]), reads=[ps], writes=[st])
                    else:
                        kb.op(kb.dve, lambda: nc.vector.tensor_copy(out=st[r0:r1, :], in_=ps[r0:r1, :]), reads=[ps], writes=[st])
                else:
                    rr0, rr1 = R_rows if R_rows is not None else rows
                    q_ = qb[i % 2]
                    kb.op(kb.act, lambda: nc.scalar.copy(out=q_[r0:r1, :], in_=ps[r0:r1, :]), reads=[ps], writes=[q_])
                    pr = pbank[4 + i % 2]
                    kb.op(kb.pe, lambda: nc.tensor.matmul(pr[r0:r1, :], rt[r0:r1, r0:r1], q_[r0:r1, :], start=True, stop=True),
                          reads=[rt, q_], writes=[pr])
                    t1 = rtmp[(2 * i) % 4]; t2 = rtmp[(2 * i + 1) % 4]
                    cs = slice(lc * 512, (lc + 1) * 512)
                    kb.op(kb.dve, lambda: nc.vector.tensor_tensor(out=t1[rr0:rr1, :], in0=ps[rr0:rr1, :], in1=cos[rr0:rr1, cs], op=ALU.mult),
                          reads=[ps, cos], writes=[t1])
                    kb.op(kb.dve, lambda: nc.vector.tensor_tensor(out=t2[rr0:rr1, :], in0=pr[rr0:rr1, :], in1=sin[rr0:rr1, cs], op=ALU.mult),
                          reads=[pr, sin], writes=[t2])
                    kb.op(kb.pool, lambda: nc.gpsimd.tensor_tensor(out=st[rr0:rr1, :], in0=t1[rr0:rr1, :], in1=t2[rr0:rr1, :], op=ALU.add),
                          reads=[t1, t2], writes=[st])
                    if rr0 > r0:
                        kb.op(kb.dve, lambda: nc.vector.tensor_copy(out=st[r0:rr0, :], in_=ps[r0:rr0, :]), reads=[ps], writes=[st])
                kb.dma(kb.sp, dst_ap, st[r0:r1, :], reads=[st], writes=[dstTk])

            for c in range(NTOK // 512):
                cond = 0 if c < 2 else 1
                lc = None if (c < 2 or "r" in SK) else c - 2
                t0 = c * 512
                key0 = t0 if c < 2 else LNEW0 + (c - 2) * 512
                x = xT[c % 2]; h = hT[c % 2]
                if c == 0:
                    kb.dma(kb.sp, x[:], self.XA[:, :, t0:t0 + 512].rearrange("k p n -> p k n"), reads=[self.XA], writes=[x])
                if c + 1 < NTOK // 512:
                    kb.dma(kb.sp, xT[(c + 1) % 2][:], self.XA[:, :, t0 + 512:t0 + 1024].rearrange("k p n -> p k n"), reads=[self.XA], writes=[xT[(c + 1) % 2]])
                for k in range(8):
                    kb.op(kb.act, lambda k=k: nc.scalar.activation(out=h[:, k, :], in_=x[:, k, :], func=AF.Identity,
                                                                   bias=self.mvc(l, 0, k, cond), scale=self.mvc(l, 1, k, cond)),
                          reads=[x, self.mv], writes=[h])

                def proj_fm(Wt, nk, col0, ncols, src, ps):
                    for k in range(nk):
                        kb.op(kb.pe, lambda k=k: nc.tensor.matmul(ps[0:ncols, :], Wt[:, k, col0:col0 + ncols], src[:, k, :],
                                                                  start=(k == 0), stop=(k == nk - 1)),
                              reads=[Wt, src], writes=[ps])

                def proj_tm(Wt, col0, ncols, j, ps):
                    for k in range(8):
                        kb.op(kb.pe, lambda k=k: nc.tensor.matmul(ps[:, 0:ncols], h[:, k, j * 128:(j + 1) * 128], Wt[:, k, col0:col0 + ncols],
                                                                  start=(k == 0), stop=(k == 7)),
                              reads=[Wt, h], writes=[ps])

                if "p" in SK:
                    continue
                if m in (0, 1, 3):
                    nq = 8
                    nkt = 8 if m != 3 else 2
                    kcol = 1024
                    vcol = 2048 if m != 3 else 1280
                    vw = 1024 if m != 3 else 256
                    use_rope = (m != 1)
                    for t in range(nq):
                        ps = nextbank(0, 4)
                        proj_fm(W, 8, t * 128, 128, h, ps)
                        rope_store(ps, (0, 128), lc if use_rope else None, self.QT[t, :, t0:t0 + 512], self.QT)
                    for t in range(nkt):
                        ps = nextbank(0, 4)
                        proj_fm(W, 8, kcol + t * 128, 128, h, ps)
                        rope_store(ps, (0, 128), lc if use_rope else None, self.KT[t, :, key0:key0 + 512], self.KT)
                    for j in range(4 if "t" not in SK else 0):
                        sv = stv[j % 2]
                        for hf in range(0, vw, 512):
                            n = min(512, vw - hf)
                            ps = pbank[TMB + (j + hf // 512) % 2]
                            proj_tm(W, vcol + hf, n, j, ps)
                            kb.op(kb.act, lambda ps=ps, hf=hf, n=n, sv=sv: nc.scalar.copy(out=sv[:, hf:hf + n], in_=ps[:, 0:n]),
                                  reads=[ps], writes=[sv])
                            if c < 2:
                                sf = stf[0]
                                kb.op(kb.dve, lambda ps=ps, hf=hf, n=n, sf=sf: nc.vector.tensor_copy(out=sf[:, hf:hf + n], in_=ps[:, 0:n]),
                                      reads=[ps], writes=[sf])
                        kb.dma(kb.sp, self.VS[key0 + j * 128:key0 + (j + 1) * 128, 0:vw], sv[:, 0:vw], reads=[sv], writes=[self.VS])
                        if c < 2:
                            vo = self.O["v%d" % m]
                            kb.dma(kb.sp, vo[t0 + j * 128:t0 + (j + 1) * 128, :], stf[0][:, 0:vw], reads=[stf[0]], writes=[vo])
                            sf = stf[1]
                            for hf in range(0, vw, 512):
                                n = min(512, vw - hf)
                                ps = pbank[TMB + (j + hf // 512) % 2]
                                proj_tm(W, kcol + hf, n, j, ps)
                                kb.op(kb.dve, lambda ps=ps, hf=hf, n=n, sf=sf: nc.vector.tensor_copy(out=sf[:, hf:hf + n], in_=ps[:, 0:n]),
                                      reads=[ps], writes=[sf])
                            ko = self.O["k%d" % m]
                            kb.dma(kb.sp, ko[t0 + j * 128:t0 + (j + 1) * 128, :], sf[:, 0:vw], reads=[sf], writes=[ko])
                else:
                    for t in range(6):
                        ps = nextbank(0, 4)
                        proj_fm(W, 8, t * 128, 128, h, ps)
                        kb.op(kb.dve, lambda t=t, ps=ps: nc.vector.tensor_copy(out=cq[:, t, :], in_=ps[:, :]), reads=[ps], writes=[cq])
                        q_ = sq2[t % 2]
                        kb.op(kb.act, lambda ps=ps, q_=q_: nc.scalar.activation(out=q_[:], in_=ps[:, :], func=AF.Square), reads=[ps], writes=[q_])
                        grp = 0 if t < 4 else 1
                        sb = pbank[4 + grp]
                        first = (t == 0) or (t == 4)
                        last = (t == 3) or (t == 5)
                        kb.op(kb.pe, lambda sb=sb, q_=q_, first=first, last=last: nc.tensor.matmul(sb[:, :], self.ones_f[:], q_[:], start=first, stop=last),
                              reads=[q_, self.ones_f], writes=[sb])
                    for grp, nf in ((0, 512), (1, 256)):
                        sb = pbank[4 + grp]
                        r = rs[grp]
                        kb.op(kb.act, lambda sb=sb, r=r, nf=nf: nc.scalar.activation(out=r[:], in_=sb[:, :], func=AF.Ln, bias=self.eps5[:, 0:1], scale=1.0 / nf),
                              reads=[sb, self.eps5], writes=[r])
                        kb.op(kb.act, lambda r=r: nc.scalar.activation(out=r[:], in_=r[:], func=AF.Exp, scale=-0.5), reads=[r], writes=[r])
                    for t in range(6):
                        nrm = qn[:, t:t + 1] if t < 4 else kvn[:, t - 4:t - 3]
                        r = rs[0 if t < 4 else 1]
                        kb.op(kb.dve, lambda t=t, nrm=nrm, r=r: nc.vector.scalar_tensor_tensor(out=cqn[:, t, :], in0=cq[:, t, :], scalar=nrm, in1=r[:],
                                                                                                op0=ALU.mult, op1=ALU.mult),
                              reads=[cq, qn, kvn, r], writes=[cqn])
                    for t in range(2):
                        kb.dma(kb.sp, self.KT[t, :, key0:key0 + 512], cqn[:, 4 + t, :], reads=[cqn], writes=[self.KT])
                    ps = nextbank(0, 4)
                    proj_fm(W, 8, 768, 32, h, ps)
                    rope_store(ps, (0, 32), lc, self.KT[2, 0:32, key0:key0 + 512], self.KT)
                    for hd in range(16):
                        ps = nextbank(0, 4)
                        proj_fm(Wuq, 4, hd * 96, 96, cqn, ps)
                        rope_store(ps, (0, 96), lc, self.QT[hd, 0:96, t0:t0 + 512], self.QT, R_rows=(64, 96))
                    if c < 2:
                        for j in range(4):
                            ps = pbank[6 + j % 2]
                            proj_tm(W, 512, 288, j, ps)
                            sf = stf[j % 2]
                            kb.op(kb.dve, lambda: nc.vector.memset(ss1[:], 0.0), writes=[ss1])
                            kb.op(kb.act, lambda ps=ps, sf=sf: nc.scalar.activation(out=sf[:, 512:768], in_=ps[:, 0:256], func=AF.Square, accum_out=ss1[:, 0:1]),
                                  reads=[ps], writes=[sf, ss1])
                            kb.op(kb.act, lambda: nc.scalar.activation(out=ss1[:], in_=ss1[:], func=AF.Ln, bias=self.eps5[:, 0:1], scale=1.0 / 256),
                                  reads=[ss1, self.eps5], writes=[ss1])
                            kb.op(kb.act, lambda: nc.scalar.activation(out=ss1[:], in_=ss1[:], func=AF.Exp, scale=-0.5), reads=[ss1], writes=[ss1])
                            kb.op(kb.dve, lambda ps=ps, sf=sf: nc.vector.scalar_tensor_tensor(out=sf[:, 0:256], in0=ps[:, 0:256], scalar=ss1[:, 0:1], in1=kvbc[:],
                                                                                               op0=ALU.mult, op1=ALU.mult),
                                  reads=[ps, ss1, kvbc], writes=[sf])
                            kb.op(kb.dve, lambda ps=ps, sf=sf: nc.vector.tensor_copy(out=sf[:, 256:288], in_=ps[:, 256:288]), reads=[ps], writes=[sf])
                            kb.dma(kb.sp, self.O["ckv2"][t0 + j * 128:t0 + (j + 1) * 128, :], sf[:, 0:256], reads=[sf], writes=[self.O["ckv2"]])
                            kb.dma(kb.sp, self.O["kpe2"][t0 + j * 128:t0 + (j + 1) * 128, :], sf[:, 256:288], reads=[sf], writes=[self.O["kpe2"]])
            kb.end_phase()

    def cache_prep(self, l, es):
        nc, kb, I = self.nc, self.kb, self.I
        m = l % 4
        if m in (0, 1, 3):
            kw = 1024 if m != 3 else 256
            ck = I["ck%d" % m]; cv = I["cv%d" % m]
            cvb = kb.tile(es, "cvb", [128, 4, 1024], BF16)
            kb.dma(kb.pool, cvb[:, :, 0:kw], cv[:, :].rearrange("(j p) f -> p j f", p=128), writes=[cvb])
            kb.dma(kb.sp, self.VS[LKEY0:LKEY0 + PAST, 0:kw].rearrange("(j p) f -> p j f", p=128), cvb[:, :, 0:kw], reads=[cvb], writes=[self.VS])
            srcs = [(ck, kw, 0)]
        else:
            srcs = [(I["cckv"], 256, 0), (I["ckpe"], 32, 2)]
        cb = kb.tile(es, "cacheb", [128, 4, 1024], F32)
        co = kb.tile(es, "cacheo", [128, 8, 512], BF16)
        for (src, kw, tile0) in srcs:
            kb.dma(kb.sp, cb[:, :, 0:kw], src[:, :].rearrange("(j p) f -> p j f", p=128), writes=[cb], nowaw=False)
            nt = (kw + 127) // 128
            for k in range(nt):
                rows = min(128, kw - k * 128)
                bk = self.bank[6 + k % 2]
                for j in range(4):
                    kb.op(kb.pe, lambda bk=bk, j=j, k=k, rows=rows: nc.tensor.transpose(
                        bk[0:rows, j * 128:(j + 1) * 128], cb[:, j, k * 128:k * 128 + rows], self.ident[:]),
                        reads=[cb, self.ident], writes=[bk])
                kb.op(kb.dve, lambda bk=bk, k=k, rows=rows: nc.vector.tensor_copy(out=co[0:rows, k, :], in_=bk[0:rows, :]),
                      reads=[bk], writes=[co])
                kb.dma(kb.sp, self.KT[tile0 + k, 0:rows, LKEY0:LKEY0 + PAST], co[0:rows, k, :], reads=[co], writes=[self.KT])

    def attn_steps(self, steps, scale, NQ, sbanks, pts):
        nc, kb = self.nc, self.kb
        n = len(steps)
        import os
        LA = int(os.environ.get("LA", "3"))
        NODEN = os.environ.get("NODEN", "") == "1"
        ns, npt = len(sbanks), len(pts)
        g0 = self.gstep

        def qk(i):
            st = steps[i]
            if st.get("pre") is not None:
                st["pre"]()
            bk = sbanks[(g0 + i) % ns]
            nk = st["nk"]
            nq = st.get("nq", NQ)
            kb.op(kb.pe, lambda: nc.tensor.matmul(bk[0:nk, 0:nq], st["kT"], st["q"], start=True, stop=True),
                  reads=st["rtk"], writes=[bk])
            pt = pts[(g0 + i) % npt]
            kb.op(kb.act, lambda: nc.scalar.activation(out=pt[0:nk, 0:nq], in_=bk[0:nk, 0:nq], func=AF.Exp, scale=scale),
                  reads=[bk], writes=[pt])
            if st.get("mask") is not None:
                self.mcnt += 1
                use_dve = (self.mcnt % 3 != 0)
                eng = kb.dve if use_dve else kb.pool
                eh = nc.vector if use_dve else nc.gpsimd
                kb.op(eng, lambda: eh.tensor_tensor(out=pt[0:nk, 0:nq], in0=pt[0:nk, 0:nq], in1=st["mask"], op=ALU.mult),
                      reads=[pt, st["mtk"]], writes=[pt])

        def pv(i):
            st = steps[i]
            pt = pts[(g0 + i) % npt]
            nk = st["nk"]
            nq = st.get("nq", NQ)
            kb.op(kb.pe, lambda: nc.tensor.matmul(st["o"], st["v"], pt[0:nk, 0:nq], start=st["first"], stop=st["last"]),
                  reads=[pt, st["vtk"]], writes=[st["otk"]])
            if st.get("den") is not None and not NODEN:
                kb.op(kb.pe, lambda: nc.tensor.matmul(st["den"], self.ones_b[0:nk, :], pt[0:nk, 0:nq], start=st["first"], stop=st["last"]),
                      reads=[pt, self.ones_b], writes=[st["dtk"]])
            if st.get("post") is not None:
                st["post"]()

        for i in range(min(LA, n)):
            qk(i)
        for i in range(n):
            if i + LA < n:
                qk(i + LA)
            pv(i)
        self.gstep += n

    def phase_B(self, l):
        m = l % 4
        self.gstep = 0
        self.mcnt = 0
        if m == 0:
            self.attn_diff(l)
        elif m == 1:
            self.attn_na(l)
        elif m == 2:
            self.attn_mla(l)
        else:
            self.attn_swa(l)

    def q_chunks(self):
        out = []
        for s in range(4):
            out.append((s * 256, 256, s * 256, 256))
        for c in range(8):
            out.append((NPR + c * 512, 512, LKEY0, PAST + NLAT))
        return out

    def attn_diff(self, l):
        import os
        nc, kb, I = self.nc, self.kb, self.I
        with ExitStack() as es:
            kt = [kb.tile(es, "kt%d" % i, [128, 2, NKEY], BF16) for i in range(2)]
            for k_ in kt:
                kb.op(kb.pool, lambda k_=k_: nc.gpsimd.memset(k_[64:128, 0, :], 0.0), writes=[k_])
                kb.op(kb.pool, lambda k_=k_: nc.gpsimd.memset(k_[0:64, 1, :], 0.0), writes=[k_])
            vt = [kb.tile(es, "vt%d" % i, [128, NKEY // 128, 128], BF16) for i in range(2)]
            qt = [kb.tile(es, "qt%d" % i, [128, 512], BF16) for i in range(3)]
            pts = [kb.tile(es, "pt%d" % i, [128, 512], BF16) for i in range(6)]
            accs = [[kb.tile(es, "acc%d_%d" % (i, j), [128, 512], F32) for j in range(4)] for i in range(2)]
            self._pend = None
            ot = [kb.tile(es, "ot%d" % i, [128, 512], BF16) for i in range(2)]
            ra = [kb.tile(es, "ra%d" % i, [128, 512], F32) for i in range(2)]
            fa = [kb.tile(es, "fa%d" % i, [128, 512], F32) for i in range(2)]
            sq = kb.tile(es, "sq", [128, 512], F32)
            rstd = kb.tile(es, "rstd", [128, 512], F32)
            lam = kb.tile(es, "lam", [128, 256], F32)
            lp = kb.tile(es, "lp", [128, 128], F32)
            ls = kb.tile(es, "ls", [128, 2], F32)
            nlam = kb.tile(es, "nlam", [128, 1], F32)
            subw = kb.tile(es, "subw", [128, 1], F32)
            kb.dma(kb.sp, lam[:], I["l0_lam_bc"][:, :], writes=[lam])
            kb.dma(kb.sp, subw[:], I["l0_subln"][:, :], writes=[subw])
            kb.op(kb.dve, lambda: nc.vector.tensor_tensor(out=lp[:, 0:64], in0=lam[:, 0:64], in1=lam[:, 64:128], op=ALU.mult), reads=[lam], writes=[lp])
            kb.op(kb.dve, lambda: nc.vector.tensor_tensor(out=lp[:, 64:128], in0=lam[:, 128:192], in1=lam[:, 192:256], op=ALU.mult), reads=[lam], writes=[lp])
            kb.op(kb.dve, lambda: nc.vector.reduce_sum(out=ls[:, 0:1], in_=lp[:, 0:64], axis=AX.X), reads=[lp], writes=[ls])
            kb.op(kb.dve, lambda: nc.vector.reduce_sum(out=ls[:, 1:2], in_=lp[:, 64:128], axis=AX.X), reads=[lp], writes=[ls])
            kb.op(kb.act, lambda: nc.scalar.activation(out=ls[:], in_=ls[:], func=AF.Exp), reads=[ls], writes=[ls])
            kb.op(kb.dve, lambda: nc.vector.tensor_tensor(out=nlam[:], in0=ls[:, 1:2], in1=ls[:, 0:1], op=ALU.subtract), reads=[ls], writes=[nlam])
            kb.op(kb.dve, lambda: nc.vector.tensor_scalar_add(out=nlam[:], in0=nlam[:], scalar1=-LAM_INIT), reads=[nlam], writes=[nlam])
            kb.op(kb.dve, lambda: nc.vector.tensor_scalar_mul(out=subw[:], in0=subw[:], scalar1=(1.0 - LAM_INIT)), reads=[subw], writes=[subw])
            B = self.bank
            sb = [B[4], B[5], B[6]]
            scale = 64 ** -0.5
            ci = 0
            chunks = self.q_chunks()

            def load_head(h):
                k_ = kt[h % 2]; v_ = vt[h % 2]
                kb.dma(kb.sp, k_[0:64, 0, :], self.KT[h, 0:64, :], reads=[self.KT], writes=[k_], nowaw=False)
                kb.dma(kb.sp, k_[64:128, 1, :], self.KT[h, 64:128, :], reads=[self.KT], writes=[k_])
                for j4 in range(4):
                    kb.dma(kb.sp, v_[:, j4 * 11:(j4 + 1) * 11, :], self.VS[j4 * 1408:(j4 + 1) * 1408, h * 128:(h + 1) * 128].rearrange("(t p) d -> p t d", p=128),
                           reads=[self.VS], writes=[v_])

            def load_q(idx):
                h, c = divmod(idx, len(chunks))
                (t0, nq, key0, nkeys) = chunks[c]
                q_ = qt[idx % 3]
                kb.dma(kb.sp, q_[:, 0:nq], self.QT[h, :, t0:t0 + nq], reads=[self.QT], writes=[q_])

            load_head(0)
            load_q(0)
            for h in range(8):
                k_ = kt[h % 2]; v_ = vt[h % 2]
                if h + 1 < 8:
                    load_head(h + 1)
                allsteps = []
                for (t0, nq, key0, nkeys) in chunks:
                    q_ = qt[ci % 3]

                    def pre(ci=ci):
                        if ci + 1 < 8 * len(chunks):
                            load_q(ci + 1)
                    steps = []
                    nkt = nkeys // 128
                    for i in range(nkt):
                        kk = key0 + i * 128
                        for hf in range(2):
                            steps.append(dict(kT=k_[:, hf, kk:kk + 128], q=q_[:, 0:nq], nk=128, nq=nq,
                                              v=v_[:, kk // 128, :], o=B[hf * 2][:, 0:nq], otk=B[hf * 2],
                                              den=B[hf * 2 + 1][:, 0:nq], dtk=B[hf * 2 + 1],
                                              first=(i == 0), last=(i == nkt - 1), rtk=[k_, q_], vtk=v_))

                    def post(ci=ci, nq=nq, h=h, t0=t0):
                        acc = accs[ci % 2]
                        if self._pend is not None:
                            self._pend()
                        for j in range(4):
                            kb.op(kb.dve, lambda j=j: nc.vector.tensor_copy(out=acc[j][:, 0:nq], in_=B[j][:, 0:nq]), reads=[B[j]], writes=[acc[j]])

                        def fin():
                            for hf in range(2):
                                kb.op(kb.dve, lambda hf=hf: nc.vector.reciprocal(out=ra[hf][:, 0:nq], in_=acc[hf * 2 + 1][:, 0:nq]), reads=[acc[hf * 2 + 1]], writes=[ra[hf]])
                                kb.op(kb.dve, lambda hf=hf: nc.vector.tensor_tensor(out=fa[hf][:, 0:nq], in0=acc[hf * 2][:, 0:nq], in1=ra[hf][:, 0:nq], op=ALU.mult),
                                      reads=[acc[hf * 2], ra[hf]], writes=[fa[hf]])
                            kb.op(kb.dve, lambda: nc.vector.scalar_tensor_tensor(out=fa[0][:, 0:nq], in0=fa[1][:, 0:nq], scalar=nlam[:, 0:1], in1=fa[0][:, 0:nq],
                                                                                 op0=ALU.mult, op1=ALU.add), reads=[fa[1], nlam, fa[0]], writes=[fa[0]])
                            kb.op(kb.pool, lambda: nc.gpsimd.tensor_tensor(out=sq[:, 0:nq], in0=fa[0][:, 0:nq], in1=fa[0][:, 0:nq], op=ALU.mult), reads=[fa[0]], writes=[sq])
                            kb.op(kb.pe, lambda: nc.tensor.matmul(B[7][:, 0:nq], self.ones_f[:], sq[:, 0:nq], start=True, stop=True),
                                  reads=[sq, self.ones_f], writes=[B[7]])
                            kb.op(kb.act, lambda: nc.scalar.activation(out=rstd[:, 0:nq], in_=B[7][:, 0:nq], func=AF.Ln, bias=self.eps5[:, 0:1], scale=1.0 / 128),
                                  reads=[B[7], self.eps5], writes=[rstd])
                            kb.op(kb.act, lambda: nc.scalar.activation(out=rstd[:, 0:nq], in_=rstd[:, 0:nq], func=AF.Exp, scale=-0.5), reads=[rstd], writes=[rstd])
                            o_ = ot[ci % 2]
                            kb.op(kb.dve, lambda: nc.vector.scalar_tensor_tensor(out=o_[:, 0:nq], in0=fa[0][:, 0:nq], scalar=subw[:, 0:1], in1=rstd[:, 0:nq],
                                                                                 op0=ALU.mult, op1=ALU.mult), reads=[fa[0], subw, rstd], writes=[o_])
                            kb.dma(kb.sp, self.OT[h, :, t0:t0 + nq], o_[:, 0:nq], reads=[o_], writes=[self.OT])
                        self._pend = fin
                    steps[0]["pre"] = pre
                    steps[-1]["post"] = post
                    allsteps.extend(steps)
                    ci += 1
                self.attn_steps(allsteps, scale, 512, sb, pts)
            if self._pend is not None:
                self._pend()
                self._pend = None
            kb.end_phase()

    def finalize_aug(self, ps, nq, rb, o_, dst_ap, extra_den=None, act_recip=False):
        nc, kb = self.nc, self.kb
        if extra_den is not None:
            tk, ap = extra_den
            if act_recip:
                kb.op(kb.act, lambda: nc.scalar.activation(out=rb[64:128, 0:nq], in_=ps[64:128, 0:nq], func=AF.Ln, bias=ap, scale=1.0),
                      reads=[ps, tk], writes=[rb])
                kb.op(kb.act, lambda: nc.scalar.activation(out=rb[64:128, 0:nq], in_=rb[64:128, 0:nq], func=AF.Exp, scale=-1.0), reads=[rb], writes=[rb])
            else:
                kb.op(kb.dve, lambda: nc.vector.tensor_scalar_add(out=rb[64:128, 0:nq], in0=ps[64:128, 0:nq], scalar1=ap),
                      reads=[ps, tk], writes=[rb])
                kb.op(kb.dve, lambda: nc.vector.reciprocal(out=rb[64:128, 0:nq], in_=rb[64:128, 0:nq]), reads=[rb], writes=[rb])
        elif act_recip:
            kb.op(kb.act, lambda: nc.scalar.activation(out=rb[64:128, 0:nq], in_=ps[64:128, 0:nq], func=AF.Ln), reads=[ps], writes=[rb])
            kb.op(kb.act, lambda: nc.scalar.activation(out=rb[64:128, 0:nq], in_=rb[64:128, 0:nq], func=AF.Exp, scale=-1.0), reads=[rb], writes=[rb])
        else:
            kb.op(kb.dve, lambda: nc.vector.reciprocal(out=rb[64:128, 0:nq], in_=ps[64:128, 0:nq]), reads=[ps], writes=[rb])
        kb.op(kb.dve, lambda: nc.vector.tensor_tensor(out=o_[0:64, 0:nq], in0=ps[0:64, 0:nq], in1=rb[64:128, 0:nq], op=ALU.mult),
              reads=[ps, rb], writes=[o_])
        kb.dma(kb.sp, dst_ap, o_[0:64, 0:nq], reads=[o_], writes=[self.OT])

    def attn_na(self, l):
        import os
        AR = os.environ.get("AR", "1") == "1"
        nc, kb, I = self.nc, self.kb, self.I
        with ExitStack() as es:
            kt = [kb.tile(es, "kt%d" % i, [128, NKEY], BF16) for i in range(2)]
            kb.op(kb.pool, lambda: nc.gpsimd.memset(kt[0][64:128, :], 0.0), writes=[kt[0]])
            kb.op(kb.pool, lambda: nc.gpsimd.memset(kt[1][0:64, :], 0.0), writes=[kt[1]])
            vt = [kb.tile(es, "vt%d" % i, [128, NKEY // 128, 128], BF16) for i in range(2)]
            qt = [kb.tile(es, "qt%d" % i, [128, 512], BF16) for i in range(3)]
            pts = [kb.tile(es, "pt%d" % i, [128, 512], BF16) for i in range(4)]
            ot = [kb.tile(es, "ot%d" % i, [64, 512], BF16) for i in range(3)]
            rb = [kb.tile(es, "rb%d" % i, [128, 512], F32) for i in range(2)]
            Mt = kb.tile(es, "Mt", [128, 2, 1024], F32)
            Et = [kb.tile(es, "Et%d" % i, [128, 1024], F32) for i in range(2)]
            tabs = [kb.tile(es, "tab%d" % i, [128, 2, 1024], BF16) for i in range(2)]
            kb.dma(kb.sp, Mt[:], I["na_M"][:, :, :].rearrange("v p n -> p v n"), writes=[Mt])
            for v_ in vt:
                kb.op(kb.pool, lambda v_=v_: nc.gpsimd.memset(v_[:, :, 64:128], 1.0), writes=[v_])
            B = self.bank
            sb = [B[4], B[5], B[6], B[7]]
            ob = [B[0], B[1], B[2], B[3]]
            scale = 64 ** -0.5
            ci = 0
            chunks = [(sq_ * 256, 256) for sq_ in range(4)] + [(NPR + c_ * 512, 512) for c_ in range(8)]

            def load_head(h):
                k_ = kt[h % 2]; v_ = vt[h % 2]
                base = (h % 2) * 64
                kb.dma(kb.sp, k_[base:base + 64, :], self.KT[h // 2, base:base + 64, :], reads=[self.KT], writes=[k_], nowaw=False)
                for j4 in range(4):
                    kb.dma(kb.sp, v_[:, j4 * 11:(j4 + 1) * 11, 0:64], self.VS[j4 * 1408:(j4 + 1) * 1408, h * 64:(h + 1) * 64].rearrange("(t p) d -> p t d", p=128),
                           reads=[self.VS], writes=[v_], nowaw=(j4 > 0))

            def load_q(idx):
                h, c = divmod(idx, len(chunks))
                (t0, nq) = chunks[c]
                kb.dma(kb.sp, qt[idx % 3][:, 0:nq], self.QT[h // 2, :, t0:t0 + nq], reads=[self.QT], writes=[qt[idx % 3]])

            load_head(0)
            load_q(0)
            allsteps = []

            def add(steps, nq_default, pre, post):
                for st in steps:
                    st.setdefault("nq", nq_default)
                if pre is not None:
                    steps[0]["pre"] = pre
                if post is not None:
                    steps[-1]["post"] = post
                allsteps.extend(steps)

            for h in range(16):
                k_ = kt[h % 2]; v_ = vt[h % 2]; E_ = Et[h % 2]; tb = tabs[h % 2]
                tl, base = h // 2, (h % 2) * 64
                def head_setup(h=h, E_=E_, tb=tb):
                    kb.dma(kb.sp, E_[:], I["na_E"][h, :, :], writes=[E_])
                    kb.op(kb.act, lambda: nc.scalar.activation(out=E_[:], in_=E_[:], func=AF.Exp), reads=[E_], writes=[E_])
                    for vv in range(2):
                        kb.op(kb.dve, lambda vv=vv: nc.vector.tensor_tensor(out=tb[:, vv, :], in0=E_[:], in1=Mt[:, vv, :], op=ALU.mult),
                              reads=[E_, Mt], writes=[tb])
                for s in range(4):
                    t0 = s * 256
                    q_ = qt[ci % 3]

                    def pre(ci=ci, s=s, hs=head_setup, h=h):
                        if s == 0:
                            hs()
                        if s == 2 and h + 1 < 16:
                            load_head(h + 1)
                        if ci + 1 < 16 * 12:
                            load_q(ci + 1)
                    o_b = ob[ci % 4]
                    steps = []
                    for i in range(2):
                        kk = t0 + i * 128
                        steps.append(dict(kT=k_[:, kk:kk + 128], q=q_[:, 0:256], nk=128, v=v_[:, kk // 128, :], o=o_b[:, 0:256], otk=o_b,
                                          first=(i == 0), last=(i == 1), rtk=[k_, q_], vtk=v_))

                    def post(o_b=o_b, ci=ci, tl=tl, base=base, t0=t0):
                        self.finalize_aug(o_b, 256, rb[ci % 2], ot[ci % 3], self.OT[tl, base:base + 64, t0:t0 + 256], act_recip=AR)
                    add(steps, 256, pre, post)
                    ci += 1
                for c in range(8):
                    t0 = NPR + c * 512
                    q_ = qt[ci % 3]

                    def pre(ci=ci):
                        if ci + 1 < 16 * 12:
                            load_q(ci + 1)
                    o_b = ob[ci % 4]
                    cst = []
                    if 1 <= c <= 6:
                        steps = []
                        for i in range(4):
                            kk = LKEY0 + i * 128
                            steps.append(dict(kT=k_[:, kk:kk + 128], q=q_[:, 0:512], nk=128, nq=512, v=v_[:, kk // 128, :], o=o_b[:, 0:512], otk=o_b,
                                              first=(i == 0), last=False, rtk=[k_, q_], vtk=v_))
                        rks = list(range(8 * c - 4, 8 * c + 12, 2))
                        for idx, Rk in enumerate(rks):
                            lo = max(Rk - 4, 8 * c); hi = min(Rk + 4, 8 * c + 6)
                            q0 = (lo - 8 * c) // 2 * 128; q1 = ((hi - 8 * c) // 2 + 1) * 128
                            w0 = 7 + lo - Rk
                            kk = LNEW0 + Rk * 64
                            steps.append(dict(kT=k_[:, kk:kk + 128], q=q_[:, q0:q1], nk=128, nq=q1 - q0, v=v_[:, kk // 128, :], o=o_b[:, q0:q1], otk=o_b,
                                              first=False, last=(idx == len(rks) - 1), rtk=[k_, q_], vtk=v_,
                                              mask=tb[:, 0, w0 * 64:w0 * 64 + (q1 - q0)], mtk=tb))
                        cst.extend(steps)
                    else:
                        steps = []
                        for i in range(4):
                            kk = LKEY0 + i * 128
                            steps.append(dict(kT=k_[:, kk:kk + 128], q=q_[:, 0:512], nk=128, nq=512, v=v_[:, kk // 128, :], o=o_b[:, 0:512], otk=o_b,
                                              first=(i == 0), last=False, rtk=[k_, q_], vtk=v_))
                        cst.extend(steps)
                        for sub in range(4):
                            mrow = c * 4 + sub
                            Rq = 2 * mrow
                            if mrow in (0, 1):
                                rks, var = [0, 2, 4, 6], 1
                            elif mrow in (30, 31):
                                rks, var = [56, 58, 60, 62], 1
                            else:
                                rks, var = [Rq - 4 + 2 * t for t in range(5)], 0
                            qs = q_[:, sub * 128:(sub + 1) * 128]
                            oap = o_b[:, sub * 128:(sub + 1) * 128]
                            steps = []
                            for i, Rk in enumerate(rks):
                                kk = LNEW0 + Rk * 64
                                w0 = 7 + Rq - Rk
                                steps.append(dict(kT=k_[:, kk:kk + 128], q=qs, nk=128, nq=128, v=v_[:, kk // 128, :], o=oap, otk=o_b,
                                                  first=False, last=(i == len(rks) - 1), rtk=[k_, q_], vtk=v_,
                                                  mask=tb[:, var, w0 * 64:(w0 + 2) * 64], mtk=tb))
                            cst.extend(steps)

                    def post(o_b=o_b, ci=ci, tl=tl, base=base, t0=t0):
                        self.finalize_aug(o_b, 512, rb[ci % 2], ot[ci % 3], self.OT[tl, base:base + 64, t0:t0 + 512], act_recip=AR)
                    add(cst, 512, pre, post)
                    ci += 1
            self.attn_steps(allsteps, scale, 512, sb, pts)
            kb.end_phase()

    def attn_swa(self, l):
        import os
        AR = os.environ.get("AR", "1") == "1"
        nc, kb, I = self.nc, self.kb, self.I
        with ExitStack() as es:
            kt = [kb.tile(es, "kt%d" % i, [128, 2, NKEY], BF16) for i in range(2)]
            for k_ in kt:
                kb.op(kb.pool, lambda k_=k_: nc.gpsimd.memset(k_[64:128, 0, :], 0.0), writes=[k_])
                kb.op(kb.pool, lambda k_=k_: nc.gpsimd.memset(k_[0:64, 1, :], 0.0), writes=[k_])
            vt = [kb.tile(es, "vt%d" % i, [128, NKEY // 128, 128], BF16) for i in range(2)]
            qt = [kb.tile(es, "qt%d" % i, [128, 512], BF16) for i in range(3)]
            pts = [kb.tile(es, "pt%d" % i, [128, 512], BF16) for i in range(4)]
            ot = [kb.tile(es, "ot%d" % i, [64, 512], BF16) for i in range(3)]
            rb = [kb.tile(es, "rb%d" % i, [128, 512], F32) for i in range(2)]
            M3 = kb.tile(es, "M3", [128, 384], BF16)
            esink = kb.tile(es, "esink", [128, 16], F32)
            kb.op(kb.pool, lambda: nc.gpsimd.memset(M3[:, 128:256], 1.0), writes=[M3])
            kb.dma(kb.pool, M3[:, 0:128], I["tri"][1, :, :], writes=[M3], nowaw=False)
            kb.dma(kb.pool, M3[:, 256:384], I["tri"][0, :, :], writes=[M3])
            kb.dma(kb.sp, esink[:], I["l3_sink_bc"][:, :], writes=[esink])
            kb.op(kb.act, lambda: nc.scalar.activation(out=esink[:], in_=esink[:], func=AF.Exp), reads=[esink], writes=[esink])
            for v_ in vt:
                kb.op(kb.pool, lambda v_=v_: nc.gpsimd.memset(v_[:, :, 64:128], 1.0), writes=[v_])
            B = self.bank
            sb = [B[4], B[5], B[6], B[7]]
            ob = [B[0], B[1], B[2], B[3]]
            scale = 64 ** -0.5
            ci = 0
            chunks = [(sq_ * 256, 256) for sq_ in range(4)] + [(NPR + c_ * 512, 512) for c_ in range(8)]

            def load_group(g):
                k2 = kt[g % 2]; v_ = vt[g % 2]
                src = self.KT[g // 2, (g % 2) * 64:(g % 2) * 64 + 64, :]
                kb.dma(kb.sp, k2[0:64, 0, :], src, reads=[self.KT], writes=[k2], nowaw=False)
                kb.dma(kb.sp, k2[64:128, 1, :], src, reads=[self.KT], writes=[k2])
                for j4 in range(4):
                    kb.dma(kb.sp, v_[:, j4 * 11:(j4 + 1) * 11, 0:64], self.VS[j4 * 1408:(j4 + 1) * 1408, g * 64:(g + 1) * 64].rearrange("(t p) d -> p t d", p=128),
                           reads=[self.VS], writes=[v_], nowaw=(j4 > 0))

            def load_q(idx):
                h, c = divmod(idx, len(chunks))
                (t0, nq) = chunks[c]
                kb.dma(kb.sp, qt[idx % 3][:, 0:nq], self.QT[h // 2, :, t0:t0 + nq], reads=[self.QT], writes=[qt[idx % 3]])

            load_group(0)
            load_q(0)
            allsteps = []

            def add(steps, nq_default, pre, post):
                for st in steps:
                    st.setdefault("nq", nq_default)
                if pre is not None:
                    steps[0]["pre"] = pre
                if post is not None:
                    steps[-1]["post"] = post
                allsteps.extend(steps)

            for g in range(4):
                k2 = kt[g % 2]; v_ = vt[g % 2]
                for hh in range(4):
                    h = g * 4 + hh
                    tl, base = h // 2, (h % 2) * 64
                    kv = h % 2
                    sk = (esink, esink[64:128, h:h + 1])
                    for s in range(4):
                        t0 = s * 256
                        q_ = qt[ci % 3]

                        def pre(ci=ci, first=(hh == 0 and s == 2), g=g):
                            if first and g + 1 < 4:
                                load_group(g + 1)
                            if ci + 1 < 16 * 12:
                                load_q(ci + 1)
                        o_b = ob[ci % 4]
                        steps = []
                        for i in range(2):
                            kk = t0 + i * 128
                            steps.append(dict(kT=k2[:, kv, kk:kk + 128], q=q_[:, 0:256], nk=128, v=v_[:, kk // 128, :], o=o_b[:, 0:256], otk=o_b,
                                              first=(i == 0), last=(i == 1), rtk=[k2, q_], vtk=v_))

                        def post(o_b=o_b, ci=ci, tl=tl, base=base, t0=t0, sk=sk):
                            self.finalize_aug(o_b, 256, rb[ci % 2], ot[ci % 3], self.OT[tl, base:base + 64, t0:t0 + 256], extra_den=sk, act_recip=AR)
                        add(steps, 256, pre, post)
                        ci += 1
                    for c in range(8):
                        t0 = NPR + c * 512
                        q_ = qt[ci % 3]

                        def pre(ci=ci):
                            if ci + 1 < 16 * 12:
                                load_q(ci + 1)
                        o_b = ob[ci % 4]
                        steps = []
                        for i in range(4):
                            kk = LKEY0 + i * 128
                            steps.append(dict(kT=k2[:, kv, kk:kk + 128], q=q_[:, 0:512], nk=128, nq=512, v=v_[:, kk // 128, :], o=o_b[:, 0:512], otk=o_b,
                                              first=(i == 0), last=False, rtk=[k2, q_], vtk=v_))
                        band = []
                        for j in range(4 * c - 1, 4 * c + 5):
                            if j < 0 or j > 31:
                                continue
                            blo = max(j - 1, 4 * c); bhi = min(j + 1, 4 * c + 3)
                            if blo > bhi:
                                continue
                            band.append((j, blo, bhi))
                        for idx, (j, blo, bhi) in enumerate(band):
                            q0 = (blo - 4 * c) * 128; q1 = (bhi - 4 * c + 1) * 128
                            kk = LNEW0 + j * 128
                            steps.append(dict(kT=k2[:, kv, kk:kk + 128], q=q_[:, q0:q1], nk=128, nq=q1 - q0, v=v_[:, kk // 128, :], o=o_b[:, q0:q1], otk=o_b,
                                              first=False, last=(idx == len(band) - 1), rtk=[k2, q_], vtk=v_,
                                              mask=M3[:, (blo - j + 1) * 128:(bhi - j + 2) * 128], mtk=M3))
                        def post(o_b=o_b, ci=ci, tl=tl, base=base, t0=t0, sk=sk):
                            self.finalize_aug(o_b, 512, rb[ci % 2], ot[ci % 3], self.OT[tl, base:base + 64, t0:t0 + 512], extra_den=sk, act_recip=AR)
                        add(steps, 512, pre, post)
                        ci += 1
            self.attn_steps(allsteps, scale, 512, sb, pts)
            kb.end_phase()

    def attn_mla(self, l):
        nc, kb, I = self.nc, self.kb, self.I
        with ExitStack() as es:
            ckv = kb.tile(es, "ckvT", [128, 2, NKEY], BF16)
            kt = [kb.tile(es, "kt%d" % i, [96, NKEY], BF16) for i in range(2)]
            vt = [kb.tile(es, "vt%d" % i, [128, NKEY // 128, 128], BF16) for i in range(2)]
            qt = [kb.tile(es, "qt%d" % i, [96, 512], BF16) for i in range(3)]
            pts = [kb.tile(es, "pt%d" % i, [128, 512], BF16) for i in range(4)]
            ot = [kb.tile(es, "ot%d" % i, [64, 512], BF16) for i in range(3)]
            rb = [kb.tile(es, "rb%d" % i, [128, 512], F32) for i in range(2)]
            Wukv = kb.tile(es, "Wukv", [128, 2, 2048], BF16)
            self.load_w(Wukv, I["l2_w_ukv"], 2)
            kb.dma(kb.sp, ckv[:], self.KT[0:2, :, :].rearrange("k p n -> p k n"), reads=[self.KT], writes=[ckv])
            for k_ in kt:
                kb.dma(kb.sp, k_[64:96, :], self.KT[2, 0:32, :], reads=[self.KT], writes=[k_])
            for v_ in vt:
                kb.op(kb.pool, lambda v_=v_: nc.gpsimd.memset(v_[:, :, 64:128], 1.0), writes=[v_])
            B = self.bank
            sb = [B[4], B[5], B[6]]
            ob = [B[0], B[1]]
            xb = [B[2], B[3], B[7]]
            scale = 96 ** -0.5
            ci = 0
            xi = 0
            chunks = self.q_chunks()

            def load_q(idx):
                h, c = divmod(idx, len(chunks))
                (t0, nq, key0, nkeys) = chunks[c]
                kb.dma(kb.sp, qt[idx % 3][:, 0:nq], self.QT[h, 0:96, t0:t0 + nq], reads=[self.QT], writes=[qt[idx % 3]])

            load_q(0)
            for h in range(16):
                k_ = kt[h % 2]; v_ = vt[h % 2]
                tl, base = h // 2, (h % 2) * 64
                for kc in range(NKEY // 512):
                    ps = xb[xi % 3]; xi += 1
                    for k in range(2):
                        kb.op(kb.pe, lambda ps=ps, k=k, kc=kc: nc.tensor.matmul(ps[0:64, :], Wukv[:, k, h * 128:h * 128 + 64], ckv[:, k, kc * 512:(kc + 1) * 512],
                                                                               start=(k == 0), stop=(k == 1)), reads=[Wukv, ckv], writes=[ps])
                    kb.op(kb.dve, lambda ps=ps, kc=kc, k_=k_: nc.vector.tensor_copy(out=k_[0:64, kc * 512:(kc + 1) * 512], in_=ps[0:64, :]),
                          reads=[ps], writes=[k_])
                ntl = NKEY // 128
                for tg in range(0, ntl, 8):
                    ps = xb[xi % 3]; xi += 1
                    nt = min(8, ntl - tg)
                    for t in range(nt):
                        for k in range(2):
                            kb.op(kb.pe, lambda ps=ps, k=k, t=t, tg=tg: nc.tensor.matmul(ps[:, t * 64:(t + 1) * 64], ckv[:, k, (tg + t) * 128:(tg + t + 1) * 128],
                                                                                        Wukv[:, k, h * 128 + 64:h * 128 + 128], start=(k == 0), stop=(k == 1)),
                                  reads=[Wukv, ckv], writes=[ps])
                    kb.op(kb.dve, lambda ps=ps, tg=tg, nt=nt, v_=v_: nc.vector.tensor_copy(
                        out=v_[:, tg:tg + nt, 0:64], in_=ps[:, 0:nt * 64].rearrange("p (t d) -> p t d", d=64)),
                        reads=[ps], writes=[v_])
                allsteps = []
                for (t0, nq, key0, nkeys) in chunks:
                    q_ = qt[ci % 3]

                    def pre(ci=ci):
                        if ci + 1 < 16 * len(chunks):
                            load_q(ci + 1)
                    o_b = ob[ci % 2]
                    steps = []
                    nkt = nkeys // 128
                    for i in range(nkt):
                        kk = key0 + i * 128
                        steps.append(dict(kT=k_[:, kk:kk + 128], q=q_[:, 0:nq], nk=128, nq=nq, v=v_[:, kk // 128, :], o=o_b[:, 0:nq], otk=o_b,
                                          first=(i == 0), last=(i == nkt - 1), rtk=[k_, q_], vtk=v_))

                    def post(o_b=o_b, ci=ci, tl=tl, base=base, t0=t0, nq=nq):
                        self.finalize_aug(o_b, nq, rb[ci % 2], ot[ci % 3], self.OT[tl, base:base + 64, t0:t0 + nq])
                    steps[0]["pre"] = pre
                    steps[-1]["post"] = post
                    allsteps.extend(steps)
                    ci += 1
                self.attn_steps(allsteps, scale, 512, sb, pts)
            kb.end_phase()

    def phase_C(self, l):
        nc, kb, I = self.nc, self.kb, self.I
        m = l % 4
        N = 256
        N1 = 512
        nchunk = NTOK // N
        B = self.bank
        with ExitStack() as es:
            W1 = kb.tile(es, "W1", [128, 8, 4096], BF16)
            W2a = kb.tile(es, "W2a", [128, 16, 1024], BF16)
            with ExitStack() as es1:
                Wo = kb.tile(es1, "Wo", [128, 8, 1024], BF16)
                self.load_w(Wo, I["l%d_w_o" % m], 8, parts=2)
                self.load_w(W1, I["w_mlp1"][l], 8, parts=8)
                self.load_w(W2a, I["w_mlp2"][l][0:2048, :], 16, parts=4)
                NB1 = 3
                xT = [kb.tile(es1, "xT%d" % i, [128, 8, N1], F32) for i in range(NB1)]
                xS = [kb.subs(x_, 8) for x_ in xT]
                oT = [kb.tile(es1, "oT%d" % i, [128, 8, N1], BF16) for i in range(2)]
                sq = [kb.tile(es1, "sq%d" % i, [128, 512], F32) for i in range(2)]
                mean = kb.tile(es1, "mean", [128, 512], F32)
                var = kb.tile(es1, "var", [128, 512], F32)
                nch1 = NTOK // N1

                def load1(c):
                    kb.dma(kb.sp, oT[c % 2][:], self.OT[:, :, c * N1:(c + 1) * N1].rearrange("k p n -> p k n"), reads=[self.OT], writes=[oT[c % 2]])
                    kb.dma(kb.sp, xT[c % NB1][:], self.XA[:, :, c * N1:(c + 1) * N1].rearrange("k p n -> p k n"), reads=[self.XA], writes=[xT[c % NB1]])
                    KB.loaded(xT[c % NB1], xS[c % NB1])

                def sA(c):
                    cond = 0 if c * N1 < NPR else 1
                    x = xT[c % NB1]; o = oT[c % 2]; xs = xS[c % NB1]
                    for n in range(8):
                        ps = B[n % 4]
                        for k in range(8):
                            kb.op(kb.pe, lambda ps=ps, k=k, n=n: nc.tensor.matmul(ps[:, 0:N1], Wo[:, k, n * 128:(n + 1) * 128], o[:, k, :],
                                                                                 start=(k == 0), stop=(k == 7)), reads=[Wo, o], writes=[ps])
                        kb.op(kb.dve, lambda ps=ps, n=n: nc.vector.scalar_tensor_tensor(out=x[:, n, :], in0=ps[:, 0:N1], scalar=self.mvc(l, 2, n, cond), in1=x[:, n, :],
                                                                                       op0=ALU.mult, op1=ALU.add), reads=[ps, self.mv, xs[n]], writes=[xs[n]])

                def sB(c):
                    self.ln_epilogue((sq, mean, var, B[4 + (c % 2) * 2], B[5 + (c % 2) * 2]), xT[c % NB1], N1, l, 0, self.XB, c * N1, sub=xS[c % NB1])

                load1(0)
                load1(1)
                sA(0)
                for c in range(nch1):
                    if c + 2 < nch1:
                        load1(c + 2)
                    if c + 1 < nch1:
                        sA(c + 1)
                    sB(c)
                kb.end_phase()

            W2b = kb.tile(es, "W2b", [128, 16, 1024], BF16)
            self.load_w(W2b, I["w_mlp2"][l][2048:4096, :], 16, parts=4)
            xT = [kb.tile(es, "xT%d" % i, [128, 8, N], F32) for i in range(2)]
            xS = [kb.subs(x_, 8) for x_ in xT]
            hT = [kb.tile(es, "hT%d" % i, [128, 8, N], BF16) for i in range(2)]
            uT = kb.tile(es, "uT", [128, 32, N], BF16)
            rl = [kb.tile(es, "rl%d" % i, [128, 2 * N], F32) for i in range(2)]
            sq = [kb.tile(es, "sq%d" % i, [128, 512], F32) for i in range(2)]
            mean = kb.tile(es, "mean", [128, 512], F32)
            var = kb.tile(es, "var", [128, 512], F32)

            def load(c):
                x = xT[c % 2]
                kb.dma(kb.sp, x[:], self.XB[:, :, c * N:(c + 1) * N].rearrange("k p n -> p k n"), reads=[self.XB], writes=[x])
                KB.loaded(x, xS[c % 2])

            def s1(c):
                cond = 0 if c * N < NPR else 1
                x = xT[c % 2]; h = hT[c % 2]; xs = xS[c % 2]
                for k in range(8):
                    kb.op(kb.act, lambda k=k: nc.scalar.activation(out=h[:, k, :], in_=x[:, k, :], func=AF.Identity,
                                                                   bias=self.mvc(l, 3, k, cond), scale=self.mvc(l, 4, k, cond)),
                          reads=[xs[k], self.mv], writes=[h])
                for fp in range(16):
                    ps = B[fp % 4]
                    for hf in range(2):
                        f = fp * 2 + hf
                        for k in range(8):
                            kb.op(kb.pe, lambda ps=ps, k=k, f=f, hf=hf: nc.tensor.matmul(ps[:, hf * N:(hf + 1) * N], W1[:, k, f * 128:(f + 1) * 128], h[:, k, :],
                                                                                        start=(k == 0), stop=(k == 7)), reads=[W1, h], writes=[ps])
                    r = rl[fp % 2]
                    kb.op(kb.act, lambda ps=ps, r=r: nc.scalar.activation(out=r[:], in_=ps[:, :], func=AF.Relu), reads=[ps], writes=[r])
                    kb.op(kb.pool, lambda r=r, fp=fp: nc.gpsimd.tensor_tensor(out=uT[:, 2 * fp:2 * fp + 2, :], in0=r[:].rearrange("p (a n) -> p a n", a=2),
                                                                            in1=r[:].rearrange("p (a n) -> p a n", a=2), op=ALU.mult),
                          reads=[r], writes=[uT])

            def s2(c):
                cond = 0 if c * N < NPR else 1
                x = xT[c % 2]; xs = xS[c % 2]
                for n in range(8):
                    ps = B[4 + n % 2]
                    for f in range(32):
                        W2x = W2a if f < 16 else W2b
                        kb.op(kb.pe, lambda ps=ps, f=f, n=n, W2x=W2x: nc.tensor.matmul(ps[:, 0:N], W2x[:, f % 16, n * 128:(n + 1) * 128], uT[:, f, :],
                                                                                      start=(f == 0), stop=(f == 31)), reads=[W2x, uT], writes=[ps])
                    kb.op(kb.dve, lambda ps=ps, n=n: nc.vector.scalar_tensor_tensor(out=x[:, n, :], in0=ps[:, 0:N], scalar=self.mvc(l, 5, n, cond), in1=x[:, n, :],
                                                                                   op0=ALU.mult, op1=ALU.add), reads=[ps, self.mv, xs[n]], writes=[xs[n]])

            def s3(c):
                self.ln_epilogue((sq, mean, var, B[6], B[7]), xT[c % 2], N, l, 1, self.XA, c * N, sub=xS[c % 2])

            load(0)
            s1(0)
            for c in range(nchunk):
                if c + 1 < nchunk:
                    load(c + 1)
                s2(c)
                if c + 1 < nchunk:
                    s1(c + 1)
                s3(c)
            kb.end_phase()

    def phase_out(self):
        nc, kb = self.nc, self.kb
        with ExitStack() as es:
            xT = [kb.tile(es, "xT%d" % i, [128, 8, 512], F32) for i in range(2)]
            yo = [kb.tile(es, "yo%d" % i, [128, 4, 1024], F32) for i in range(2)]
            for c in range(NTOK // 512):
                x = xT[c % 2]; y = yo[c % 2]
                t0 = c * 512
                kb.dma(kb.sp, x[:], self.XA[:, :, t0:t0 + 512].rearrange("k p n -> p k n"), reads=[self.XA], writes=[x])
                for j in range(4):
                    for kh in range(2):
                        bk = self.bank[(j * 2 + kh) % 4]
                        for kk in range(4):
                            k = kh * 4 + kk
                            kb.op(kb.pe, lambda bk=bk, j=j, k=k, kk=kk: nc.tensor.transpose(
                                bk[:, kk * 128:(kk + 1) * 128], x[:, k, j * 128:(j + 1) * 128], self.ident[:]),
                                reads=[x, self.ident], writes=[bk])
                        if kh == 0:
                            kb.op(kb.dve, lambda bk=bk, j=j, kh=kh: nc.vector.tensor_copy(out=y[:, j, kh * 512:(kh + 1) * 512], in_=bk[:, :]),
                                  reads=[bk], writes=[y])
                        else:
                            kb.op(kb.act, lambda bk=bk, j=j, kh=kh: nc.scalar.copy(out=y[:, j, kh * 512:(kh + 1) * 512], in_=bk[:, :]),
                                  reads=[bk], writes=[y])
                dst = self.O["yp"] if c < 2 else self.O["ys"]
                r0 = c * 512 if c < 2 else (c - 2) * 512
                kb.dma(kb.sp, dst[r0:r0 + 512, :].rearrange("(j p) f -> p j f", p=128), y[:], reads=[y], writes=[dst])
            kb.end_phase()


_PROG = {}


def _colT(v, ntile):
    return np.ascontiguousarray(np.asarray(v, np.float32).reshape(ntile, 128).T)


def make_in_maps(inp, ncores=8, nl=NLAYERS, names=None):
    f = lambda a: np.ascontiguousarray(np.asarray(a, dtype=np.float32))
    shared = {}
    nlw = max(nl, 1)
    shared["w_mod"] = f(inp["w_mod"][:nlw])
    bm = f(inp["b_mod"]).reshape(4, 6, 8, 128)
    shared["bmodT"] = np.ascontiguousarray(bm.transpose(0, 3, 1, 2).reshape(4, 128, 48))
    g = f(inp["ln_g"]).reshape(4, 2, 8, 128)
    b = f(inp["ln_b"]).reshape(4, 2, 8, 128)
    shared["lngT"] = np.ascontiguousarray(g.transpose(3, 0, 1, 2).reshape(128, 64))
    shared["lnbT"] = np.ascontiguousarray(b.transpose(3, 0, 1, 2).reshape(128, 64))
    shared["w_mlp1"] = f(inp["w_mlp1"][:nlw]); shared["w_mlp2"] = f(inp["w_mlp2"][:nlw])
    shared["l0_w_qkv"] = f(inp["l0_w_qkv"])
    shared["l0_lam_bc"] = np.ascontiguousarray(np.broadcast_to(f(inp["l0_lam"]).reshape(1, 256), (128, 256)))
    shared["l0_subln"] = f(inp["l0_subln"]).reshape(128, 1)
    shared["l0_w_o"] = f(inp["l0_w_o"])
    shared["l1_w_qkv"] = f(inp["l1_w_qkv"])
    E, M = _na_tables(f(inp["l1_rpb"]))
    shared["na_E"] = E; shared["na_M"] = M
    shared["l1_w_o"] = f(inp["l1_w_o"])
    shared["l2_w_a"] = f(inp["l2_w_a"])
    shared["l2_qnormT"] = _colT(inp["l2_q_norm"], 4)
    shared["l2_kvnormT"] = _colT(inp["l2_kv_norm"], 2)
    shared["l2_kvnorm_bc"] = np.ascontiguousarray(np.broadcast_to(f(inp["l2_kv_norm"]).reshape(1, 256), (128, 256)))
    shared["l2_w_uq"] = f(inp["l2_w_uq"]); shared["l2_w_ukv"] = f(inp["l2_w_ukv"]); shared["l2_w_o"] = f(inp["l2_w_o"])
    shared["l3_w_qkv"] = f(inp["l3_w_qkv"])
    shared["l3_sink_bc"] = np.ascontiguousarray(np.broadcast_to(f(inp["l3_sink"]).reshape(1, 16), (128, 16)))
    shared["l3_w_o"] = f(inp["l3_w_o"])
    shared["ident"] = np.eye(128, dtype=np.float32)
    shared["rt64"] = _rot_lhsT(64, [0, 64], 128)
    shared["rt32"] = _rot_lhsT(32, [0, 64], 128)
    c64, s64 = _rope_tables(64, [0, 64])
    c32, s32 = _rope_tables(32, [0, 64])
    shared["cos64"] = c64; shared["sin64"] = s64; shared["cos32"] = c32; shared["sin32"] = s32
    kk = np.arange(128)[:, None]; qq = np.arange(128)[None, :]
    shared["tri"] = np.stack([(qq <= kk), (kk <= qq)]).astype(np.float32)
    maps = []
    cctx = f(inp["c_ctx"])
    if names is not None:
        shared = {k: v for k, v in shared.items() if k in names}
    for i in range(ncores):
        d = dict(shared)
        d["xp"] = f(inp["x_prompt"][4 * i:4 * i + 4]).reshape(NPR, D)
        d["xs"] = f(inp["x_sample"][i]).reshape(NLAT, D)
        d["ck0"] = f(inp["cache_l0_k"][i]).reshape(PAST, 1024); d["cv0"] = f(inp["cache_l0_v"][i]).reshape(PAST, 1024)
        d["ck1"] = f(inp["cache_l1_k"][i]).reshape(PAST, 1024); d["cv1"] = f(inp["cache_l1_v"][i]).reshape(PAST, 1024)
        d["cckv"] = f(inp["cache_l2_ckv"][i]).reshape(PAST, 256); d["ckpe"] = f(inp["cache_l2_kpe"][i]).reshape(PAST, 32)
        d["ck3"] = f(inp["cache_l3_k"][i]).reshape(PAST, 256); d["cv3"] = f(inp["cache_l3_v"][i]).reshape(PAST, 256)
        d["condT"] = np.ascontiguousarray(np.stack([_colT(cctx, 8), _colT(f(inp["c"][i]), 8)], axis=-1))
        if names is not None:
            d = {k: v for k, v in d.items() if k in names}
        maps.append(d)
    return maps


def assemble(results, ncores=8):
    cat = lambda k: np.concatenate([np.asarray(r[k], np.float32) for r in results], axis=0)
    yp = cat("yp").reshape(4 * ncores, 256, D)
    ys = cat("ys").reshape(ncores, NLAT, D)
    k0 = cat("k0").reshape(4 * ncores, 256, 8, 128); v0 = cat("v0").reshape(4 * ncores, 256, 8, 128)
    k1 = cat("k1").reshape(4 * ncores, 256, 16, 64); v1 = cat("v1").reshape(4 * ncores, 256, 16, 64)
    ckv2 = cat("ckv2").reshape(4 * ncores, 256, 256); kpe2 = cat("kpe2").reshape(4 * ncores, 256, 32)
    k3 = cat("k3").reshape(4 * ncores, 256, 4, 64); v3 = cat("v3").reshape(4 * ncores, 256, 4, 64)
    return (yp, ys, k0, v0, k1, v1, ckv2, kpe2, k3, v3)


def kernel(**inputs):
    if "p" not in _PROG:
        _PROG["p"] = Prog(NLAYERS)
    prog = _PROG["p"]
    maps = make_in_maps(inputs, 8, NLAYERS, set(prog.I.keys()))
    res = run_bass_kernel_spmd(prog.nc, maps, core_ids=list(range(8)))
    return assemble(res.results, 8)
```

```python
import math
from contextlib import ExitStack
import numpy as np
import concourse.bass as bass
import concourse.mybir as mybir
from concourse.bass_utils import run_bass_kernel_spmd

F32 = mybir.dt.float32
BF16 = mybir.dt.bfloat16
AF = mybir.ActivationFunctionType
ALU = mybir.AluOpType
AX = mybir.AxisListType

NLAYERS = 4
D = 1024
NPR = 1024
NLAT = 4096
NTOK = NPR + NLAT
PAST = 512
NKEY = NPR + PAST + NLAT
LKEY0 = NPR
LNEW0 = NPR + PAST
ALPHA = 8.0 ** 0.25
LN_EPS = 1e-5
EPS_LN = LN_EPS / (ALPHA * ALPHA)
LAM_INIT = 0.8 - 0.6 * math.exp(-0.3 * 0)
GRID_W = 64
SEM_ROLL = 30000


class Sem:
    __slots__ = ("h", "total", "dma", "id")

    def __init__(self, h, dma, i):
        self.h = h
        self.total = 0
        self.dma = dma
        self.id = i


class Tk:
    __slots__ = ("name", "w", "r", "dsem", "t", "psum")

    def __init__(self, name, t=None):
        self.psum = False
        self.name = name
        self.w = None
        self.r = {}
        self.dsem = None
        self.t = t

    def __getitem__(self, idx):
        return self.t[idx]


class Eng:
    def __init__(self, kb, name, h, compute=True):
        self.kb = kb
        self.name = name
        self.h = h
        self.sem = kb.newsem(False)
        self.waited = {}

    def wait(self, sem, val):
        if sem.dma:
            val = sem.total
        if val <= 0:
            return
        if self.waited.get(sem.id, 0) >= val:
            return
        if sem is self.sem and self.name == "pe":
            return
        self.h.wait_ge(sem.h, val)
        self.waited[sem.id] = val


class KB:
    def __init__(self, nc):
        self.nc = nc
        self.es = ExitStack()
        self.nsem = 0
        self.sems = []
        self.pe = Eng(self, "pe", nc.tensor)
        self.act = Eng(self, "act", nc.scalar)
        self.dve = Eng(self, "dve", nc.vector)
        self.pool = Eng(self, "pool", nc.gpsimd)
        self.sp = Eng(self, "sp", nc.sync)
        self.engs = [self.pe, self.act, self.dve, self.pool, self.sp]
        self.nins = 0
        self.ntile = 0
        self.free_dsems = []
        self.phase_tiles = []

    def newsem(self, dma):
        if dma and self.free_dsems:
            return self.free_dsems.pop()
        h = self.es.enter_context(self.nc.semaphore("s%d" % self.nsem))
        s = Sem(h, dma, self.nsem)
        self.nsem += 1
        self.sems.append(s)
        return s

    def tile(self, es, name, shape, dtype):
        self.ntile += 1
        t = es.enter_context(self.nc.sbuf_tensor("%s_%d" % (name, self.ntile), shape, dtype))
        tk = Tk(name, t)
        self.phase_tiles.append(tk)
        return tk

    def subs(self, tk, n):
        out = [Tk("%s.%d" % (tk.name, i), tk.t) for i in range(n)]
        self.phase_tiles.extend(out)
        return out

    @staticmethod
    def loaded(tk, subs):
        for s_ in subs:
            s_.w = tk.w
            s_.r = {}

    def psum(self, es, name, shape, dtype=F32):
        t = es.enter_context(self.nc.psum_tensor(name, shape, dtype))
        tk = Tk(name, t)
        tk.psum = True
        return tk

    def dram(self, name, shape, dtype, kind="Internal"):
        t = self.nc.dram_tensor(name, shape, dtype, kind=kind)
        return Tk(name, t.ap())

    def _deps(self, eng, reads, writes, nowaw=False):
        for t in reads:
            if t.w is not None:
                eng.wait(*t.w)
            if t.psum:
                for s, v in t.r.values():
                    if s is not eng.sem:
                        eng.wait(s, v)
        for t in writes:
            if t.w is not None and not nowaw:
                eng.wait(*t.w)
            for s, v in t.r.values():
                eng.wait(s, v)

    def op(self, eng, ins, reads=(), writes=()):
        self._deps(eng, reads, writes)
        i = ins()
        if eng.sem.total >= SEM_ROLL:
            eng.sem = self.newsem(False)
        eng.sem.total += 1
        i.then_inc(eng.sem.h, 1)
        ev = (eng.sem, eng.sem.total)
        sid = eng.sem.id
        for t in reads:
            t.r[sid] = ev
        for t in writes:
            t.w = ev
            t.r = {}
        self.nins += 1
        return i

    def dma(self, eng, out_ap, in_ap, reads=(), writes=(), nowaw=True, **kw):
        dst = writes[0]
        self._deps(eng, reads, writes, nowaw=nowaw)
        if dst.dsem is None:
            dst.dsem = self.newsem(True)
        if dst.dsem.total >= SEM_ROLL * 16:
            eng.wait(dst.dsem, dst.dsem.total)
            dst.dsem = self.newsem(True)
        i = eng.h.dma_start(out=out_ap, in_=in_ap, **kw)
        dst.dsem.total += 16
        i.then_inc(dst.dsem.h, 16)
        ev = (dst.dsem, dst.dsem.total)
        for t in reads:
            t.r[dst.dsem.id] = ev
        dst.w = ev
        self.nins += 1
        return i

    def barrier(self):
        for e in self.engs:
            for s in self.sems:
                if s.total > 0:
                    e.wait(s, s.total)

    def end_phase(self):
        self.barrier()
        for tk in self.phase_tiles:
            if tk.dsem is not None:
                self.free_dsems.append(tk.dsem)
                tk.dsem = None
                tk.w = None
                tk.r = {}
        self.phase_tiles = []


def _rope_tables(R, rows):
    n = R // 4
    t = np.arange(NLAT)
    inv = (10000.0 ** (-np.arange(n, dtype=np.float32) / n)).astype(np.float32)
    cos = np.zeros((128, NLAT), np.float32)
    sin = np.zeros((128, NLAT), np.float32)
    half = R // 2
    for base in rows:
        for d in range(R):
            pos = (t // GRID_W) if d < half else (t % GRID_W)
            f = inv[(d % half) % n]
            ang = pos.astype(np.float32) * f
            cos[base + d] = np.cos(ang).astype(np.float32)
            sin[base + d] = np.sin(ang).astype(np.float32)
    return cos, sin


def _rot_lhsT(R, bases, size):
    n = R // 4
    half = R // 2
    Rm = np.zeros((size, size), np.float32)
    for base in bases:
        for d in range(R):
            if (d % half) < n:
                Rm[base + d, base + d + n] = -1.0
            else:
                Rm[base + d, base + d - n] = 1.0
    return np.ascontiguousarray(Rm.T)


def _na_tables(rpb):
    H = rpb.shape[0]
    a = np.arange(2)[:, None, None, None]
    j = np.arange(64)[None, :, None, None]
    w = np.arange(16)[None, None, :, None]
    c = np.arange(64)[None, None, None, :]
    dlt = a - w + 7 + 0 * j + 0 * c
    ri = np.clip(dlt + 7, 0, 14)
    ci = np.clip(j - c + 15 + 0 * a + 0 * w, 0, 30)
    E = rpb[:, ri, ci].reshape(H, 128, 16 * 64).astype(np.float32)
    c0 = np.clip(c - 8, 0, 64 - 16)
    colok = (j >= c0) & (j < c0 + 16)
    m_int = colok & (dlt >= -4) & (dlt <= 3)
    m_bnd = colok & (np.abs(dlt) <= 7)
    M = np.stack([m_int, m_bnd]).reshape(2, 128, 16 * 64).astype(np.float32)
    return E, M


class Prog:
    def __init__(self, nlayers=NLAYERS, ph="MABCD", dbg=(), layers=None):
        self.layers = layers
        self.nl = nlayers
        self.ph = ph
        self.dbg = dbg
        nc = bass.Bass("TRN2", target_bir_lowering=False)
        self.nc = nc
        self.kb = KB(nc)
        self.build()

    def din(self, name, shape, dtype=F32):
        return self.kb.dram(name, list(shape), dtype, kind="ExternalInput")

    def dout(self, name, shape):
        t = self.kb.dram(name, list(shape), F32, kind="ExternalOutput")
        self.outs.append(t)
        return t

    def build(self):
        nc, kb = self.nc, self.kb
        self.outs = []
        nlw = max(self.nl, 1)
        shapes = {
            "xp": [NPR, D], "xs": [NLAT, D],
            "ck0": [PAST, 1024], "cv0": [PAST, 1024], "ck1": [PAST, 1024], "cv1": [PAST, 1024],
            "cckv": [PAST, 256], "ckpe": [PAST, 32], "ck3": [PAST, 256], "cv3": [PAST, 256],
            "condT": [128, 8, 2],
            "w_mod": [nlw, 1024, 6144], "bmodT": [4, 128, 48], "lngT": [128, 64], "lnbT": [128, 64],
            "w_mlp1": [nlw, 1024, 4096], "w_mlp2": [nlw, 4096, 1024],
            "l0_w_qkv": [1024, 3072], "l0_lam_bc": [128, 256], "l0_subln": [128, 1], "l0_w_o": [1024, 1024],
            "l1_w_qkv": [1024, 3072], "na_E": [16, 128, 1024], "na_M": [2, 128, 1024], "l1_w_o": [1024, 1024],
            "l2_w_a": [1024, 800], "l2_qnormT": [128, 4], "l2_kvnormT": [128, 2], "l2_kvnorm_bc": [128, 256],
            "l2_w_uq": [512, 1536], "l2_w_ukv": [256, 2048], "l2_w_o": [1024, 1024],
            "l3_w_qkv": [1024, 1536], "l3_sink_bc": [128, 16], "l3_w_o": [1024, 1024],
            "ident": [128, 128], "rt64": [128, 128], "rt32": [128, 128],
            "cos64": [128, NLAT], "sin64": [128, NLAT], "cos32": [128, NLAT], "sin32": [128, NLAT],
            "tri": [2, 128, 128],
        }
        prog = self

        class LazyIn(dict):
            def __missing__(self, name):
                t = prog.din(name, shapes[name])
                self[name] = t
                return t

        I = self.I = LazyIn()
        O = self.O = {}
        O["yp"] = self.dout("yp", [NPR, D])
        O["ys"] = self.dout("ys", [NLAT, D])
        O["k0"] = self.dout("k0", [NPR, 1024]); O["v0"] = self.dout("v0", [NPR, 1024])
        O["k1"] = self.dout("k1", [NPR, 1024]); O["v1"] = self.dout("v1", [NPR, 1024])
        O["ckv2"] = self.dout("ckv2", [NPR, 256]); O["kpe2"] = self.dout("kpe2", [NPR, 32])
        O["k3"] = self.dout("k3", [NPR, 256]); O["v3"] = self.dout("v3", [NPR, 256])
        kd = lambda n: ("ExternalOutput" if n in self.dbg else "Internal")
        self.XA = kb.dram("XA", [8, 128, NTOK], F32, kind=kd("XA"))
        self.XB = kb.dram("XB", [8, 128, NTOK], F32, kind=kd("XB"))
        self.QT = kb.dram("QT", [16, 128, NTOK], BF16, kind=kd("QT"))
        self.KT = kb.dram("KT", [8, 128, NKEY], BF16, kind=kd("KT"))
        self.VS = kb.dram("VS", [NKEY, 1024], BF16, kind=kd("VS"))
        self.OT = kb.dram("OT", [8, 128, NTOK], BF16, kind=kd("OT"))

        with ExitStack() as es:
            self.ges = es
            self.bank = [kb.psum(es, "bank%d" % i, [128, 512], F32) for i in range(8)]
            self.ones_f = kb.tile(es, "ones_f", [128, 128], F32)
            self.ones_b = kb.tile(es, "ones_b", [128, 128], BF16)
            self.ident = kb.tile(es, "ident", [128, 128], F32)
            self.epsln = kb.tile(es, "epsln", [128, 1], F32)
            self.eps5 = kb.tile(es, "eps5", [128, 1], F32)
            kb.op(kb.dve, lambda: nc.vector.memset(self.ones_f[:], 1.0), writes=[self.ones_f])
            kb.op(kb.dve, lambda: nc.vector.memset(self.ones_b[:], 1.0), writes=[self.ones_b])
            kb.op(kb.dve, lambda: nc.vector.memset(self.epsln[:], EPS_LN), writes=[self.epsln])
            kb.op(kb.dve, lambda: nc.vector.memset(self.eps5[:], LN_EPS), writes=[self.eps5])
            kb.dma(kb.sp, self.ident[:], I["ident"][:, :], writes=[self.ident])
            self.mv = kb.tile(es, "mv", [128, 4 * 6 * 8 * 2], F32)
            self.lng = kb.tile(es, "lng", [128, 64], F32)
            self.lnb = kb.tile(es, "lnb", [128, 64], F32)
            kb.dma(kb.sp, self.lng[:], I["lngT"][:, :], writes=[self.lng])
            kb.dma(kb.sp, self.lnb[:], I["lnbT"][:, :], writes=[self.lnb])

            self.phase_in()
            if "M" in self.ph:
                self.phase_mod()
            for l in range(self.nl):
                if self.layers is not None and l not in self.layers:
                    continue
                if "A" in self.ph:
                    self.phase_A(l)
                if "B" in self.ph:
                    self.phase_B(l)
                if "C" in self.ph or "D" in self.ph:
                    self.phase_C(l)
            self.phase_out()
            kb.end_phase()
        kb.es.close()

    def mvc(self, l, j, k, c):
        i = ((l * 6 + j) * 8 + k) * 2 + c
        return self.mv[:, i:i + 1]

    def lnc(self, t, l, s, k):
        i = (l * 2 + s) * 8 + k
        return t[:, i:i + 1]

    def phase_in(self):
        nc, kb = self.nc, self.kb
        with ExitStack() as es:
            xin = [kb.tile(es, "xin%d" % i, [128, 4, 1024], F32) for i in range(2)]
            xo = [kb.tile(es, "xo%d" % i, [128, 8, 512], F32) for i in range(2)]
            for c in range(NTOK // 512):
                src = self.I["xp"] if c < 2 else self.I["xs"]
                r0 = c * 512 if c < 2 else (c - 2) * 512
                xi = xin[c % 2]
                kb.dma(kb.sp, xi[:], src[r0:r0 + 512, :].rearrange("(j p) f -> p j f", p=128), writes=[xi])
                xt = xo[c % 2]
                for k in range(8):
                    bk = self.bank[k % 4]
                    for j in range(4):
                        kb.op(kb.pe, lambda bk=bk, j=j, k=k, xi=xi: nc.tensor.transpose(
                            bk[:, j * 128:(j + 1) * 128], xi[:, j, k * 128:(k + 1) * 128], self.ident[:]),
                            reads=[xi, self.ident], writes=[bk])
                    if k % 2 == 0:
                        kb.op(kb.dve, lambda bk=bk, k=k, xt=xt: nc.vector.tensor_copy(out=xt[:, k, :], in_=bk[:, :]),
                              reads=[bk], writes=[xt])
                    else:
                        kb.op(kb.act, lambda bk=bk, k=k, xt=xt: nc.scalar.copy(out=xt[:, k, :], in_=bk[:, :]),
                              reads=[bk], writes=[xt])
                kb.dma(kb.sp, self.XA[:, :, c * 512:(c + 1) * 512].rearrange("k p n -> p k n"), xt[:],
                       reads=[xt], writes=[self.XA])
            kb.end_phase()

    def phase_mod(self):
        nc, kb = self.nc, self.kb
        with ExitStack() as es:
            cond = kb.tile(es, "cond", [128, 8, 2], F32)
            sc = kb.tile(es, "silu", [128, 8, 2], BF16)
            kb.dma(kb.sp, cond[:], self.I["condT"][:, :, :], writes=[cond])
            kb.op(kb.act, lambda: nc.scalar.activation(out=sc[:], in_=cond[:], func=AF.Silu), reads=[cond], writes=[sc])
            wm = [kb.tile(es, "wm%d" % i, [128, 8, 1024], BF16) for i in range(2)]
            bm = kb.tile(es, "bm", [128, 4, 48], F32)
            kb.dma(kb.sp, bm[:], self.I["bmodT"][:, :, :].rearrange("l p n -> p l n"), writes=[bm])
            it = 0
            for l in range(self.nl):
                bk = self.bank[l % 2]
                for j in range(6):
                    w = wm[it % 2]
                    it += 1
                    kb.dma(kb.pool, w[:], self.I["w_mod"][l, :, j * 1024:(j + 1) * 1024].rearrange("(k p) n -> p k n", p=128),
                           writes=[w])
                    for nt in range(8):
                        col = (j * 8 + nt) * 2
                        for k in range(8):
                            kb.op(kb.pe, lambda bk=bk, col=col, w=w, k=k, nt=nt: nc.tensor.matmul(
                                bk[:, col:col + 2], w[:, k, nt * 128:(nt + 1) * 128], sc[:, k, :],
                                start=(k == 0), stop=(k == 7)), reads=[w, sc], writes=[bk])
                base = l * 96
                for c in range(2):
                    kb.op(kb.dve, lambda bk=bk, c=c, base=base, l=l: nc.vector.tensor_tensor(
                        out=self.mv[:, base + c:base + 96:2], in0=bk[:, c:96:2], in1=bm[:, l, :], op=ALU.add),
                        reads=[bk, bm], writes=[self.mv])
                for j in (1, 4):
                    a = base + j * 16
                    kb.op(kb.dve, lambda a=a: nc.vector.tensor_scalar_add(out=self.mv[:, a:a + 16], in0=self.mv[:, a:a + 16], scalar1=1.0),
                          reads=[self.mv], writes=[self.mv])
                for j in (2, 5):
                    a = base + j * 16
                    kb.op(kb.dve, lambda a=a: nc.vector.tensor_scalar_mul(out=self.mv[:, a:a + 16], in0=self.mv[:, a:a + 16], scalar1=1.0 / ALPHA),
                          reads=[self.mv], writes=[self.mv])
            kb.end_phase()

    def ln_epilogue(self, es_tiles, tT, N, l, s, dst, c0, sub=None):
        nc, kb = self.nc, self.kb
        sq, mean, var, s1b, s2b = es_tiles
        if sub is None:
            sub = [tT] * 8
        for k in range(8):
            q = sq[k % 2]
            kb.op(kb.act, lambda q=q, k=k: nc.scalar.activation(out=q[:, 0:N], in_=tT[:, k, :], func=AF.Square),
                  reads=[sub[k]], writes=[q])
            kb.op(kb.pe, lambda k=k: nc.tensor.matmul(s1b[:, 0:N], self.ones_f[:], tT[:, k, :], start=(k == 0), stop=(k == 7)),
                  reads=[sub[k], self.ones_f], writes=[s1b])
            kb.op(kb.pe, lambda q=q, k=k: nc.tensor.matmul(s2b[:, 0:N], self.ones_f[:], q[:, 0:N], start=(k == 0), stop=(k == 7)),
                  reads=[q, self.ones_f], writes=[s2b])
        kb.op(kb.act, lambda: nc.scalar.mul(out=mean[:, 0:N], in_=s1b[:, 0:N], mul=1.0 / D), reads=[s1b], writes=[mean])
        kb.op(kb.dve, lambda: nc.vector.scalar_tensor_tensor(out=var[:, 0:N], in0=mean[:, 0:N], scalar=-1.0, in1=mean[:, 0:N],
                                                             op0=ALU.mult, op1=ALU.mult), reads=[mean], writes=[var])
        kb.op(kb.dve, lambda: nc.vector.scalar_tensor_tensor(out=var[:, 0:N], in0=s2b[:, 0:N], scalar=1.0 / D, in1=var[:, 0:N],
                                                             op0=ALU.mult, op1=ALU.add), reads=[s2b, var], writes=[var])
        kb.op(kb.act, lambda: nc.scalar.activation(out=var[:, 0:N], in_=var[:, 0:N], func=AF.Ln, bias=self.epsln[:, 0:1], scale=1.0),
              reads=[var, self.epsln], writes=[var])
        kb.op(kb.act, lambda: nc.scalar.activation(out=var[:, 0:N], in_=var[:, 0:N], func=AF.Exp, scale=-0.5), reads=[var], writes=[var])
        for k in range(8):
            kb.op(kb.dve, lambda k=k: nc.vector.tensor_tensor(out=tT[:, k, :], in0=tT[:, k, :], in1=mean[:, 0:N], op=ALU.subtract),
                  reads=[sub[k], mean], writes=[sub[k]])
            if k % 4 == 3:
                kb.op(kb.dve, lambda k=k: nc.vector.tensor_tensor(out=tT[:, k, :], in0=tT[:, k, :], in1=var[:, 0:N], op=ALU.mult),
                      reads=[sub[k], var], writes=[sub[k]])
            else:
                kb.op(kb.pool, lambda k=k: nc.gpsimd.tensor_tensor(out=tT[:, k, :], in0=tT[:, k, :], in1=var[:, 0:N], op=ALU.mult),
                      reads=[sub[k], var], writes=[sub[k]])
            kb.op(kb.act, lambda k=k: nc.scalar.activation(out=tT[:, k, :], in_=tT[:, k, :], func=AF.Identity,
                                                           bias=self.lnc(self.lnb, l, s, k), scale=self.lnc(self.lng, l, s, k)),
                  reads=[sub[k], self.lng, self.lnb], writes=[sub[k]])
        rd = [tT] + ([] if sub[0] is tT else list(sub))
        kb.dma(kb.sp, dst[:, :, c0:c0 + N].rearrange("k p n -> p k n"), tT[:], reads=rd, writes=[dst])

    def load_w(self, dst, src_ap, nk, parts=1):
        kb = self.kb
        per = nk // parts if nk >= parts else nk
        k = 0
        while k < nk:
            k1 = min(nk, k + max(per, 1))
            kb.dma(kb.pool, dst[:, k:k1, :], src_ap[k * 128:k1 * 128, :].rearrange("(k p) n -> p k n", p=128), writes=[dst])
            k = k1

    def phase_A(self, l):
        nc, kb, I = self.nc, self.kb, self.I
        m = l % 4
        import os
        SK = os.environ.get("SKIP", "")
        TMB = int(os.environ.get("TMB", "6"))
        with ExitStack() as es:
            xT = [kb.tile(es, "xT%d" % i, [128, 8, 512], F32) for i in range(2)]
            hT = [kb.tile(es, "hT%d" % i, [128, 8, 512], BF16) for i in range(2)]
            stq = [kb.tile(es, "stq%d" % i, [128, 512], BF16) for i in range(4)]
            stv = [kb.tile(es, "stv%d" % i, [128, 1024], BF16) for i in range(2)]
            stf = [kb.tile(es, "stf%d" % i, [128, 1024], F32) for i in range(2)]
            self.cnt = 0
            self._rope_pend = None
            if m in (0, 3):
                cos = kb.tile(es, "cos", [128, NLAT], F32); sin = kb.tile(es, "sin", [128, NLAT], F32)
                if "s" not in SK:
                    kb.dma(kb.sp, cos[:], I["cos64"][:, :], writes=[cos]); kb.dma(kb.sp, sin[:], I["sin64"][:, :], writes=[sin])
                rt = kb.tile(es, "rt", [128, 128], BF16)
                kb.dma(kb.pool, rt[:], I["rt64"][:, :], writes=[rt])
            elif m == 2:
                cos = kb.tile(es, "cos", [128, NLAT], F32); sin = kb.tile(es, "sin", [128, NLAT], F32)
                kb.dma(kb.sp, cos[:], I["cos32"][:, :], writes=[cos]); kb.dma(kb.sp, sin[:], I["sin32"][:, :], writes=[sin])
                rt = kb.tile(es, "rt", [128, 128], BF16)
                kb.dma(kb.pool, rt[:], I["rt32"][:, :], writes=[rt])
            else:
                cos = sin = rt = None
            rtmp = [kb.tile(es, "rtmp%d" % i, [128, 512], F32) for i in range(4)]
            qb = [kb.tile(es, "qb%d" % i, [128, 512], BF16) for i in range(2)]

            if m == 0:
                W = kb.tile(es, "Wqkv", [128, 8, 3072], BF16)
                if "w" not in SK:
                    self.load_w(W, I["l0_w_qkv"], 8, parts=4)
            elif m == 1:
                W = kb.tile(es, "Wqkv", [128, 8, 3072], BF16)
                self.load_w(W, I["l1_w_qkv"], 8, parts=4)
            elif m == 3:
                W = kb.tile(es, "Wqkv", [128, 8, 1536], BF16)
                self.load_w(W, I["l3_w_qkv"], 8, parts=2)
            else:
                W = kb.tile(es, "Wa", [128, 8, 800], BF16)
                self.load_w(W, I["l2_w_a"], 8, parts=2)
                Wuq = kb.tile(es, "Wuq", [128, 4, 1536], BF16)
                self.load_w(Wuq, I["l2_w_uq"], 4, parts=2)
                qn = kb.tile(es, "qn", [128, 4], F32); kvn = kb.tile(es, "kvn", [128, 2], F32)
                kvbc = kb.tile(es, "kvbc", [128, 256], F32)
                kb.dma(kb.sp, qn[:], I["l2_qnormT"][:, :], writes=[qn])
                kb.dma(kb.sp, kvn[:], I["l2_kvnormT"][:, :], writes=[kvn])
                kb.dma(kb.sp, kvbc[:], I["l2_kvnorm_bc"][:, :], writes=[kvbc])
                cq = kb.tile(es, "cq", [128, 6, 512], F32)
                cqn = kb.tile(es, "cqn", [128, 6, 512], BF16)
                sq2 = [kb.tile(es, "sq2_%d" % i, [128, 512], F32) for i in range(2)]
                rs = [kb.tile(es, "rs%d" % i, [128, 512], F32) for i in range(2)]
                ss1 = kb.tile(es, "ss1", [128, 1], F32)

            if "c" not in SK:
                self.cache_prep(l, es)

            pbank = self.bank
            bi = [0]

            def nextbank(lo, n):
                b = pbank[lo + bi[0] % n]
                bi[0] += 1
                return b

            def rope_store(ps, rows, lc, dst_ap, dstTk, R_rows=None):
                i = self.cnt
                self.cnt += 1
                st = stq[i % 4]
                r0, r1 = rows
                if lc is None:
                    if i % 2 == 0:
                        kb.op(kb.act, lambda: nc.scalar.copy(out=st[r0:r1, :], in_=ps[r0:r1, :]), reads=[ps], writes=[st])
                    else:
                        kb.op(kb.dve, lambda: nc.vector.tensor_copy(out=st[r0:r1, :], in_=ps[r0:r1, :]), reads=[ps], writes=[st])
                else:
                    rr0, rr1 = R_rows if R_rows is not None else rows
                    q_ = qb[i % 2]
                    kb.op(kb.act, lambda: nc.scalar.copy(out=q_[r0:r1, :], in_=ps[r0:r1, :]), reads=[ps], writes=[q_])
                    t1 = rtmp[(2 * i) % 4]; t2 = rtmp[(2 * i + 1) % 4]
                    cs = slice(lc * 512, (lc + 1) * 512)
                    kb.op(kb.dve, lambda: nc.vector.tensor_tensor(out=t1[rr0:rr1, :], in0=ps[rr0:rr1, :], in1=cos[rr0:rr1, cs], op=ALU.mult),
                          reads=[ps, cos], writes=[t1])
                    if rr0 > r0:
                        kb.op(kb.dve, lambda: nc.vector.tensor_copy(out=st[r0:rr0, :], in_=ps[r0:rr0, :]), reads=[ps], writes=[st])

                    def late():
                        pr = pbank[4 + i % 2]
                        kb.op(kb.pe, lambda: nc.tensor.matmul(pr[r0:r1, :], rt[r0:r1, r0:r1], q_[r0:r1, :], start=True, stop=True),
                              reads=[rt, q_], writes=[pr])
                        kb.op(kb.dve, lambda: nc.vector.tensor_tensor(out=t2[rr0:rr1, :], in0=pr[rr0:rr1, :], in1=sin[rr0:rr1, cs], op=ALU.mult),
                              reads=[pr, sin], writes=[t2])
                        kb.op(kb.pool, lambda: nc.gpsimd.tensor_tensor(out=st[rr0:rr1, :], in0=t1[rr0:rr1, :], in1=t2[rr0:rr1, :], op=ALU.add),
                              reads=[t1, t2], writes=[st])
                        kb.dma(kb.sp, dst_ap, st[r0:r1, :], reads=[st], writes=[dstTk])
                    prev = self._rope_pend
                    self._rope_pend = late
                    if prev is not None:
                        prev()
                    return
                if self._rope_pend is not None:
                    self._rope_pend()
                    self._rope_pend = None
                kb.dma(kb.sp, dst_ap, st[r0:r1, :], reads=[st], writes=[dstTk])

            for c in range(NTOK // 512):
                cond = 0 if c < 2 else 1
                lc = None if (c < 2 or "r" in SK) else c - 2
                t0 = c * 512
                key0 = t0 if c < 2 else LNEW0 + (c - 2) * 512
                x = xT[c % 2]; h = hT[c % 2]
                if c == 0:
                    kb.dma(kb.sp, x[:], self.XA[:, :, t0:t0 + 512].rearrange("k p n -> p k n"), reads=[self.XA], writes=[x])
                if c + 1 < NTOK // 512:
                    kb.dma(kb.sp, xT[(c + 1) % 2][:], self.XA[:, :, t0 + 512:t0 + 1024].rearrange("k p n -> p k n"), reads=[self.XA], writes=[xT[(c + 1) % 2]])
                for k in range(8):
                    kb.op(kb.act, lambda k=k: nc.scalar.activation(out=h[:, k, :], in_=x[:, k, :], func=AF.Identity,
                                                                   bias=self.mvc(l, 0, k, cond), scale=self.mvc(l, 1, k, cond)),
                          reads=[x, self.mv], writes=[h])

                def proj_fm(Wt, nk, col0, ncols, src, ps):
                    for k in range(nk):
                        kb.op(kb.pe, lambda k=k: nc.tensor.matmul(ps[0:ncols, :], Wt[:, k, col0:col0 + ncols], src[:, k, :],
                                                                  start=(k == 0), stop=(k == nk - 1)),
                              reads=[Wt, src], writes=[ps])

                def proj_tm(Wt, col0, ncols, j, ps):
                    for k in range(8):
                        kb.op(kb.pe, lambda k=k: nc.tensor.matmul(ps[:, 0:ncols], h[:, k, j * 128:(j + 1) * 128], Wt[:, k, col0:col0 + ncols],
                                                                  start=(k == 0), stop=(k == 7)),
                              reads=[Wt, h], writes=[ps])

                if "p" in SK:
                    continue
                if m in (0, 1, 3):
                    nq = 8
                    nkt = 8 if m != 3 else 2
                    kcol = 1024
                    vcol = 2048 if m != 3 else 1280
                    vw = 1024 if m != 3 else 256
                    use_rope = (m != 1)
                    for t in range(nq):
                        ps = nextbank(0, 4)
                        proj_fm(W, 8, t * 128, 128, h, ps)
                        rope_store(ps, (0, 128), lc if use_rope else None, self.QT[t, :, t0:t0 + 512], self.QT)
                    for t in range(nkt):
                        ps = nextbank(0, 4)
                        proj_fm(W, 8, kcol + t * 128, 128, h, ps)
                        rope_store(ps, (0, 128), lc if use_rope else None, self.KT[t, :, key0:key0 + 512], self.KT)
                    if self._rope_pend is not None:
                        self._rope_pend()
                        self._rope_pend = None
                    for j in range(4 if "t" not in SK else 0):
                        sv = stv[j % 2]
                        for hf in range(0, vw, 512):
                            n = min(512, vw - hf)
                            ps = pbank[TMB + (j + hf // 512) % 2]
                            proj_tm(W, vcol + hf, n, j, ps)
                            kb.op(kb.act, lambda ps=ps, hf=hf, n=n, sv=sv: nc.scalar.copy(out=sv[:, hf:hf + n], in_=ps[:, 0:n]),
                                  reads=[ps], writes=[sv])
                            if c < 2:
                                sf = stf[0]
                                kb.op(kb.dve, lambda ps=ps, hf=hf, n=n, sf=sf: nc.vector.tensor_copy(out=sf[:, hf:hf + n], in_=ps[:, 0:n]),
                                      reads=[ps], writes=[sf])
                        kb.dma(kb.sp, self.VS[key0 + j * 128:key0 + (j + 1) * 128, 0:vw], sv[:, 0:vw], reads=[sv], writes=[self.VS])
                        if c < 2:
                            vo = self.O["v%d" % m]
                            kb.dma(kb.sp, vo[t0 + j * 128:t0 + (j + 1) * 128, :], stf[0][:, 0:vw], reads=[stf[0]], writes=[vo])
                            sf = stf[1]
                            for hf in range(0, vw, 512):
                                n = min(512, vw - hf)
                                ps = pbank[TMB + (j + hf // 512) % 2]
                                proj_tm(W, kcol + hf, n, j, ps)
                                kb.op(kb.dve, lambda ps=ps, hf=hf, n=n, sf=sf: nc.vector.tensor_copy(out=sf[:, hf:hf + n], in_=ps[:, 0:n]),
                                      reads=[ps], writes=[sf])
                            ko = self.O["k%d" % m]
                            kb.dma(kb.sp, ko[t0 + j * 128:t0 + (j + 1) * 128, :], sf[:, 0:vw], reads=[sf], writes=[ko])
                else:
                    for t in range(6):
                        ps = nextbank(0, 4)
                        proj_fm(W, 8, t * 128, 128, h, ps)
                        kb.op(kb.dve, lambda t=t, ps=ps: nc.vector.tensor_copy(out=cq[:, t, :], in_=ps[:, :]), reads=[ps], writes=[cq])
                        q_ = sq2[t % 2]
                        kb.op(kb.act, lambda ps=ps, q_=q_: nc.scalar.activation(out=q_[:], in_=ps[:, :], func=AF.Square), reads=[ps], writes=[q_])
                        grp = 0 if t < 4 else 1
                        sb = pbank[4 + grp]
                        first = (t == 0) or (t == 4)
                        last = (t == 3) or (t == 5)
                        kb.op(kb.pe, lambda sb=sb, q_=q_, first=first, last=last: nc.tensor.matmul(sb[:, :], self.ones_f[:], q_[:], start=first, stop=last),
                              reads=[q_, self.ones_f], writes=[sb])
                    for grp, nf in ((0, 512), (1, 256)):
                        sb = pbank[4 + grp]
                        r = rs[grp]
                        kb.op(kb.act, lambda sb=sb, r=r, nf=nf: nc.scalar.activation(out=r[:], in_=sb[:, :], func=AF.Ln, bias=self.eps5[:, 0:1], scale=1.0 / nf),
                              reads=[sb, self.eps5], writes=[r])
                        kb.op(kb.act, lambda r=r: nc.scalar.activation(out=r[:], in_=r[:], func=AF.Exp, scale=-0.5), reads=[r], writes=[r])
                    for t in range(6):
                        nrm = qn[:, t:t + 1] if t < 4 else kvn[:, t - 4:t - 3]
                        r = rs[0 if t < 4 else 1]
                        kb.op(kb.dve, lambda t=t, nrm=nrm, r=r: nc.vector.scalar_tensor_tensor(out=cqn[:, t, :], in0=cq[:, t, :], scalar=nrm, in1=r[:],
                                                                                                op0=ALU.mult, op1=ALU.mult),
                              reads=[cq, qn, kvn, r], writes=[cqn])
                    for t in range(2):
                        kb.dma(kb.sp, self.KT[t, :, key0:key0 + 512], cqn[:, 4 + t, :], reads=[cqn], writes=[self.KT])
                    ps = nextbank(0, 4)
                    proj_fm(W, 8, 768, 32, h, ps)
                    rope_store(ps, (0, 32), lc, self.KT[2, 0:32, key0:key0 + 512], self.KT)
                    for hd in range(16):
                        ps = nextbank(0, 4)
                        proj_fm(Wuq, 4, hd * 96, 96, cqn, ps)
                        rope_store(ps, (0, 96), lc, self.QT[hd, 0:96, t0:t0 + 512], self.QT, R_rows=(64, 96))
                    if self._rope_pend is not None:
                        self._rope_pend()
                        self._rope_pend = None
                    if c < 2:
                        for j in range(4):
                            ps = pbank[6 + j % 2]
                            proj_tm(W, 512, 288, j, ps)
                            sf = stf[j % 2]
                            kb.op(kb.dve, lambda: nc.vector.memset(ss1[:], 0.0), writes=[ss1])
                            kb.op(kb.act, lambda ps=ps, sf=sf: nc.scalar.activation(out=sf[:, 512:768], in_=ps[:, 0:256], func=AF.Square, accum_out=ss1[:, 0:1]),
                                  reads=[ps], writes=[sf, ss1])
                            kb.op(kb.act, lambda: nc.scalar.activation(out=ss1[:], in_=ss1[:], func=AF.Ln, bias=self.eps5[:, 0:1], scale=1.0 / 256),
                                  reads=[ss1, self.eps5], writes=[ss1])
                            kb.op(kb.act, lambda: nc.scalar.activation(out=ss1[:], in_=ss1[:], func=AF.Exp, scale=-0.5), reads=[ss1], writes=[ss1])
                            kb.op(kb.dve, lambda ps=ps, sf=sf: nc.vector.scalar_tensor_tensor(out=sf[:, 0:256], in0=ps[:, 0:256], scalar=ss1[:, 0:1], in1=kvbc[:],
                                                                                               op0=ALU.mult, op1=ALU.mult),
                                  reads=[ps, ss1, kvbc], writes=[sf])
                            kb.op(kb.dve, lambda ps=ps, sf=sf: nc.vector.tensor_copy(out=sf[:, 256:288], in_=ps[:, 256:288]), reads=[ps], writes=[sf])
                            kb.dma(kb.sp, self.O["ckv2"][t0 + j * 128:t0 + (j + 1) * 128, :], sf[:, 0:256], reads=[sf], writes=[self.O["ckv2"]])
                            kb.dma(kb.sp, self.O["kpe2"][t0 + j * 128:t0 + (j + 1) * 128, :], sf[:, 256:288], reads=[sf], writes=[self.O["kpe2"]])
            kb.end_phase()

    def cache_prep(self, l, es):
        nc, kb, I = self.nc, self.kb, self.I
        m = l % 4
        if m in (0, 1, 3):
            kw = 1024 if m != 3 else 256
            ck = I["ck%d" % m]; cv = I["cv%d" % m]
            cvb = kb.tile(es, "cvb", [128, 4, 1024], BF16)
            kb.dma(kb.pool, cvb[:, :, 0:kw], cv[:, :].rearrange("(j p) f -> p j f", p=128), writes=[cvb])
            kb.dma(kb.sp, self.VS[LKEY0:LKEY0 + PAST, 0:kw].rearrange("(j p) f -> p j f", p=128), cvb[:, :, 0:kw], reads=[cvb], writes=[self.VS])
            srcs = [(ck, kw, 0)]
        else:
            srcs = [(I["cckv"], 256, 0), (I["ckpe"], 32, 2)]
        cb = kb.tile(es, "cacheb", [128, 4, 1024], F32)
        co = kb.tile(es, "cacheo", [128, 8, 512], BF16)
        for (src, kw, tile0) in srcs:
            kb.dma(kb.sp, cb[:, :, 0:kw], src[:, :].rearrange("(j p) f -> p j f", p=128), writes=[cb], nowaw=False)
            nt = (kw + 127) // 128
            for k in range(nt):
                rows = min(128, kw - k * 128)
                bk = self.bank[6 + k % 2]
                for j in range(4):
                    kb.op(kb.pe, lambda bk=bk, j=j, k=k, rows=rows: nc.tensor.transpose(
                        bk[0:rows, j * 128:(j + 1) * 128], cb[:, j, k * 128:k * 128 + rows], self.ident[:]),
                        reads=[cb, self.ident], writes=[bk])
                kb.op(kb.dve, lambda bk=bk, k=k, rows=rows: nc.vector.tensor_copy(out=co[0:rows, k, :], in_=bk[0:rows, :]),
                      reads=[bk], writes=[co])
                kb.dma(kb.sp, self.KT[tile0 + k, 0:rows, LKEY0:LKEY0 + PAST], co[0:rows, k, :], reads=[co], writes=[self.KT])

    def attn_steps(self, steps, scale, NQ, sbanks, pts):
        nc, kb = self.nc, self.kb
        n = len(steps)
        import os
        LA = int(os.environ.get("LA", "3"))
        NODEN = os.environ.get("NODEN", "") == "1"
        ns, npt = len(sbanks), len(pts)
        g0 = self.gstep

        def qk(i):
            st = steps[i]
            if st.get("pre") is not None:
                st["pre"]()
            bk = sbanks[(g0 + i) % ns]
            nk = st["nk"]
            nq = st.get("nq", NQ)
            kb.op(kb.pe, lambda: nc.tensor.matmul(bk[0:nk, 0:nq], st["kT"], st["q"], start=True, stop=True),
                  reads=st["rtk"], writes=[bk])
            pt = pts[(g0 + i) % npt]
            kb.op(kb.act, lambda: nc.scalar.activation(out=pt[0:nk, 0:nq], in_=bk[0:nk, 0:nq], func=AF.Exp, scale=scale),
                  reads=[bk], writes=[pt])
            if st.get("mask") is not None:
                self.mcnt += 1
                use_dve = (self.mcnt % 3 != 0)
                eng = kb.dve if use_dve else kb.pool
                eh = nc.vector if use_dve else nc.gpsimd
                kb.op(eng, lambda: eh.tensor_tensor(out=pt[0:nk, 0:nq], in0=pt[0:nk, 0:nq], in1=st["mask"], op=ALU.mult),
                      reads=[pt, st["mtk"]], writes=[pt])

        def pv(i):
            st = steps[i]
            pt = pts[(g0 + i) % npt]
            nk = st["nk"]
            nq = st.get("nq", NQ)
            kb.op(kb.pe, lambda: nc.tensor.matmul(st["o"], st["v"], pt[0:nk, 0:nq], start=st["first"], stop=st["last"]),
                  reads=[pt, st["vtk"]], writes=[st["otk"]])
            if st.get("den") is not None and not NODEN:
                kb.op(kb.pe, lambda: nc.tensor.matmul(st["den"], self.ones_b[0:nk, :], pt[0:nk, 0:nq], start=st["first"], stop=st["last"]),
                      reads=[pt, self.ones_b], writes=[st["dtk"]])
            if st.get("post") is not None:
                st["post"]()

        for i in range(min(LA, n)):
            qk(i)
        for i in range(n):
            if i + LA < n:
                qk(i + LA)
            pv(i)
        self.gstep += n

    def phase_B(self, l):
        m = l % 4
        self.gstep = 0
        self.mcnt = 0
        if m == 0:
            self.attn_diff(l)
        elif m == 1:
            self.attn_na(l)
        elif m == 2:
            self.attn_mla(l)
        else:
            self.attn_swa(l)

    def q_chunks(self):
        out = []
        for s in range(4):
            out.append((s * 256, 256, s * 256, 256))
        for c in range(8):
            out.append((NPR + c * 512, 512, LKEY0, PAST + NLAT))
        return out

    def attn_diff(self, l):
        import os
        nc, kb, I = self.nc, self.kb, self.I
        with ExitStack() as es:
            kt = [kb.tile(es, "kt%d" % i, [128, 2, NKEY], BF16) for i in range(2)]
            for k_ in kt:
                kb.op(kb.pool, lambda k_=k_: nc.gpsimd.memset(k_[64:128, 0, :], 0.0), writes=[k_])
                kb.op(kb.pool, lambda k_=k_: nc.gpsimd.memset(k_[0:64, 1, :], 0.0), writes=[k_])
            vt = [kb.tile(es, "vt%d" % i, [128, NKEY // 128, 128], BF16) for i in range(2)]
            qt = [kb.tile(es, "qt%d" % i, [128, 512], BF16) for i in range(3)]
            pts = [kb.tile(es, "pt%d" % i, [128, 512], BF16) for i in range(6)]
            accs = [[kb.tile(es, "acc%d_%d" % (i, j), [128, 512], F32) for j in range(4)] for i in range(2)]
            self._pend = None
            ot = [kb.tile(es, "ot%d" % i, [128, 512], BF16) for i in range(2)]
            ra = [kb.tile(es, "ra%d" % i, [128, 512], F32) for i in range(2)]
            fa = [kb.tile(es, "fa%d" % i, [128, 512], F32) for i in range(2)]
            sq = kb.tile(es, "sq", [128, 512], F32)
            rstd = kb.tile(es, "rstd", [128, 512], F32)
            lam = kb.tile(es, "lam", [128, 256], F32)
            lp = kb.tile(es, "lp", [128, 128], F32)
            ls = kb.tile(es, "ls", [128, 2], F32)
            nlam = kb.tile(es, "nlam", [128, 1], F32)
            subw = kb.tile(es, "subw", [128, 1], F32)
            kb.dma(kb.sp, lam[:], I["l0_lam_bc"][:, :], writes=[lam])
            kb.dma(kb.sp, subw[:], I["l0_subln"][:, :], writes=[subw])
            kb.op(kb.dve, lambda: nc.vector.tensor_tensor(out=lp[:, 0:64], in0=lam[:, 0:64], in1=lam[:, 64:128], op=ALU.mult), reads=[lam], writes=[lp])
            kb.op(kb.dve, lambda: nc.vector.tensor_tensor(out=lp[:, 64:128], in0=lam[:, 128:192], in1=lam[:, 192:256], op=ALU.mult), reads=[lam], writes=[lp])
            kb.op(kb.dve, lambda: nc.vector.reduce_sum(out=ls[:, 0:1], in_=lp[:, 0:64], axis=AX.X), reads=[lp], writes=[ls])
            kb.op(kb.dve, lambda: nc.vector.reduce_sum(out=ls[:, 1:2], in_=lp[:, 64:128], axis=AX.X), reads=[lp], writes=[ls])
            kb.op(kb.act, lambda: nc.scalar.activation(out=ls[:], in_=ls[:], func=AF.Exp), reads=[ls], writes=[ls])
            kb.op(kb.dve, lambda: nc.vector.tensor_tensor(out=nlam[:], in0=ls[:, 1:2], in1=ls[:, 0:1], op=ALU.subtract), reads=[ls], writes=[nlam])
            kb.op(kb.dve, lambda: nc.vector.tensor_scalar_add(out=nlam[:], in0=nlam[:], scalar1=-LAM_INIT), reads=[nlam], writes=[nlam])
            kb.op(kb.dve, lambda: nc.vector.tensor_scalar_mul(out=subw[:], in0=subw[:], scalar1=(1.0 - LAM_INIT)), reads=[subw], writes=[subw])
            B = self.bank
            sb = [B[4], B[5], B[6]]
            scale = 64 ** -0.5
            ci = 0
            chunks = self.q_chunks()

            def load_head(h):
                k_ = kt[h % 2]; v_ = vt[h % 2]
                kb.dma(kb.sp, k_[0:64, 0, :], self.KT[h, 0:64, :], reads=[self.KT], writes=[k_], nowaw=False)
                kb.dma(kb.sp, k_[64:128, 1, :], self.KT[h, 64:128, :], reads=[self.KT], writes=[k_])
                for j4 in range(4):
                    kb.dma(kb.sp, v_[:, j4 * 11:(j4 + 1) * 11, :], self.VS[j4 * 1408:(j4 + 1) * 1408, h * 128:(h + 1) * 128].rearrange("(t p) d -> p t d", p=128),
                           reads=[self.VS], writes=[v_])

            def load_q(idx):
                h, c = divmod(idx, len(chunks))
                (t0, nq, key0, nkeys) = chunks[c]
                q_ = qt[idx % 3]
                kb.dma(kb.sp, q_[:, 0:nq], self.QT[h, :, t0:t0 + nq], reads=[self.QT], writes=[q_])

            load_head(0)
            load_q(0)
            for h in range(8):
                k_ = kt[h % 2]; v_ = vt[h % 2]
                if h + 1 < 8:
                    load_head(h + 1)
                allsteps = []
                for (t0, nq, key0, nkeys) in chunks:
                    q_ = qt[ci % 3]

                    def pre(ci=ci):
                        if ci + 1 < 8 * len(chunks):
                            load_q(ci + 1)
                    steps = []
                    nkt = nkeys // 128
                    for i in range(nkt):
                        kk = key0 + i * 128
                        for hf in range(2):
                            steps.append(dict(kT=k_[:, hf, kk:kk + 128], q=q_[:, 0:nq], nk=128, nq=nq,
                                              v=v_[:, kk // 128, :], o=B[hf * 2][:, 0:nq], otk=B[hf * 2],
                                              den=B[hf * 2 + 1][:, 0:nq], dtk=B[hf * 2 + 1],
                                              first=(i == 0), last=(i == nkt - 1), rtk=[k_, q_], vtk=v_))

                    def post(ci=ci, nq=nq, h=h, t0=t0):
                        acc = accs[ci % 2]
                        if self._pend is not None:
                            self._pend()
                        for j in range(4):
                            kb.op(kb.dve, lambda j=j: nc.vector.tensor_copy(out=acc[j][:, 0:nq], in_=B[j][:, 0:nq]), reads=[B[j]], writes=[acc[j]])

                        def fin():
                            for hf in range(2):
                                kb.op(kb.dve, lambda hf=hf: nc.vector.reciprocal(out=ra[hf][:, 0:nq], in_=acc[hf * 2 + 1][:, 0:nq]), reads=[acc[hf * 2 + 1]], writes=[ra[hf]])
                                kb.op(kb.dve, lambda hf=hf: nc.vector.tensor_tensor(out=fa[hf][:, 0:nq], in0=acc[hf * 2][:, 0:nq], in1=ra[hf][:, 0:nq], op=ALU.mult),
                                      reads=[acc[hf * 2], ra[hf]], writes=[fa[hf]])
                            kb.op(kb.dve, lambda: nc.vector.scalar_tensor_tensor(out=fa[0][:, 0:nq], in0=fa[1][:, 0:nq], scalar=nlam[:, 0:1], in1=fa[0][:, 0:nq],
                                                                                 op0=ALU.mult, op1=ALU.add), reads=[fa[1], nlam, fa[0]], writes=[fa[0]])
                            kb.op(kb.pool, lambda: nc.gpsimd.tensor_tensor(out=sq[:, 0:nq], in0=fa[0][:, 0:nq], in1=fa[0][:, 0:nq], op=ALU.mult), reads=[fa[0]], writes=[sq])
                            kb.op(kb.pe, lambda: nc.tensor.matmul(B[7][:, 0:nq], self.ones_f[:], sq[:, 0:nq], start=True, stop=True),
                                  reads=[sq, self.ones_f], writes=[B[7]])
                            kb.op(kb.act, lambda: nc.scalar.activation(out=rstd[:, 0:nq], in_=B[7][:, 0:nq], func=AF.Ln, bias=self.eps5[:, 0:1], scale=1.0 / 128),
                                  reads=[B[7], self.eps5], writes=[rstd])
                            kb.op(kb.act, lambda: nc.scalar.activation(out=rstd[:, 0:nq], in_=rstd[:, 0:nq], func=AF.Exp, scale=-0.5), reads=[rstd], writes=[rstd])
                            o_ = ot[ci % 2]
                            kb.op(kb.dve, lambda: nc.vector.scalar_tensor_tensor(out=o_[:, 0:nq], in0=fa[0][:, 0:nq], scalar=subw[:, 0:1], in1=rstd[:, 0:nq],
                                                                                 op0=ALU.mult, op1=ALU.mult), reads=[fa[0], subw, rstd], writes=[o_])
                            kb.dma(kb.sp, self.OT[h, :, t0:t0 + nq], o_[:, 0:nq], reads=[o_], writes=[self.OT])
                        self._pend = fin
                    steps[0]["pre"] = pre
                    steps[-1]["post"] = post
                    allsteps.extend(steps)
                    ci += 1
                self.attn_steps(allsteps, scale, 512, sb, pts)
            if self._pend is not None:
                self._pend()
                self._pend = None
            kb.end_phase()

    def finalize_aug(self, ps, nq, rb, o_, dst_ap, extra_den=None, act_recip=False):
        nc, kb = self.nc, self.kb
        if extra_den is not None:
            tk, ap = extra_den
            if act_recip:
                kb.op(kb.act, lambda: nc.scalar.activation(out=rb[64:128, 0:nq], in_=ps[64:128, 0:nq], func=AF.Ln, bias=ap, scale=1.0),
                      reads=[ps, tk], writes=[rb])
                kb.op(kb.act, lambda: nc.scalar.activation(out=rb[64:128, 0:nq], in_=rb[64:128, 0:nq], func=AF.Exp, scale=-1.0), reads=[rb], writes=[rb])
            else:
                kb.op(kb.dve, lambda: nc.vector.tensor_scalar_add(out=rb[64:128, 0:nq], in0=ps[64:128, 0:nq], scalar1=ap),
                      reads=[ps, tk], writes=[rb])
                kb.op(kb.dve, lambda: nc.vector.reciprocal(out=rb[64:128, 0:nq], in_=rb[64:128, 0:nq]), reads=[rb], writes=[rb])
        elif act_recip:
            kb.op(kb.act, lambda: nc.scalar.activation(out=rb[64:128, 0:nq], in_=ps[64:128, 0:nq], func=AF.Ln), reads=[ps], writes=[rb])
            kb.op(kb.act, lambda: nc.scalar.activation(out=rb[64:128, 0:nq], in_=rb[64:128, 0:nq], func=AF.Exp, scale=-1.0), reads=[rb], writes=[rb])
        else:
            kb.op(kb.dve, lambda: nc.vector.reciprocal(out=rb[64:128, 0:nq], in_=ps[64:128, 0:nq]), reads=[ps], writes=[rb])
        kb.op(kb.dve, lambda: nc.vector.tensor_tensor(out=o_[0:64, 0:nq], in0=ps[0:64, 0:nq], in1=rb[64:128, 0:nq], op=ALU.mult),
              reads=[ps, rb], writes=[o_])
        kb.dma(kb.sp, dst_ap, o_[0:64, 0:nq], reads=[o_], writes=[self.OT])

    def attn_na(self, l):
        import os
        AR = os.environ.get("AR", "1") == "1"
        nc, kb, I = self.nc, self.kb, self.I
        with ExitStack() as es:
            kt = [kb.tile(es, "kt%d" % i, [128, NKEY], BF16) for i in range(2)]
            kb.op(kb.pool, lambda: nc.gpsimd.memset(kt[0][64:128, :], 0.0), writes=[kt[0]])
            kb.op(kb.pool, lambda: nc.gpsimd.memset(kt[1][0:64, :], 0.0), writes=[kt[1]])
            vt = [kb.tile(es, "vt%d" % i, [128, NKEY // 128, 128], BF16) for i in range(2)]
            qt = [kb.tile(es, "qt%d" % i, [128, 512], BF16) for i in range(3)]
            pts = [kb.tile(es, "pt%d" % i, [128, 512], BF16) for i in range(4)]
            ot = [kb.tile(es, "ot%d" % i, [64, 512], BF16) for i in range(3)]
            rb = [kb.tile(es, "rb%d" % i, [128, 512], F32) for i in range(2)]
            Mt = kb.tile(es, "Mt", [128, 2, 1024], F32)
            Et = [kb.tile(es, "Et%d" % i, [128, 1024], F32) for i in range(2)]
            tabs = [kb.tile(es, "tab%d" % i, [128, 2, 1024], BF16) for i in range(2)]
            kb.dma(kb.sp, Mt[:], I["na_M"][:, :, :].rearrange("v p n -> p v n"), writes=[Mt])
            for v_ in vt:
                kb.op(kb.pool, lambda v_=v_: nc.gpsimd.memset(v_[:, :, 64:128], 1.0), writes=[v_])
            B = self.bank
            sb = [B[4], B[5], B[6], B[7]]
            ob = [B[0], B[1], B[2], B[3]]
            scale = 64 ** -0.5
            ci = 0
            chunks = [(sq_ * 256, 256) for sq_ in range(4)] + [(NPR + c_ * 512, 512) for c_ in range(8)]

            def load_head(h):
                k_ = kt[h % 2]; v_ = vt[h % 2]
                base = (h % 2) * 64
                kb.dma(kb.sp, k_[base:base + 64, :], self.KT[h // 2, base:base + 64, :], reads=[self.KT], writes=[k_], nowaw=False)
                for j4 in range(4):
                    kb.dma(kb.sp, v_[:, j4 * 11:(j4 + 1) * 11, 0:64], self.VS[j4 * 1408:(j4 + 1) * 1408, h * 64:(h + 1) * 64].rearrange("(t p) d -> p t d", p=128),
                           reads=[self.VS], writes=[v_], nowaw=(j4 > 0))

            def load_q(idx):
                h, c = divmod(idx, len(chunks))
                (t0, nq) = chunks[c]
                kb.dma(kb.sp, qt[idx % 3][:, 0:nq], self.QT[h // 2, :, t0:t0 + nq], reads=[self.QT], writes=[qt[idx % 3]])

            load_head(0)
            load_q(0)
            allsteps = []

            def add(steps, nq_default, pre, post):
                for st in steps:
                    st.setdefault("nq", nq_default)
                if pre is not None:
                    steps[0]["pre"] = pre
                if post is not None:
                    steps[-1]["post"] = post
                allsteps.extend(steps)

            for h in range(16):
                k_ = kt[h % 2]; v_ = vt[h % 2]; E_ = Et[h % 2]; tb = tabs[h % 2]
                tl, base = h // 2, (h % 2) * 64
                def head_setup(h=h, E_=E_, tb=tb):
                    kb.dma(kb.sp, E_[:], I["na_E"][h, :, :], writes=[E_])
                    kb.op(kb.act, lambda: nc.scalar.activation(out=E_[:], in_=E_[:], func=AF.Exp), reads=[E_], writes=[E_])
                    for vv in range(2):
                        kb.op(kb.dve, lambda vv=vv: nc.vector.tensor_tensor(out=tb[:, vv, :], in0=E_[:], in1=Mt[:, vv, :], op=ALU.mult),
                              reads=[E_, Mt], writes=[tb])
                for s in range(4):
                    t0 = s * 256
                    q_ = qt[ci % 3]

                    def pre(ci=ci, s=s, hs=head_setup, h=h):
                        if s == 0:
                            hs()
                        if s == 2 and h + 1 < 16:
                            load_head(h + 1)
                        if ci + 1 < 16 * 12:
                            load_q(ci + 1)
                    o_b = ob[ci % 4]
                    steps = []
                    for i in range(2):
                        kk = t0 + i * 128
                        steps.append(dict(kT=k_[:, kk:kk + 128], q=q_[:, 0:256], nk=128, v=v_[:, kk // 128, :], o=o_b[:, 0:256], otk=o_b,
                                          first=(i == 0), last=(i == 1), rtk=[k_, q_], vtk=v_))

                    def post(o_b=o_b, ci=ci, tl=tl, base=base, t0=t0):
                        self.finalize_aug(o_b, 256, rb[ci % 2], ot[ci % 3], self.OT[tl, base:base + 64, t0:t0 + 256], act_recip=AR)
                    add(steps, 256, pre, post)
                    ci += 1
                for c in range(8):
                    t0 = NPR + c * 512
                    q_ = qt[ci % 3]

                    def pre(ci=ci):
                        if ci + 1 < 16 * 12:
                            load_q(ci + 1)
                    o_b = ob[ci % 4]
                    cst = []
                    if 1 <= c <= 6:
                        steps = []
                        for i in range(4):
                            kk = LKEY0 + i * 128
                            steps.append(dict(kT=k_[:, kk:kk + 128], q=q_[:, 0:512], nk=128, nq=512, v=v_[:, kk // 128, :], o=o_b[:, 0:512], otk=o_b,
                                              first=(i == 0), last=False, rtk=[k_, q_], vtk=v_))
                        rks = list(range(8 * c - 4, 8 * c + 12, 2))
                        for idx, Rk in enumerate(rks):
                            lo = max(Rk - 4, 8 * c); hi = min(Rk + 4, 8 * c + 6)
                            q0 = (lo - 8 * c) // 2 * 128; q1 = ((hi - 8 * c) // 2 + 1) * 128
                            w0 = 7 + lo - Rk
                            kk = LNEW0 + Rk * 64
                            steps.append(dict(kT=k_[:, kk:kk + 128], q=q_[:, q0:q1], nk=128, nq=q1 - q0, v=v_[:, kk // 128, :], o=o_b[:, q0:q1], otk=o_b,
                                              first=False, last=(idx == len(rks) - 1), rtk=[k_, q_], vtk=v_,
                                              mask=tb[:, 0, w0 * 64:w0 * 64 + (q1 - q0)], mtk=tb))
                        cst.extend(steps)
                    else:
                        steps = []
                        for i in range(4):
                            kk = LKEY0 + i * 128
                            steps.append(dict(kT=k_[:, kk:kk + 128], q=q_[:, 0:512], nk=128, nq=512, v=v_[:, kk // 128, :], o=o_b[:, 0:512], otk=o_b,
                                              first=(i == 0), last=False, rtk=[k_, q_], vtk=v_))
                        cst.extend(steps)
                        for sub in range(4):
                            mrow = c * 4 + sub
                            Rq = 2 * mrow
                            if mrow in (0, 1):
                                rks, var = [0, 2, 4, 6], 1
                            elif mrow in (30, 31):
                                rks, var = [56, 58, 60, 62], 1
                            else:
                                rks, var = [Rq - 4 + 2 * t for t in range(5)], 0
                            qs = q_[:, sub * 128:(sub + 1) * 128]
                            oap = o_b[:, sub * 128:(sub + 1) * 128]
                            steps = []
                            for i, Rk in enumerate(rks):
                                kk = LNEW0 + Rk * 64
                                w0 = 7 + Rq - Rk
                                steps.append(dict(kT=k_[:, kk:kk + 128], q=qs, nk=128, nq=128, v=v_[:, kk // 128, :], o=oap, otk=o_b,
                                                  first=False, last=(i == len(rks) - 1), rtk=[k_, q_], vtk=v_,
                                                  mask=tb[:, var, w0 * 64:(w0 + 2) * 64], mtk=tb))
                            cst.extend(steps)

                    def post(o_b=o_b, ci=ci, tl=tl, base=base, t0=t0):
                        self.finalize_aug(o_b, 512, rb[ci % 2], ot[ci % 3], self.OT[tl, base:base + 64, t0:t0 + 512], act_recip=AR)
                    add(cst, 512, pre, post)
                    ci += 1
            self.attn_steps(allsteps, scale, 512, sb, pts)
            kb.end_phase()

    def attn_swa(self, l):
        import os
        AR = os.environ.get("AR", "1") == "1"
        nc, kb, I = self.nc, self.kb, self.I
        with ExitStack() as es:
            kt = [kb.tile(es, "kt%d" % i, [128, 2, NKEY], BF16) for i in range(2)]
            for k_ in kt:
                kb.op(kb.pool, lambda k_=k_: nc.gpsimd.memset(k_[64:128, 0, :], 0.0), writes=[k_])
                kb.op(kb.pool, lambda k_=k_: nc.gpsimd.memset(k_[0:64, 1, :], 0.0), writes=[k_])
            vt = [kb.tile(es, "vt%d" % i, [128, NKEY // 128, 128], BF16) for i in range(2)]
            qt = [kb.tile(es, "qt%d" % i, [128, 512], BF16) for i in range(3)]
            pts = [kb.tile(es, "pt%d" % i, [128, 512], BF16) for i in range(4)]
            ot = [kb.tile(es, "ot%d" % i, [64, 512], BF16) for i in range(3)]
            rb = [kb.tile(es, "rb%d" % i, [128, 512], F32) for i in range(2)]
            M3 = kb.tile(es, "M3", [128, 384], BF16)
            esink = kb.tile(es, "esink", [128, 16], F32)
            kb.op(kb.pool, lambda: nc.gpsimd.memset(M3[:, 128:256], 1.0), writes=[M3])
            kb.dma(kb.pool, M3[:, 0:128], I["tri"][1, :, :], writes=[M3], nowaw=False)
            kb.dma(kb.pool, M3[:, 256:384], I["tri"][0, :, :], writes=[M3])
            kb.dma(kb.sp, esink[:], I["l3_sink_bc"][:, :], writes=[esink])
            kb.op(kb.act, lambda: nc.scalar.activation(out=esink[:], in_=esink[:], func=AF.Exp), reads=[esink], writes=[esink])
            for v_ in vt:
                kb.op(kb.pool, lambda v_=v_: nc.gpsimd.memset(v_[:, :, 64:128], 1.0), writes=[v_])
            B = self.bank
            sb = [B[4], B[5], B[6], B[7]]
            ob = [B[0], B[1], B[2], B[3]]
            scale = 64 ** -0.5
            ci = 0
            chunks = [(sq_ * 256, 256) for sq_ in range(4)] + [(NPR + c_ * 512, 512) for c_ in range(8)]

            def load_group(g):
                k2 = kt[g % 2]; v_ = vt[g % 2]
                src = self.KT[g // 2, (g % 2) * 64:(g % 2) * 64 + 64, :]
                kb.dma(kb.sp, k2[0:64, 0, :], src, reads=[self.KT], writes=[k2], nowaw=False)
                kb.dma(kb.sp, k2[64:128, 1, :], src, reads=[self.KT], writes=[k2])
                for j4 in range(4):
                    kb.dma(kb.sp, v_[:, j4 * 11:(j4 + 1) * 11, 0:64], self.VS[j4 * 1408:(j4 + 1) * 1408, g * 64:(g + 1) * 64].rearrange("(t p) d -> p t d", p=128),
                           reads=[self.VS], writes=[v_], nowaw=(j4 > 0))

            def load_q(idx):
                h, c = divmod(idx, len(chunks))
                (t0, nq) = chunks[c]
                kb.dma(kb.sp, qt[idx % 3][:, 0:nq], self.QT[h // 2, :, t0:t0 + nq], reads=[self.QT], writes=[qt[idx % 3]])

            load_group(0)
            load_q(0)
            allsteps = []

            def add(steps, nq_default, pre, post):
                for st in steps:
                    st.setdefault("nq", nq_default)
                if pre is not None:
                    steps[0]["pre"] = pre
                if post is not None:
                    steps[-1]["post"] = post
                allsteps.extend(steps)

            for g in range(4):
                k2 = kt[g % 2]; v_ = vt[g % 2]
                for hh in range(4):
                    h = g * 4 + hh
                    tl, base = h // 2, (h % 2) * 64
                    kv = h % 2
                    sk = (esink, esink[64:128, h:h + 1])
                    for s in range(4):
                        t0 = s * 256
                        q_ = qt[ci % 3]

                        def pre(ci=ci, first=(hh == 0 and s == 2), g=g):
                            if first and g + 1 < 4:
                                load_group(g + 1)
                            if ci + 1 < 16 * 12:
                                load_q(ci + 1)
                        o_b = ob[ci % 4]
                        steps = []
                        for i in range(2):
                            kk = t0 + i * 128
                            steps.append(dict(kT=k2[:, kv, kk:kk + 128], q=q_[:, 0:256], nk=128, v=v_[:, kk // 128, :], o=o_b[:, 0:256], otk=o_b,
                                              first=(i == 0), last=(i == 1), rtk=[k2, q_], vtk=v_))

                        def post(o_b=o_b, ci=ci, tl=tl, base=base, t0=t0, sk=sk):
                            self.finalize_aug(o_b, 256, rb[ci % 2], ot[ci % 3], self.OT[tl, base:base + 64, t0:t0 + 256], extra_den=sk, act_recip=AR)
                        add(steps, 256, pre, post)
                        ci += 1
                    for c in range(8):
                        t0 = NPR + c * 512
                        q_ = qt[ci % 3]

                        def pre(ci=ci):
                            if ci + 1 < 16 * 12:
                                load_q(ci + 1)
                        o_b = ob[ci % 4]
                        steps = []
                        for i in range(4):
                            kk = LKEY0 + i * 128
                            steps.append(dict(kT=k2[:, kv, kk:kk + 128], q=q_[:, 0:512], nk=128, nq=512, v=v_[:, kk // 128, :], o=o_b[:, 0:512], otk=o_b,
                                              first=(i == 0), last=False, rtk=[k2, q_], vtk=v_))
                        band = []
                        for j in range(4 * c - 1, 4 * c + 5):
                            if j < 0 or j > 31:
                                continue
                            blo = max(j - 1, 4 * c); bhi = min(j + 1, 4 * c + 3)
                            if blo > bhi:
                                continue
                            band.append((j, blo, bhi))
                        for idx, (j, blo, bhi) in enumerate(band):
                            q0 = (blo - 4 * c) * 128; q1 = (bhi - 4 * c + 1) * 128
                            kk = LNEW0 + j * 128
                            steps.append(dict(kT=k2[:, kv, kk:kk + 128], q=q_[:, q0:q1], nk=128, nq=q1 - q0, v=v_[:, kk // 128, :], o=o_b[:, q0:q1], otk=o_b,
                                              first=False, last=(idx == len(band) - 1), rtk=[k2, q_], vtk=v_,
                                              mask=M3[:, (blo - j + 1) * 128:(bhi - j + 2) * 128], mtk=M3))
                        def post(o_b=o_b, ci=ci, tl=tl, base=base, t0=t0, sk=sk):
                            self.finalize_aug(o_b, 512, rb[ci % 2], ot[ci % 3], self.OT[tl, base:base + 64, t0:t0 + 512], extra_den=sk, act_recip=AR)
                        add(steps, 512, pre, post)
                        ci += 1
            self.attn_steps(allsteps, scale, 512, sb, pts)
            kb.end_phase()

    def attn_mla(self, l):
        nc, kb, I = self.nc, self.kb, self.I
        with ExitStack() as es:
            ckv = kb.tile(es, "ckvT", [128, 2, NKEY], BF16)
            kt = [kb.tile(es, "kt%d" % i, [96, NKEY], BF16) for i in range(2)]
            vt = [kb.tile(es, "vt%d" % i, [128, NKEY // 128, 128], BF16) for i in range(2)]
            qt = [kb.tile(es, "qt%d" % i, [96, 512], BF16) for i in range(3)]
            pts = [kb.tile(es, "pt%d" % i, [128, 512], BF16) for i in range(4)]
            ot = [kb.tile(es, "ot%d" % i, [64, 512], BF16) for i in range(3)]
            rb = [kb.tile(es, "rb%d" % i, [128, 512], F32) for i in range(2)]
            Wukv = kb.tile(es, "Wukv", [128, 2, 2048], BF16)
            self.load_w(Wukv, I["l2_w_ukv"], 2)
            kb.dma(kb.sp, ckv[:], self.KT[0:2, :, :].rearrange("k p n -> p k n"), reads=[self.KT], writes=[ckv])
            for k_ in kt:
                kb.dma(kb.sp, k_[64:96, :], self.KT[2, 0:32, :], reads=[self.KT], writes=[k_])
            for v_ in vt:
                kb.op(kb.pool, lambda v_=v_: nc.gpsimd.memset(v_[:, :, 64:128], 1.0), writes=[v_])
            B = self.bank
            sb = [B[4], B[5], B[6]]
            ob = [B[0], B[1]]
            xb = [B[2], B[3], B[7]]
            scale = 96 ** -0.5
            ci = 0
            xi = 0
            chunks = self.q_chunks()

            def load_q(idx):
                h, c = divmod(idx, len(chunks))
                (t0, nq, key0, nkeys) = chunks[c]
                kb.dma(kb.sp, qt[idx % 3][:, 0:nq], self.QT[h, 0:96, t0:t0 + nq], reads=[self.QT], writes=[qt[idx % 3]])

            load_q(0)
            for h in range(16):
                k_ = kt[h % 2]; v_ = vt[h % 2]
                tl, base = h // 2, (h % 2) * 64
                for kc in range(NKEY // 512):
                    ps = xb[xi % 3]; xi += 1
                    for k in range(2):
                        kb.op(kb.pe, lambda ps=ps, k=k, kc=kc: nc.tensor.matmul(ps[0:64, :], Wukv[:, k, h * 128:h * 128 + 64], ckv[:, k, kc * 512:(kc + 1) * 512],
                                                                               start=(k == 0), stop=(k == 1)), reads=[Wukv, ckv], writes=[ps])
                    kb.op(kb.dve, lambda ps=ps, kc=kc, k_=k_: nc.vector.tensor_copy(out=k_[0:64, kc * 512:(kc + 1) * 512], in_=ps[0:64, :]),
                          reads=[ps], writes=[k_])
                ntl = NKEY // 128
                for tg in range(0, ntl, 8):
                    ps = xb[xi % 3]; xi += 1
                    nt = min(8, ntl - tg)
                    for t in range(nt):
                        for k in range(2):
                            kb.op(kb.pe, lambda ps=ps, k=k, t=t, tg=tg: nc.tensor.matmul(ps[:, t * 64:(t + 1) * 64], ckv[:, k, (tg + t) * 128:(tg + t + 1) * 128],
                                                                                        Wukv[:, k, h * 128 + 64:h * 128 + 128], start=(k == 0), stop=(k == 1)),
                                  reads=[Wukv, ckv], writes=[ps])
                    kb.op(kb.dve, lambda ps=ps, tg=tg, nt=nt, v_=v_: nc.vector.tensor_copy(
                        out=v_[:, tg:tg + nt, 0:64], in_=ps[:, 0:nt * 64].rearrange("p (t d) -> p t d", d=64)),
                        reads=[ps], writes=[v_])
                allsteps = []
                for (t0, nq, key0, nkeys) in chunks:
                    q_ = qt[ci % 3]

                    def pre(ci=ci):
                        if ci + 1 < 16 * len(chunks):
                            load_q(ci + 1)
                    o_b = ob[ci % 2]
                    steps = []
                    nkt = nkeys // 128
                    for i in range(nkt):
                        kk = key0 + i * 128
                        steps.append(dict(kT=k_[:, kk:kk + 128], q=q_[:, 0:nq], nk=128, nq=nq, v=v_[:, kk // 128, :], o=o_b[:, 0:nq], otk=o_b,
                                          first=(i == 0), last=(i == nkt - 1), rtk=[k_, q_], vtk=v_))

                    def post(o_b=o_b, ci=ci, tl=tl, base=base, t0=t0, nq=nq):
                        self.finalize_aug(o_b, nq, rb[ci % 2], ot[ci % 3], self.OT[tl, base:base + 64, t0:t0 + nq])
                    steps[0]["pre"] = pre
                    steps[-1]["post"] = post
                    allsteps.extend(steps)
                    ci += 1
                self.attn_steps(allsteps, scale, 512, sb, pts)
            kb.end_phase()

    def phase_C(self, l):
        nc, kb, I = self.nc, self.kb, self.I
        m = l % 4
        N = 256
        N1 = 512
        nchunk = NTOK // N
        B = self.bank
        with ExitStack() as es:
            W1 = kb.tile(es, "W1", [128, 8, 4096], BF16)
            W2a = kb.tile(es, "W2a", [128, 16, 1024], BF16)
            with ExitStack() as es1:
                Wo = kb.tile(es1, "Wo", [128, 8, 1024], BF16)
                self.load_w(Wo, I["l%d_w_o" % m], 8, parts=2)
                self.load_w(W1, I["w_mlp1"][l], 8, parts=8)
                self.load_w(W2a, I["w_mlp2"][l][0:2048, :], 16, parts=4)
                NB1 = 3
                xT = [kb.tile(es1, "xT%d" % i, [128, 8, N1], F32) for i in range(NB1)]
                xS = [kb.subs(x_, 8) for x_ in xT]
                oT = [kb.tile(es1, "oT%d" % i, [128, 8, N1], BF16) for i in range(2)]
                sq = [kb.tile(es1, "sq%d" % i, [128, 512], F32) for i in range(2)]
                mean = kb.tile(es1, "mean", [128, 512], F32)
                var = kb.tile(es1, "var", [128, 512], F32)
                nch1 = NTOK // N1

                def load1(c):
                    kb.dma(kb.sp, oT[c % 2][:], self.OT[:, :, c * N1:(c + 1) * N1].rearrange("k p n -> p k n"), reads=[self.OT], writes=[oT[c % 2]])
                    kb.dma(kb.sp, xT[c % NB1][:], self.XA[:, :, c * N1:(c + 1) * N1].rearrange("k p n -> p k n"), reads=[self.XA], writes=[xT[c % NB1]])
                    KB.loaded(xT[c % NB1], xS[c % NB1])

                def sA(c):
                    cond = 0 if c * N1 < NPR else 1
                    x = xT[c % NB1]; o = oT[c % 2]; xs = xS[c % NB1]
                    for n in range(8):
                        ps = B[n % 4]
                        for k in range(8):
                            kb.op(kb.pe, lambda ps=ps, k=k, n=n: nc.tensor.matmul(ps[:, 0:N1], Wo[:, k, n * 128:(n + 1) * 128], o[:, k, :],
                                                                                 start=(k == 0), stop=(k == 7)), reads=[Wo, o], writes=[ps])
                        kb.op(kb.dve, lambda ps=ps, n=n: nc.vector.scalar_tensor_tensor(out=x[:, n, :], in0=ps[:, 0:N1], scalar=self.mvc(l, 2, n, cond), in1=x[:, n, :],
                                                                                       op0=ALU.mult, op1=ALU.add), reads=[ps, self.mv, xs[n]], writes=[xs[n]])

                def sB(c):
                    self.ln_epilogue((sq, mean, var, B[4 + (c % 2) * 2], B[5 + (c % 2) * 2]), xT[c % NB1], N1, l, 0, self.XB, c * N1, sub=xS[c % NB1])

                load1(0)
                load1(1)
                sA(0)
                for c in range(nch1):
                    if c + 2 < nch1:
                        load1(c + 2)
                    if c + 1 < nch1:
                        sA(c + 1)
                    sB(c)
                kb.end_phase()

            W2b = kb.tile(es, "W2b", [128, 16, 1024], BF16)
            self.load_w(W2b, I["w_mlp2"][l][2048:4096, :], 16, parts=4)
            xT = [kb.tile(es, "xT%d" % i, [128, 8, N], F32) for i in range(2)]
            xS = [kb.subs(x_, 8) for x_ in xT]
            hT = [kb.tile(es, "hT%d" % i, [128, 8, N], BF16) for i in range(2)]
            uT = kb.tile(es, "uT", [128, 32, N], BF16)
            rl = [kb.tile(es, "rl%d" % i, [128, 2 * N], F32) for i in range(2)]
            sq = [kb.tile(es, "sq%d" % i, [128, 512], F32) for i in range(2)]
            mean = kb.tile(es, "mean", [128, 512], F32)
            var = kb.tile(es, "var", [128, 512], F32)

            def load(c):
                x = xT[c % 2]
                kb.dma(kb.sp, x[:], self.XB[:, :, c * N:(c + 1) * N].rearrange("k p n -> p k n"), reads=[self.XB], writes=[x])
                KB.loaded(x, xS[c % 2])

            def s1(c):
                cond = 0 if c * N < NPR else 1
                x = xT[c % 2]; h = hT[c % 2]; xs = xS[c % 2]
                for k in range(8):
                    kb.op(kb.act, lambda k=k: nc.scalar.activation(out=h[:, k, :], in_=x[:, k, :], func=AF.Identity,
                                                                   bias=self.mvc(l, 3, k, cond), scale=self.mvc(l, 4, k, cond)),
                          reads=[xs[k], self.mv], writes=[h])
                for fp in range(16):
                    ps = B[fp % 4]
                    for hf in range(2):
                        f = fp * 2 + hf
                        for k in range(8):
                            kb.op(kb.pe, lambda ps=ps, k=k, f=f, hf=hf: nc.tensor.matmul(ps[:, hf * N:(hf + 1) * N], W1[:, k, f * 128:(f + 1) * 128], h[:, k, :],
                                                                                        start=(k == 0), stop=(k == 7)), reads=[W1, h], writes=[ps])
                    r = rl[fp % 2]
                    kb.op(kb.act, lambda ps=ps, r=r: nc.scalar.activation(out=r[:], in_=ps[:, :], func=AF.Relu), reads=[ps], writes=[r])
                    kb.op(kb.pool, lambda r=r, fp=fp: nc.gpsimd.tensor_tensor(out=uT[:, 2 * fp:2 * fp + 2, :], in0=r[:].rearrange("p (a n) -> p a n", a=2),
                                                                            in1=r[:].rearrange("p (a n) -> p a n", a=2), op=ALU.mult),
                          reads=[r], writes=[uT])

            def s2(c):
                cond = 0 if c * N < NPR else 1
                x = xT[c % 2]; xs = xS[c % 2]
                for n in range(8):
                    ps = B[4 + n % 2]
                    for f in range(32):
                        W2x = W2a if f < 16 else W2b
                        kb.op(kb.pe, lambda ps=ps, f=f, n=n, W2x=W2x: nc.tensor.matmul(ps[:, 0:N], W2x[:, f % 16, n * 128:(n + 1) * 128], uT[:, f, :],
                                                                                      start=(f == 0), stop=(f == 31)), reads=[W2x, uT], writes=[ps])
                    kb.op(kb.dve, lambda ps=ps, n=n: nc.vector.scalar_tensor_tensor(out=x[:, n, :], in0=ps[:, 0:N], scalar=self.mvc(l, 5, n, cond), in1=x[:, n, :],
                                                                                   op0=ALU.mult, op1=ALU.add), reads=[ps, self.mv, xs[n]], writes=[xs[n]])

            def s3(c):
                self.ln_epilogue((sq, mean, var, B[6], B[7]), xT[c % 2], N, l, 1, self.XA, c * N, sub=xS[c % 2])

            load(0)
            s1(0)
            for c in range(nchunk):
                if c + 1 < nchunk:
                    load(c + 1)
                s2(c)
                if c + 1 < nchunk:
                    s1(c + 1)
                s3(c)
            kb.end_phase()

    def phase_out(self):
        nc, kb = self.nc, self.kb
        with ExitStack() as es:
            xT = [kb.tile(es, "xT%d" % i, [128, 8, 512], F32) for i in range(2)]
            yo = [kb.tile(es, "yo%d" % i, [128, 4, 1024], F32) for i in range(2)]
            for c in range(NTOK // 512):
                x = xT[c % 2]; y = yo[c % 2]
                t0 = c * 512
                kb.dma(kb.sp, x[:], self.XA[:, :, t0:t0 + 512].rearrange("k p n -> p k n"), reads=[self.XA], writes=[x])
                for j in range(4):
                    for kh in range(2):
                        bk = self.bank[(j * 2 + kh) % 4]
                        for kk in range(4):
                            k = kh * 4 + kk
                            kb.op(kb.pe, lambda bk=bk, j=j, k=k, kk=kk: nc.tensor.transpose(
                                bk[:, kk * 128:(kk + 1) * 128], x[:, k, j * 128:(j + 1) * 128], self.ident[:]),
                                reads=[x, self.ident], writes=[bk])
                        if kh == 0:
                            kb.op(kb.dve, lambda bk=bk, j=j, kh=kh: nc.vector.tensor_copy(out=y[:, j, kh * 512:(kh + 1) * 512], in_=bk[:, :]),
                                  reads=[bk], writes=[y])
                        else:
                            kb.op(kb.act, lambda bk=bk, j=j, kh=kh: nc.scalar.copy(out=y[:, j, kh * 512:(kh + 1) * 512], in_=bk[:, :]),
                                  reads=[bk], writes=[y])
                dst = self.O["yp"] if c < 2 else self.O["ys"]
                r0 = c * 512 if c < 2 else (c - 2) * 512
                kb.dma(kb.sp, dst[r0:r0 + 512, :].rearrange("(j p) f -> p j f", p=128), y[:], reads=[y], writes=[dst])
            kb.end_phase()


_PROG = {}


def _colT(v, ntile):
    return np.ascontiguousarray(np.asarray(v, np.float32).reshape(ntile, 128).T)


def make_in_maps(inp, ncores=8, nl=NLAYERS, names=None):
    f = lambda a: np.ascontiguousarray(np.asarray(a, dtype=np.float32))
    shared = {}
    nlw = max(nl, 1)
    shared["w_mod"] = f(inp["w_mod"][:nlw])
    bm = f(inp["b_mod"]).reshape(4, 6, 8, 128)
    shared["bmodT"] = np.ascontiguousarray(bm.transpose(0, 3, 1, 2).reshape(4, 128, 48))
    g = f(inp["ln_g"]).reshape(4, 2, 8, 128)
    b = f(inp["ln_b"]).reshape(4, 2, 8, 128)
    shared["lngT"] = np.ascontiguousarray(g.transpose(3, 0, 1, 2).reshape(128, 64))
    shared["lnbT"] = np.ascontiguousarray(b.transpose(3, 0, 1, 2).reshape(128, 64))
    shared["w_mlp1"] = f(inp["w_mlp1"][:nlw]); shared["w_mlp2"] = f(inp["w_mlp2"][:nlw])
    shared["l0_w_qkv"] = f(inp["l0_w_qkv"])
    shared["l0_lam_bc"] = np.ascontiguousarray(np.broadcast_to(f(inp["l0_lam"]).reshape(1, 256), (128, 256)))
    shared["l0_subln"] = f(inp["l0_subln"]).reshape(128, 1)
    shared["l0_w_o"] = f(inp["l0_w_o"])
    shared["l1_w_qkv"] = f(inp["l1_w_qkv"])
    E, M = _na_tables(f(inp["l1_rpb"]))
    shared["na_E"] = E; shared["na_M"] = M
    shared["l1_w_o"] = f(inp["l1_w_o"])
    shared["l2_w_a"] = f(inp["l2_w_a"])
    shared["l2_qnormT"] = _colT(inp["l2_q_norm"], 4)
    shared["l2_kvnormT"] = _colT(inp["l2_kv_norm"], 2)
    shared["l2_kvnorm_bc"] = np.ascontiguousarray(np.broadcast_to(f(inp["l2_kv_norm"]).reshape(1, 256), (128, 256)))
    shared["l2_w_uq"] = f(inp["l2_w_uq"]); shared["l2_w_ukv"] = f(inp["l2_w_ukv"]); shared["l2_w_o"] = f(inp["l2_w_o"])
    shared["l3_w_qkv"] = f(inp["l3_w_qkv"])
    shared["l3_sink_bc"] = np.ascontiguousarray(np.broadcast_to(f(inp["l3_sink"]).reshape(1, 16), (128, 16)))
    shared["l3_w_o"] = f(inp["l3_w_o"])
    shared["ident"] = np.eye(128, dtype=np.float32)
    shared["rt64"] = _rot_lhsT(64, [0, 64], 128)
    shared["rt32"] = _rot_lhsT(32, [0, 64], 128)
    c64, s64 = _rope_tables(64, [0, 64])
    c32, s32 = _rope_tables(32, [0, 64])
    shared["cos64"] = c64; shared["sin64"] = s64; shared["cos32"] = c32; shared["sin32"] = s32
    kk = np.arange(128)[:, None]; qq = np.arange(128)[None, :]
    shared["tri"] = np.stack([(qq <= kk), (kk <= qq)]).astype(np.float32)
    maps = []
    cctx = f(inp["c_ctx"])
    if names is not None:
        shared = {k: v for k, v in shared.items() if k in names}
    for i in range(ncores):
        d = dict(shared)
        d["xp"] = f(inp["x_prompt"][4 * i:4 * i + 4]).reshape(NPR, D)
        d["xs"] = f(inp["x_sample"][i]).reshape(NLAT, D)
        d["ck0"] = f(inp["cache_l0_k"][i]).reshape(PAST, 1024); d["cv0"] = f(inp["cache_l0_v"][i]).reshape(PAST, 1024)
        d["ck1"] = f(inp["cache_l1_k"][i]).reshape(PAST, 1024); d["cv1"] = f(inp["cache_l1_v"][i]).reshape(PAST, 1024)
        d["cckv"] = f(inp["cache_l2_ckv"][i]).reshape(PAST, 256); d["ckpe"] = f(inp["cache_l2_kpe"][i]).reshape(PAST, 32)
        d["ck3"] = f(inp["cache_l3_k"][i]).reshape(PAST, 256); d["cv3"] = f(inp["cache_l3_v"][i]).reshape(PAST, 256)
        d["condT"] = np.ascontiguousarray(np.stack([_colT(cctx, 8), _colT(f(inp["c"][i]), 8)], axis=-1))
        if names is not None:
            d = {k: v for k, v in d.items() if k in names}
        maps.append(d)
    return maps


def assemble(results, ncores=8):
    cat = lambda k: np.concatenate([np.asarray(r[k], np.float32) for r in results], axis=0)
    yp = cat("yp").reshape(4 * ncores, 256, D)
    ys = cat("ys").reshape(ncores, NLAT, D)
    k0 = cat("k0").reshape(4 * ncores, 256, 8, 128); v0 = cat("v0").reshape(4 * ncores, 256, 8, 128)
    k1 = cat("k1").reshape(4 * ncores, 256, 16, 64); v1 = cat("v1").reshape(4 * ncores, 256, 16, 64)
    ckv2 = cat("ckv2").reshape(4 * ncores, 256, 256); kpe2 = cat("kpe2").reshape(4 * ncores, 256, 32)
    k3 = cat("k3").reshape(4 * ncores, 256, 4, 64); v3 = cat("v3").reshape(4 * ncores, 256, 4, 64)
    return (yp, ys, k0, v0, k1, v1, ckv2, kpe2, k3, v3)


def kernel(**inputs):
    if "p" not in _PROG:
        _PROG["p"] = Prog(NLAYERS)
    prog = _PROG["p"]
    maps = make_in_maps(inputs, 8, NLAYERS, set(prog.I.keys()))
    res = run_bass_kernel_spmd(prog.nc, maps, core_ids=list(range(8)))
    return assemble(res.results, 8)
```

```python
import math
from contextlib import ExitStack
import numpy as np
import concourse.bass as bass
import concourse.mybir as mybir
from concourse.bass_utils import run_bass_kernel_spmd

F32 = mybir.dt.float32
BF16 = mybir.dt.bfloat16
AF = mybir.ActivationFunctionType
ALU = mybir.AluOpType
AX = mybir.AxisListType

NLAYERS = 4
D = 1024
NPR = 1024
NLAT = 4096
NTOK = NPR + NLAT
PAST = 512
NKEY = NPR + PAST + NLAT
LKEY0 = NPR
LNEW0 = NPR + PAST
ALPHA = 8.0 ** 0.25
LN_EPS = 1e-5
EPS_LN = LN_EPS / (ALPHA * ALPHA)
LAM_INIT = 0.8 - 0.6 * math.exp(-0.3 * 0)
GRID_W = 64
SEM_ROLL = 30000


class Sem:
    __slots__ = ("h", "total", "dma", "id")

    def __init__(self, h, dma, i):
        self.h = h
        self.total = 0
        self.dma = dma
        self.id = i


class Tk:
    __slots__ = ("name", "w", "r", "dsem", "t", "psum")

    def __init__(self, name, t=None):
        self.psum = False
        self.name = name
        self.w = None
        self.r = {}
        self.dsem = None
        self.t = t

    def __getitem__(self, idx):
        return self.t[idx]


class Eng:
    def __init__(self, kb, name, h, compute=True):
        self.kb = kb
        self.name = name
        self.h = h
        self.sem = kb.newsem(False)
        self.waited = {}

    def wait(self, sem, val):
        if sem.dma:
            val = sem.total
        if val <= 0:
            return
        if self.waited.get(sem.id, 0) >= val:
            return
        if sem is self.sem and self.name == "pe":
            return
        self.h.wait_ge(sem.h, val)
        self.waited[sem.id] = val


class KB:
    def __init__(self, nc):
        self.nc = nc
        self.es = ExitStack()
        self.nsem = 0
        self.sems = []
        self.pe = Eng(self, "pe", nc.tensor)
        self.act = Eng(self, "act", nc.scalar)
        self.dve = Eng(self, "dve", nc.vector)
        self.pool = Eng(self, "pool", nc.gpsimd)
        self.sp = Eng(self, "sp", nc.sync)
        self.engs = [self.pe, self.act, self.dve, self.pool, self.sp]
        self.nins = 0
        self.ntile = 0
        self.free_dsems = []
        self.phase_tiles = []

    def newsem(self, dma):
        if dma and self.free_dsems:
            return self.free_dsems.pop()
        h = self.es.enter_context(self.nc.semaphore("s%d" % self.nsem))
        s = Sem(h, dma, self.nsem)
        self.nsem += 1
        self.sems.append(s)
        return s

    def tile(self, es, name, shape, dtype):
        self.ntile += 1
        t = es.enter_context(self.nc.sbuf_tensor("%s_%d" % (name, self.ntile), shape, dtype))
        tk = Tk(name, t)
        self.phase_tiles.append(tk)
        return tk

    def subs(self, tk, n):
        out = [Tk("%s.%d" % (tk.name, i), tk.t) for i in range(n)]
        self.phase_tiles.extend(out)
        return out

    @staticmethod
    def loaded(tk, subs):
        for s_ in subs:
            s_.w = tk.w
            s_.r = {}

    def psum(self, es, name, shape, dtype=F32):
        t = es.enter_context(self.nc.psum_tensor(name, shape, dtype))
        tk = Tk(name, t)
        tk.psum = True
        return tk

    def dram(self, name, shape, dtype, kind="Internal"):
        t = self.nc.dram_tensor(name, shape, dtype, kind=kind)
        return Tk(name, t.ap())

    def _deps(self, eng, reads, writes, nowaw=False):
        for t in reads:
            if t.w is not None:
                eng.wait(*t.w)
            if t.psum:
                for s, v in t.r.values():
                    if s is not eng.sem:
                        eng.wait(s, v)
        for t in writes:
            if t.w is not None and not nowaw:
                eng.wait(*t.w)
            for s, v in t.r.values():
                eng.wait(s, v)

    def op(self, eng, ins, reads=(), writes=()):
        self._deps(eng, reads, writes)
        i = ins()
        if eng.sem.total >= SEM_ROLL:
            eng.sem = self.newsem(False)
        eng.sem.total += 1
        i.then_inc(eng.sem.h, 1)
        ev = (eng.sem, eng.sem.total)
        sid = eng.sem.id
        for t in reads:
            t.r[sid] = ev
        for t in writes:
            t.w = ev
            t.r = {}
        self.nins += 1
        return i

    def dma(self, eng, out_ap, in_ap, reads=(), writes=(), nowaw=True, **kw):
        dst = writes[0]
        self._deps(eng, reads, writes, nowaw=nowaw)
        if dst.dsem is None:
            dst.dsem = self.newsem(True)
        if dst.dsem.total >= SEM_ROLL * 16:
            eng.wait(dst.dsem, dst.dsem.total)
            dst.dsem = self.newsem(True)
        i = eng.h.dma_start(out=out_ap, in_=in_ap, **kw)
        dst.dsem.total += 16
        i.then_inc(dst.dsem.h, 16)
        ev = (dst.dsem, dst.dsem.total)
        for t in reads:
            t.r[dst.dsem.id] = ev
        dst.w = ev
        self.nins += 1
        return i

    def barrier(self):
        for e in self.engs:
            for s in self.sems:
                if s.total > 0:
                    e.wait(s, s.total)

    def end_phase(self):
        self.barrier()
        for tk in self.phase_tiles:
            if tk.dsem is not None:
                self.free_dsems.append(tk.dsem)
                tk.dsem = None
                tk.w = None
                tk.r = {}
        self.phase_tiles = []


def _rope_tables(R, rows):
    n = R // 4
    t = np.arange(NLAT)
    inv = (10000.0 ** (-np.arange(n, dtype=np.float32) / n)).astype(np.float32)
    cos = np.zeros((128, NLAT), np.float32)
    sin = np.zeros((128, NLAT), np.float32)
    half = R // 2
    for base in rows:
        for d in range(R):
            pos = (t // GRID_W) if d < half else (t % GRID_W)
            f = inv[(d % half) % n]
            ang = pos.astype(np.float32) * f
            cos[base + d] = np.cos(ang).astype(np.float32)
            sin[base + d] = np.sin(ang).astype(np.float32)
    return cos, sin


def _rot_lhsT(R, bases, size):
    n = R // 4
    half = R // 2
    Rm = np.zeros((size, size), np.float32)
    for base in bases:
        for d in range(R):
            if (d % half) < n:
                Rm[base + d, base + d + n] = -1.0
            else:
                Rm[base + d, base + d - n] = 1.0
    return np.ascontiguousarray(Rm.T)


def _na_tables(rpb):
    H = rpb.shape[0]
    a = np.arange(2)[:, None, None, None]
    j = np.arange(64)[None, :, None, None]
    w = np.arange(16)[None, None, :, None]
    c = np.arange(64)[None, None, None, :]
    dlt = a - w + 7 + 0 * j + 0 * c
    ri = np.clip(dlt + 7, 0, 14)
    ci = np.clip(j - c + 15 + 0 * a + 0 * w, 0, 30)
    E = rpb[:, ri, ci].reshape(H, 128, 16 * 64).astype(np.float32)
    c0 = np.clip(c - 8, 0, 64 - 16)
    colok = (j >= c0) & (j < c0 + 16)
    m_int = colok & (dlt >= -4) & (dlt <= 3)
    m_bnd = colok & (np.abs(dlt) <= 7)
    M = np.stack([m_int, m_bnd]).reshape(2, 128, 16 * 64).astype(np.float32)
    return E, M


class Prog:
    def __init__(self, nlayers=NLAYERS, ph="MABCD", dbg=(), layers=None):
        self.layers = layers
        self.nl = nlayers
        self.ph = ph
        self.dbg = dbg
        nc = bass.Bass("TRN2", target_bir_lowering=False)
        self.nc = nc
        self.kb = KB(nc)
        self.build()

    def din(self, name, shape, dtype=F32):
        return self.kb.dram(name, list(shape), dtype, kind="ExternalInput")

    def dout(self, name, shape):
        t = self.kb.dram(name, list(shape), F32, kind="ExternalOutput")
        self.outs.append(t)
        return t

    def build(self):
        nc, kb = self.nc, self.kb
        self.outs = []
        nlw = max(self.nl, 1)
        shapes = {
            "xp": [NPR, D], "xs": [NLAT, D],
            "ck0": [PAST, 1024], "cv0": [PAST, 1024], "ck1": [PAST, 1024], "cv1": [PAST, 1024],
            "cckv": [PAST, 256], "ckpe": [PAST, 32], "ck3": [PAST, 256], "cv3": [PAST, 256],
            "condT": [128, 8, 2],
            "w_mod": [nlw, 1024, 6144], "bmodT": [4, 128, 48], "lngT": [128, 64], "lnbT": [128, 64],
            "w_mlp1": [nlw, 1024, 4096], "w_mlp2": [nlw, 4096, 1024],
            "l0_w_qkv": [1024, 3072], "l0_lam_bc": [128, 256], "l0_subln": [128, 1], "l0_w_o": [1024, 1024],
            "l1_w_qkv": [1024, 3072], "na_E": [16, 128, 1024], "na_M": [2, 128, 1024], "l1_w_o": [1024, 1024],
            "l2_w_a": [1024, 800], "l2_qnormT": [128, 4], "l2_kvnormT": [128, 2], "l2_kvnorm_bc": [128, 256],
            "l2_w_uq": [512, 1536], "l2_w_ukv": [256, 2048], "l2_w_o": [1024, 1024],
            "l3_w_qkv": [1024, 1536], "l3_sink_bc": [128, 16], "l3_w_o": [1024, 1024],
            "ident": [128, 128], "rt64": [128, 128], "rt32": [128, 128],
            "cos64": [128, NLAT], "sin64": [128, NLAT], "cos32": [128, NLAT], "sin32": [128, NLAT],
            "tri": [2, 128, 128],
        }
        prog = self

        class LazyIn(dict):
            def __missing__(self, name):
                t = prog.din(name, shapes[name])
                self[name] = t
                return t

        I = self.I = LazyIn()
        O = self.O = {}
        O["yp"] = self.dout("yp", [NPR, D])
        O["ys"] = self.dout("ys", [NLAT, D])
        O["k0"] = self.dout("k0", [NPR, 1024]); O["v0"] = self.dout("v0", [NPR, 1024])
        O["k1"] = self.dout("k1", [NPR, 1024]); O["v1"] = self.dout("v1", [NPR, 1024])
        O["ckv2"] = self.dout("ckv2", [NPR, 256]); O["kpe2"] = self.dout("kpe2", [NPR, 32])
        O["k3"] = self.dout("k3", [NPR, 256]); O["v3"] = self.dout("v3", [NPR, 256])
        kd = lambda n: ("ExternalOutput" if n in self.dbg else "Internal")
        self.XA = kb.dram("XA", [8, 128, NTOK], F32, kind=kd("XA"))
        self.XB = kb.dram("XB", [8, 128, NTOK], F32, kind=kd("XB"))
        self.QT = kb.dram("QT", [16, 128, NTOK], BF16, kind=kd("QT"))
        self.KT = kb.dram("KT", [8, 128, NKEY], BF16, kind=kd("KT"))
        self.VS = kb.dram("VS", [NKEY, 1024], BF16, kind=kd("VS"))
        self.OT = kb.dram("OT", [8, 128, NTOK], BF16, kind=kd("OT"))

        with ExitStack() as es:
            self.ges = es
            self.bank = [kb.psum(es, "bank%d" % i, [128, 512], F32) for i in range(8)]
            self.ones_f = kb.tile(es, "ones_f", [128, 128], F32)
            self.ones_b = kb.tile(es, "ones_b", [128, 128], BF16)
            self.ident = kb.tile(es, "ident", [128, 128], F32)
            self.epsln = kb.tile(es, "epsln", [128, 1], F32)
            self.eps5 = kb.tile(es, "eps5", [128, 1], F32)
            kb.op(kb.dve, lambda: nc.vector.memset(self.ones_f[:], 1.0), writes=[self.ones_f])
            kb.op(kb.dve, lambda: nc.vector.memset(self.ones_b[:], 1.0), writes=[self.ones_b])
            kb.op(kb.dve, lambda: nc.vector.memset(self.epsln[:], EPS_LN), writes=[self.epsln])
            kb.op(kb.dve, lambda: nc.vector.memset(self.eps5[:], LN_EPS), writes=[self.eps5])
            kb.dma(kb.sp, self.ident[:], I["ident"][:, :], writes=[self.ident])
            self.mv = kb.tile(es, "mv", [128, 4 * 6 * 8 * 2], F32)
            self.lng = kb.tile(es, "lng", [128, 64], F32)
            self.lnb = kb.tile(es, "lnb", [128, 64], F32)
            kb.dma(kb.sp, self.lng[:], I["lngT"][:, :], writes=[self.lng])
            kb.dma(kb.sp, self.lnb[:], I["lnbT"][:, :], writes=[self.lnb])

            self.phase_in()
            if "M" in self.ph:
                self.phase_mod()
            for l in range(self.nl):
                if self.layers is not None and l not in self.layers:
                    continue
                if "A" in self.ph:
                    self.phase_A(l)
                if "B" in self.ph:
                    self.phase_B(l)
                if "C" in self.ph or "D" in self.ph:
                    self.phase_C(l)
            self.phase_out()
            kb.end_phase()
        kb.es.close()

    def mvc(self, l, j, k, c):
        i = ((l * 6 + j) * 8 + k) * 2 + c
        return self.mv[:, i:i + 1]

    def lnc(self, t, l, s, k):
        i = (l * 2 + s) * 8 + k
        return t[:, i:i + 1]

    def phase_in(self):
        nc, kb = self.nc, self.kb
        with ExitStack() as es:
            xin = [kb.tile(es, "xin%d" % i, [128, 4, 1024], F32) for i in range(2)]
            xo = [kb.tile(es, "xo%d" % i, [128, 8, 512], F32) for i in range(2)]
            for c in range(NTOK // 512):
                src = self.I["xp"] if c < 2 else self.I["xs"]
                r0 = c * 512 if c < 2 else (c - 2) * 512
                xi = xin[c % 2]
                kb.dma(kb.sp, xi[:], src[r0:r0 + 512, :].rearrange("(j p) f -> p j f", p=128), writes=[xi])
                xt = xo[c % 2]
                for k in range(8):
                    bk = self.bank[k % 4]
                    for j in range(4):
                        kb.op(kb.pe, lambda bk=bk, j=j, k=k, xi=xi: nc.tensor.transpose(
                            bk[:, j * 128:(j + 1) * 128], xi[:, j, k * 128:(k + 1) * 128], self.ident[:]),
                            reads=[xi, self.ident], writes=[bk])
                    if k % 2 == 0:
                        kb.op(kb.dve, lambda bk=bk, k=k, xt=xt: nc.vector.tensor_copy(out=xt[:, k, :], in_=bk[:, :]),
                              reads=[bk], writes=[xt])
                    else:
                        kb.op(kb.act, lambda bk=bk, k=k, xt=xt: nc.scalar.copy(out=xt[:, k, :], in_=bk[:, :]),
                              reads=[bk], writes=[xt])
                kb.dma(kb.sp, self.XA[:, :, c * 512:(c + 1) * 512].rearrange("k p n -> p k n"), xt[:],
                       reads=[xt], writes=[self.XA])
            kb.end_phase()

    def phase_mod(self):
        nc, kb = self.nc, self.kb
        with ExitStack() as es:
            cond = kb.tile(es, "cond", [128, 8, 2], F32)
            sc = kb.tile(es, "silu", [128, 8, 2], BF16)
            kb.dma(kb.sp, cond[:], self.I["condT"][:, :, :], writes=[cond])
            kb.op(kb.act, lambda: nc.scalar.activation(out=sc[:], in_=cond[:], func=AF.Silu), reads=[cond], writes=[sc])
            wm = [kb.tile(es, "wm%d" % i, [128, 8, 1024], BF16) for i in range(2)]
            bm = kb.tile(es, "bm", [128, 4, 48], F32)
            kb.dma(kb.sp, bm[:], self.I["bmodT"][:, :, :].rearrange("l p n -> p l n"), writes=[bm])
            it = 0
            for l in range(self.nl):
                bk = self.bank[l % 2]
                for j in range(6):
                    w = wm[it % 2]
                    it += 1
                    kb.dma(kb.pool, w[:], self.I["w_mod"][l, :, j * 1024:(j + 1) * 1024].rearrange("(k p) n -> p k n", p=128),
                           writes=[w])
                    for nt in range(8):
                        col = (j * 8 + nt) * 2
                        for k in range(8):
                            kb.op(kb.pe, lambda bk=bk, col=col, w=w, k=k, nt=nt: nc.tensor.matmul(
                                bk[:, col:col + 2], w[:, k, nt * 128:(nt + 1) * 128], sc[:, k, :],
                                start=(k == 0), stop=(k == 7)), reads=[w, sc], writes=[bk])
                base = l * 96
                for c in range(2):
                    kb.op(kb.dve, lambda bk=bk, c=c, base=base, l=l: nc.vector.tensor_tensor(
                        out=self.mv[:, base + c:base + 96:2], in0=bk[:, c:96:2], in1=bm[:, l, :], op=ALU.add),
                        reads=[bk, bm], writes=[self.mv])
                for j in (1, 4):
                    a = base + j * 16
                    kb.op(kb.dve, lambda a=a: nc.vector.tensor_scalar_add(out=self.mv[:, a:a + 16], in0=self.mv[:, a:a + 16], scalar1=1.0),
                          reads=[self.mv], writes=[self.mv])
                for j in (2, 5):
                    a = base + j * 16
                    kb.op(kb.dve, lambda a=a: nc.vector.tensor_scalar_mul(out=self.mv[:, a:a + 16], in0=self.mv[:, a:a + 16], scalar1=1.0 / ALPHA),
                          reads=[self.mv], writes=[self.mv])
            kb.end_phase()

    def ln_epilogue(self, es_tiles, tT, N, l, s, dst, c0, sub=None):
        nc, kb = self.nc, self.kb
        sq, mean, var, s1b, s2b = es_tiles
        if sub is None:
            sub = [tT] * 8
        for k in range(8):
            q = sq[k % 2]
            kb.op(kb.act, lambda q=q, k=k: nc.scalar.activation(out=q[:, 0:N], in_=tT[:, k, :], func=AF.Square),
                  reads=[sub[k]], writes=[q])
            kb.op(kb.pe, lambda k=k: nc.tensor.matmul(s1b[:, 0:N], self.ones_f[:], tT[:, k, :], start=(k == 0), stop=(k == 7)),
                  reads=[sub[k], self.ones_f], writes=[s1b])
            kb.op(kb.pe, lambda q=q, k=k: nc.tensor.matmul(s2b[:, 0:N], self.ones_f[:], q[:, 0:N], start=(k == 0), stop=(k == 7)),
                  reads=[q, self.ones_f], writes=[s2b])
        kb.op(kb.act, lambda: nc.scalar.mul(out=mean[:, 0:N], in_=s1b[:, 0:N], mul=1.0 / D), reads=[s1b], writes=[mean])
        kb.op(kb.dve, lambda: nc.vector.scalar_tensor_tensor(out=var[:, 0:N], in0=mean[:, 0:N], scalar=-1.0, in1=mean[:, 0:N],
                                                             op0=ALU.mult, op1=ALU.mult), reads=[mean], writes=[var])
        kb.op(kb.dve, lambda: nc.vector.scalar_tensor_tensor(out=var[:, 0:N], in0=s2b[:, 0:N], scalar=1.0 / D, in1=var[:, 0:N],
                                                             op0=ALU.mult, op1=ALU.add), reads=[s2b, var], writes=[var])
        kb.op(kb.act, lambda: nc.scalar.activation(out=var[:, 0:N], in_=var[:, 0:N], func=AF.Ln, bias=self.epsln[:, 0:1], scale=1.0),
              reads=[var, self.epsln], writes=[var])
        kb.op(kb.act, lambda: nc.scalar.activation(out=var[:, 0:N], in_=var[:, 0:N], func=AF.Exp, scale=-0.5), reads=[var], writes=[var])
        for k in range(8):
            kb.op(kb.dve, lambda k=k: nc.vector.tensor_tensor(out=tT[:, k, :], in0=tT[:, k, :], in1=mean[:, 0:N], op=ALU.subtract),
                  reads=[sub[k], mean], writes=[sub[k]])
            if k % 4 == 3:
                kb.op(kb.dve, lambda k=k: nc.vector.tensor_tensor(out=tT[:, k, :], in0=tT[:, k, :], in1=var[:, 0:N], op=ALU.mult),
                      reads=[sub[k], var], writes=[sub[k]])
            else:
                kb.op(kb.pool, lambda k=k: nc.gpsimd.tensor_tensor(out=tT[:, k, :], in0=tT[:, k, :], in1=var[:, 0:N], op=ALU.mult),
                      reads=[sub[k], var], writes=[sub[k]])
            kb.op(kb.act, lambda k=k: nc.scalar.activation(out=tT[:, k, :], in_=tT[:, k, :], func=AF.Identity,
                                                           bias=self.lnc(self.lnb, l, s, k), scale=self.lnc(self.lng, l, s, k)),
                  reads=[sub[k], self.lng, self.lnb], writes=[sub[k]])
        rd = [tT] + ([] if sub[0] is tT else list(sub))
        kb.dma(kb.sp, dst[:, :, c0:c0 + N].rearrange("k p n -> p k n"), tT[:], reads=rd, writes=[dst])

    def load_w(self, dst, src_ap, nk, parts=1):
        kb = self.kb
        per = nk // parts if nk >= parts else nk
        k = 0
        while k < nk:
            k1 = min(nk, k + max(per, 1))
            kb.dma(kb.pool, dst[:, k:k1, :], src_ap[k * 128:k1 * 128, :].rearrange("(k p) n -> p k n", p=128), writes=[dst])
            k = k1

    def phase_A(self, l):
        nc, kb, I = self.nc, self.kb, self.I
        m = l % 4
        import os
        SK = os.environ.get("SKIP", "")
        TMB = int(os.environ.get("TMB", "6"))
        with ExitStack() as es:
            xT = [kb.tile(es, "xT%d" % i, [128, 8, 512], F32) for i in range(2)]
            hT = [kb.tile(es, "hT%d" % i, [128, 8, 512], BF16) for i in range(2)]
            stq = [kb.tile(es, "stq%d" % i, [128, 512], BF16) for i in range(4)]
            stv = [kb.tile(es, "stv%d" % i, [128, 1024], BF16) for i in range(2)]
            stf = [kb.tile(es, "stf%d" % i, [128, 1024], F32) for i in range(2)]
            self.cnt = 0
            if m in (0, 3):
                cos = kb.tile(es, "cos", [128, NLAT], F32); sin = kb.tile(es, "sin", [128, NLAT], F32)
                if "s" not in SK:
                    kb.dma(kb.sp, cos[:], I["cos64"][:, :], writes=[cos]); kb.dma(kb.sp, sin[:], I["sin64"][:, :], writes=[sin])
                rt = kb.tile(es, "rt", [128, 128], BF16)
                kb.dma(kb.pool, rt[:], I["rt64"][:, :], writes=[rt])
            elif m == 2:
                cos = kb.tile(es, "cos", [128, NLAT], F32); sin = kb.tile(es, "sin", [128, NLAT], F32)
                kb.dma(kb.sp, cos[:], I["cos32"][:, :], writes=[cos]); kb.dma(kb.sp, sin[:], I["sin32"][:, :], writes=[sin])
                rt = kb.tile(es, "rt", [128, 128], BF16)
                kb.dma(kb.pool, rt[:], I["rt32"][:, :], writes=[rt])
            else:
                cos = sin = rt = None
            rtmp = [kb.tile(es, "rtmp%d" % i, [128, 512], F32) for i in range(4)]
            qb = [kb.tile(es, "qb%d" % i, [128, 512], BF16) for i in range(2)]

            if m == 0:
                W = kb.tile(es, "Wqkv", [128, 8, 3072], BF16)
                if "w" not in SK:
                    self.load_w(W, I["l0_w_qkv"], 8, parts=4)
            elif m == 1:
                W = kb.tile(es, "Wqkv", [128, 8, 3072], BF16)
                self.load_w(W, I["l1_w_qkv"], 8, parts=4)
            elif m == 3:
                W = kb.tile(es, "Wqkv", [128, 8, 1536], BF16)
                self.load_w(W, I["l3_w_qkv"], 8, parts=2)
            else:
                W = kb.tile(es, "Wa", [128, 8, 800], BF16)
                self.load_w(W, I["l2_w_a"], 8, parts=2)
                Wuq = kb.tile(es, "Wuq", [128, 4, 1536], BF16)
                self.load_w(Wuq, I["l2_w_uq"], 4, parts=2)
                qn = kb.tile(es, "qn", [128, 4], F32); kvn = kb.tile(es, "kvn", [128, 2], F32)
                kvbc = kb.tile(es, "kvbc", [128, 256], F32)
                kb.dma(kb.sp, qn[:], I["l2_qnormT"][:, :], writes=[qn])
                kb.dma(kb.sp, kvn[:], I["l2_kvnormT"][:, :], writes=[kvn])
                kb.dma(kb.sp, kvbc[:], I["l2_kvnorm_bc"][:, :], writes=[kvbc])
                cq = kb.tile(es, "cq", [128, 6, 512], F32)
                cqn = kb.tile(es, "cqn", [128, 6, 512], BF16)
                sq2 = [kb.tile(es, "sq2_%d" % i, [128, 512], F32) for i in range(2)]
                rs = [kb.tile(es, "rs%d" % i, [128, 512], F32) for i in range(2)]
                ss1 = kb.tile(es, "ss1", [128, 1], F32)

            if "c" not in SK:
                self.cache_prep(l, es)

            pbank = self.bank
            bi = [0]

            def nextbank(lo, n):
                b = pbank[lo + bi[0] % n]
                bi[0] += 1
                return b

            def rope_store(ps, rows, lc, dst_ap, dstTk, R_rows=None):
                i = self.cnt
                self.cnt += 1
                st = stq[i % 4]
                r0, r1 = rows
                if lc is None:
                    if i % 2 == 0:
                        kb.op(kb.act, lambda: nc.scalar.copy(out=st[r0:r1, :], in_=ps[r0:r1, :]), reads=[ps], writes=[st])
                    else:
                        kb.op(kb.dve, lambda: nc.vector.tensor_copy(out=st[r0:r1, :], in_=ps[r0:r1, :]), reads=[ps], writes=[st])
                else:
                    rr0, rr1 = R_rows if R_rows is not None else rows
                    q_ = qb[i % 2]
                    kb.op(kb.act, lambda: nc.scalar.copy(out=q_[r0:r1, :], in_=ps[r0:r1, :]), reads=[ps], writes=[q_])
                    pr = pbank[4 + i % 2]
                    kb.op(kb.pe, lambda: nc.tensor.matmul(pr[r0:r1, :], rt[r0:r1, r0:r1], q_[r0:r1, :], start=True, stop=True),
                          reads=[rt, q_], writes=[pr])
                    t1 = rtmp[(2 * i) % 4]; t2 = rtmp[(2 * i + 1) % 4]
                    cs = slice(lc * 512, (lc + 1) * 512)
                    kb.op(kb.dve, lambda: nc.vector.tensor_tensor(out=t1[rr0:rr1, :], in0=ps[rr0:rr1, :], in1=cos[rr0:rr1, cs], op=ALU.mult),
                          reads=[ps, cos], writes=[t1])
                    kb.op(kb.dve, lambda: nc.vector.tensor_tensor(out=t2[rr0:rr1, :], in0=pr[rr0:rr1, :], in1=sin[rr0:rr1, cs], op=ALU.mult),
                          reads=[pr, sin], writes=[t2])
                    kb.op(kb.pool, lambda: nc.gpsimd.tensor_tensor(out=st[rr0:rr1, :], in0=t1[rr0:rr1, :], in1=t2[rr0:rr1, :], op=ALU.add),
                          reads=[t1, t2], writes=[st])
                    if rr0 > r0:
                        kb.op(kb.dve, lambda: nc.vector.tensor_copy(out=st[r0:rr0, :], in_=ps[r0:rr0, :]), reads=[ps], writes=[st])
                kb.dma(kb.sp, dst_ap, st[r0:r1, :], reads=[st], writes=[dstTk])

            for c in range(NTOK // 512):
                cond = 0 if c < 2 else 1
                lc = None if (c < 2 or "r" in SK) else c - 2
                t0 = c * 512
                key0 = t0 if c < 2 else LNEW0 + (c - 2) * 512
                x = xT[c % 2]; h = hT[c % 2]
                if c == 0:
                    kb.dma(kb.sp, x[:], self.XA[:, :, t0:t0 + 512].rearrange("k p n -> p k n"), reads=[self.XA], writes=[x])
                if c + 1 < NTOK // 512:
                    kb.dma(kb.sp, xT[(c + 1) % 2][:], self.XA[:, :, t0 + 512:t0 + 1024].rearrange("k p n -> p k n"), reads=[self.XA], writes=[xT[(c + 1) % 2]])
                for k in range(8):
                    kb.op(kb.act, lambda k=k: nc.scalar.activation(out=h[:, k, :], in_=x[:, k, :], func=AF.Identity,
                                                                   bias=self.mvc(l, 0, k, cond), scale=self.mvc(l, 1, k, cond)),
                          reads=[x, self.mv], writes=[h])

                def proj_fm(Wt, nk, col0, ncols, src, ps):
                    for k in range(nk):
                        kb.op(kb.pe, lambda k=k: nc.tensor.matmul(ps[0:ncols, :], Wt[:, k, col0:col0 + ncols], src[:, k, :],
                                                                  start=(k == 0), stop=(k == nk - 1)),
                              reads=[Wt, src], writes=[ps])

                def proj_tm(Wt, col0, ncols, j, ps):
                    for k in range(8):
                        kb.op(kb.pe, lambda k=k: nc.tensor.matmul(ps[:, 0:ncols], h[:, k, j * 128:(j + 1) * 128], Wt[:, k, col0:col0 + ncols],
                                                                  start=(k == 0), stop=(k == 7)),
                              reads=[Wt, h], writes=[ps])

                if "p" in SK:
                    continue
                if m in (0, 1, 3):
                    nq = 8
                    nkt = 8 if m != 3 else 2
                    kcol = 1024
                    vcol = 2048 if m != 3 else 1280
                    vw = 1024 if m != 3 else 256
                    use_rope = (m != 1)
                    for t in range(nq):
                        ps = nextbank(0, 4)
                        proj_fm(W, 8, t * 128, 128, h, ps)
                        rope_store(ps, (0, 128), lc if use_rope else None, self.QT[t, :, t0:t0 + 512], self.QT)
                    for t in range(nkt):
                        ps = nextbank(0, 4)
                        proj_fm(W, 8, kcol + t * 128, 128, h, ps)
                        rope_store(ps, (0, 128), lc if use_rope else None, self.KT[t, :, key0:key0 + 512], self.KT)
                    for j in range(4 if "t" not in SK else 0):
                        sv = stv[j % 2]
                        for hf in range(0, vw, 512):
                            n = min(512, vw - hf)
                            ps = pbank[TMB + (j + hf // 512) % 2]
                            proj_tm(W, vcol + hf, n, j, ps)
                            kb.op(kb.act, lambda ps=ps, hf=hf, n=n, sv=sv: nc.scalar.copy(out=sv[:, hf:hf + n], in_=ps[:, 0:n]),
                                  reads=[ps], writes=[sv])
                            if c < 2:
                                sf = stf[0]
                                kb.op(kb.dve, lambda ps=ps, hf=hf, n=n, sf=sf: nc.vector.tensor_copy(out=sf[:, hf:hf + n], in_=ps[:, 0:n]),
                                      reads=[ps], writes=[sf])
                        kb.dma(kb.sp, self.VS[key0 + j * 128:key0 + (j + 1) * 128, 0:vw], sv[:, 0:vw], reads=[sv], writes=[self.VS])
                        if c < 2:
                            vo = self.O["v%d" % m]
                            kb.dma(kb.sp, vo[t0 + j * 128:t0 + (j + 1) * 128, :], stf[0][:, 0:vw], reads=[stf[0]], writes=[vo])
                            sf = stf[1]
                            for hf in range(0, vw, 512):
                                n = min(512, vw - hf)
                                ps = pbank[TMB + (j + hf // 512) % 2]
                                proj_tm(W, kcol + hf, n, j, ps)
                                kb.op(kb.dve, lambda ps=ps, hf=hf, n=n, sf=sf: nc.vector.tensor_copy(out=sf[:, hf:hf + n], in_=ps[:, 0:n]),
                                      reads=[ps], writes=[sf])
                            ko = self.O["k%d" % m]
                            kb.dma(kb.sp, ko[t0 + j * 128:t0 + (j + 1) * 128, :], sf[:, 0:vw], reads=[sf], writes=[ko])
                else:
                    for t in range(6):
                        ps = nextbank(0, 4)
                        proj_fm(W, 8, t * 128, 128, h, ps)
                        kb.op(kb.dve, lambda t=t, ps=ps: nc.vector.tensor_copy(out=cq[:, t, :], in_=ps[:, :]), reads=[ps], writes=[cq])
                        q_ = sq2[t % 2]
                        kb.op(kb.act, lambda ps=ps, q_=q_: nc.scalar.activation(out=q_[:], in_=ps[:, :], func=AF.Square), reads=[ps], writes=[q_])
                        grp = 0 if t < 4 else 1
                        sb = pbank[4 + grp]
                        first = (t == 0) or (t == 4)
                        last = (t == 3) or (t == 5)
                        kb.op(kb.pe, lambda sb=sb, q_=q_, first=first, last=last: nc.tensor.matmul(sb[:, :], self.ones_f[:], q_[:], start=first, stop=last),
                              reads=[q_, self.ones_f], writes=[sb])
                    for grp, nf in ((0, 512), (1, 256)):
                        sb = pbank[4 + grp]
                        r = rs[grp]
                        kb.op(kb.act, lambda sb=sb, r=r, nf=nf: nc.scalar.activation(out=r[:], in_=sb[:, :], func=AF.Ln, bias=self.eps5[:, 0:1], scale=1.0 / nf),
                              reads=[sb, self.eps5], writes=[r])
                        kb.op(kb.act, lambda r=r: nc.scalar.activation(out=r[:], in_=r[:], func=AF.Exp, scale=-0.5), reads=[r], writes=[r])
                    for t in range(6):
                        nrm = qn[:, t:t + 1] if t < 4 else kvn[:, t - 4:t - 3]
                        r = rs[0 if t < 4 else 1]
                        kb.op(kb.dve, lambda t=t, nrm=nrm, r=r: nc.vector.scalar_tensor_tensor(out=cqn[:, t, :], in0=cq[:, t, :], scalar=nrm, in1=r[:],
                                                                                                op0=ALU.mult, op1=ALU.mult),
                              reads=[cq, qn, kvn, r], writes=[cqn])
                    for t in range(2):
                        kb.dma(kb.sp, self.KT[t, :, key0:key0 + 512], cqn[:, 4 + t, :], reads=[cqn], writes=[self.KT])
                    ps = nextbank(0, 4)
                    proj_fm(W, 8, 768, 32, h, ps)
                    rope_store(ps, (0, 32), lc, self.KT[2, 0:32, key0:key0 + 512], self.KT)
                    for hd in range(16):
                        ps = nextbank(0, 4)
                        proj_fm(Wuq, 4, hd * 96, 96, cqn, ps)
                        rope_store(ps, (0, 96), lc, self.QT[hd, 0:96, t0:t0 + 512], self.QT, R_rows=(64, 96))
                    if c < 2:
                        for j in range(4):
                            ps = pbank[6 + j % 2]
                            proj_tm(W, 512, 288, j, ps)
                            sf = stf[j % 2]
                            kb.op(kb.dve, lambda: nc.vector.memset(ss1[:], 0.0), writes=[ss1])
                            kb.op(kb.act, lambda ps=ps, sf=sf: nc.scalar.activation(out=sf[:, 512:768], in_=ps[:, 0:256], func=AF.Square, accum_out=ss1[:, 0:1]),
                                  reads=[ps], writes=[sf, ss1])
                            kb.op(kb.act, lambda: nc.scalar.activation(out=ss1[:], in_=ss1[:], func=AF.Ln, bias=self.eps5[:, 0:1], scale=1.0 / 256),
                                  reads=[ss1, self.eps5], writes=[ss1])
                            kb.op(kb.act, lambda: nc.scalar.activation(out=ss1[:], in_=ss1[:], func=AF.Exp, scale=-0.5), reads=[ss1], writes=[ss1])
                            kb.op(kb.dve, lambda ps=ps, sf=sf: nc.vector.scalar_tensor_tensor(out=sf[:, 0:256], in0=ps[:, 0:256], scalar=ss1[:, 0:1], in1=kvbc[:],
                                                                                               op0=ALU.mult, op1=ALU.mult),
                                  reads=[ps, ss1, kvbc], writes=[sf])
                            kb.op(kb.dve, lambda ps=ps, sf=sf: nc.vector.tensor_copy(out=sf[:, 256:288], in_=ps[:, 256:288]), reads=[ps], writes=[sf])
                            kb.dma(kb.sp, self.O["ckv2"][t0 + j * 128:t0 + (j + 1) * 128, :], sf[:, 0:256], reads=[sf], writes=[self.O["ckv2"]])
                            kb.dma(kb.sp, self.O["kpe2"][t0 + j * 128:t0 + (j + 1) * 128, :], sf[:, 256:288], reads=[sf], writes=[self.O["kpe2"]])
            kb.end_phase()

    def cache_prep(self, l, es):
        nc, kb, I = self.nc, self.kb, self.I
        m = l % 4
        if m in (0, 1, 3):
            kw = 1024 if m != 3 else 256
            ck = I["ck%d" % m]; cv = I["cv%d" % m]
            cvb = kb.tile(es, "cvb", [128, 4, 1024], BF16)
            kb.dma(kb.pool, cvb[:, :, 0:kw], cv[:, :].rearrange("(j p) f -> p j f", p=128), writes=[cvb])
            kb.dma(kb.sp, self.VS[LKEY0:LKEY0 + PAST, 0:kw].rearrange("(j p) f -> p j f", p=128), cvb[:, :, 0:kw], reads=[cvb], writes=[self.VS])
            srcs = [(ck, kw, 0)]
        else:
            srcs = [(I["cckv"], 256, 0), (I["ckpe"], 32, 2)]
        cb = kb.tile(es, "cacheb", [128, 4, 1024], F32)
        co = kb.tile(es, "cacheo", [128, 8, 512], BF16)
        for (src, kw, tile0) in srcs:
            kb.dma(kb.sp, cb[:, :, 0:kw], src[:, :].rearrange("(j p) f -> p j f", p=128), writes=[cb], nowaw=False)
            nt = (kw + 127) // 128
            for k in range(nt):
                rows = min(128, kw - k * 128)
                bk = self.bank[6 + k % 2]
                for j in range(4):
                    kb.op(kb.pe, lambda bk=bk, j=j, k=k, rows=rows: nc.tensor.transpose(
                        bk[0:rows, j * 128:(j + 1) * 128], cb[:, j, k * 128:k * 128 + rows], self.ident[:]),
                        reads=[cb, self.ident], writes=[bk])
                kb.op(kb.dve, lambda bk=bk, k=k, rows=rows: nc.vector.tensor_copy(out=co[0:rows, k, :], in_=bk[0:rows, :]),
                      reads=[bk], writes=[co])
                kb.dma(kb.sp, self.KT[tile0 + k, 0:rows, LKEY0:LKEY0 + PAST], co[0:rows, k, :], reads=[co], writes=[self.KT])

    def attn_steps(self, steps, scale, NQ, sbanks, pts):
        nc, kb = self.nc, self.kb
        n = len(steps)
        import os
        LA = int(os.environ.get("LA", "4"))
        NODEN = os.environ.get("NODEN", "") == "1"
        ns, npt = len(sbanks), len(pts)
        g0 = self.gstep

        def qk(i):
            st = steps[i]
            if st.get("pre") is not None:
                st["pre"]()
            bk = sbanks[(g0 + i) % ns]
            nk = st["nk"]
            nq = st.get("nq", NQ)
            kb.op(kb.pe, lambda: nc.tensor.matmul(bk[0:nk, 0:nq], st["kT"], st["q"], start=True, stop=True),
                  reads=st["rtk"], writes=[bk])
            pt = pts[(g0 + i) % npt]
            kb.op(kb.act, lambda: nc.scalar.activation(out=pt[0:nk, 0:nq], in_=bk[0:nk, 0:nq], func=AF.Exp, scale=scale),
                  reads=[bk], writes=[pt])
            if st.get("mask") is not None:
                self.mcnt += 1
                use_dve = (self.mcnt % 3 != 0)
                eng = kb.dve if use_dve else kb.pool
                eh = nc.vector if use_dve else nc.gpsimd
                kb.op(eng, lambda: eh.tensor_tensor(out=pt[0:nk, 0:nq], in0=pt[0:nk, 0:nq], in1=st["mask"], op=ALU.mult),
                      reads=[pt, st["mtk"]], writes=[pt])

        def pv(i):
            st = steps[i]
            pt = pts[(g0 + i) % npt]
            nk = st["nk"]
            nq = st.get("nq", NQ)
            kb.op(kb.pe, lambda: nc.tensor.matmul(st["o"], st["v"], pt[0:nk, 0:nq], start=st["first"], stop=st["last"]),
                  reads=[pt, st["vtk"]], writes=[st["otk"]])
            if st.get("den") is not None and not NODEN:
                kb.op(kb.pe, lambda: nc.tensor.matmul(st["den"], self.ones_b[0:nk, :], pt[0:nk, 0:nq], start=st["first"], stop=st["last"]),
                      reads=[pt, self.ones_b], writes=[st["dtk"]])
            if st.get("post") is not None:
                st["post"]()

        for i in range(min(LA, n)):
            qk(i)
        for i in range(n):
            if i + LA < n:
                qk(i + LA)
            pv(i)
        self.gstep += n

    def phase_B(self, l):
        m = l % 4
        self.gstep = 0
        self.mcnt = 0
        if m == 0:
            self.attn_diff(l)
        elif m == 1:
            self.attn_na(l)
        elif m == 2:
            self.attn_mla(l)
        else:
            self.attn_swa(l)

    def q_chunks(self):
        out = []
        for s in range(4):
            out.append((s * 256, 256, s * 256, 256))
        for c in range(8):
            out.append((NPR + c * 512, 512, LKEY0, PAST + NLAT))
        return out

    def attn_diff(self, l):
        import os
        nc, kb, I = self.nc, self.kb, self.I
        with ExitStack() as es:
            kt = [kb.tile(es, "kt%d" % i, [128, 2, NKEY], BF16) for i in range(2)]
            for k_ in kt:
                kb.op(kb.pool, lambda k_=k_: nc.gpsimd.memset(k_[64:128, 0, :], 0.0), writes=[k_])
                kb.op(kb.pool, lambda k_=k_: nc.gpsimd.memset(k_[0:64, 1, :], 0.0), writes=[k_])
            vt = [kb.tile(es, "vt%d" % i, [128, NKEY // 128, 128], BF16) for i in range(2)]
            qt = [kb.tile(es, "qt%d" % i, [128, 512], BF16) for i in range(3)]
            pts = [kb.tile(es, "pt%d" % i, [128, 512], BF16) for i in range(6)]
            accs = [[kb.tile(es, "acc%d_%d" % (i, j), [128, 512], F32) for j in range(4)] for i in range(2)]
            self._pend = None
            ot = [kb.tile(es, "ot%d" % i, [128, 512], BF16) for i in range(2)]
            ra = [kb.tile(es, "ra%d" % i, [128, 512], F32) for i in range(2)]
            fa = [kb.tile(es, "fa%d" % i, [128, 512], F32) for i in range(2)]
            sq = kb.tile(es, "sq", [128, 512], F32)
            rstd = kb.tile(es, "rstd", [128, 512], F32)
            lam = kb.tile(es, "lam", [128, 256], F32)
            lp = kb.tile(es, "lp", [128, 128], F32)
            ls = kb.tile(es, "ls", [128, 2], F32)
            nlam = kb.tile(es, "nlam", [128, 1], F32)
            subw = kb.tile(es, "subw", [128, 1], F32)
            kb.dma(kb.sp, lam[:], I["l0_lam_bc"][:, :], writes=[lam])
            kb.dma(kb.sp, subw[:], I["l0_subln"][:, :], writes=[subw])
            kb.op(kb.dve, lambda: nc.vector.tensor_tensor(out=lp[:, 0:64], in0=lam[:, 0:64], in1=lam[:, 64:128], op=ALU.mult), reads=[lam], writes=[lp])
            kb.op(kb.dve, lambda: nc.vector.tensor_tensor(out=lp[:, 64:128], in0=lam[:, 128:192], in1=lam[:, 192:256], op=ALU.mult), reads=[lam], writes=[lp])
            kb.op(kb.dve, lambda: nc.vector.reduce_sum(out=ls[:, 0:1], in_=lp[:, 0:64], axis=AX.X), reads=[lp], writes=[ls])
            kb.op(kb.dve, lambda: nc.vector.reduce_sum(out=ls[:, 1:2], in_=lp[:, 64:128], axis=AX.X), reads=[lp], writes=[ls])
            kb.op(kb.act, lambda: nc.scalar.activation(out=ls[:], in_=ls[:], func=AF.Exp), reads=[ls], writes=[ls])
            kb.op(kb.dve, lambda: nc.vector.tensor_tensor(out=nlam[:], in0=ls[:, 1:2], in1=ls[:, 0:1], op=ALU.subtract), reads=[ls], writes=[nlam])
            kb.op(kb.dve, lambda: nc.vector.tensor_scalar_add(out=nlam[:], in0=nlam[:], scalar1=-LAM_INIT), reads=[nlam], writes=[nlam])
            kb.op(kb.dve, lambda: nc.vector.tensor_scalar_mul(out=subw[:], in0=subw[:], scalar1=(1.0 - LAM_INIT)), reads=[subw], writes=[subw])
            B = self.bank
            sb = [B[4], B[5], B[6]]
            scale = 64 ** -0.5
            ci = 0
            chunks = self.q_chunks()

            def load_head(h):
                k_ = kt[h % 2]; v_ = vt[h % 2]
                kb.dma(kb.sp, k_[0:64, 0, :], self.KT[h, 0:64, :], reads=[self.KT], writes=[k_], nowaw=False)
                kb.dma(kb.sp, k_[64:128, 1, :], self.KT[h, 64:128, :], reads=[self.KT], writes=[k_])
                for j4 in range(4):
                    kb.dma(kb.sp, v_[:, j4 * 11:(j4 + 1) * 11, :], self.VS[j4 * 1408:(j4 + 1) * 1408, h * 128:(h + 1) * 128].rearrange("(t p) d -> p t d", p=128),
                           reads=[self.VS], writes=[v_])

            def load_q(idx):
                h, c = divmod(idx, len(chunks))
                (t0, nq, key0, nkeys) = chunks[c]
                q_ = qt[idx % 3]
                kb.dma(kb.sp, q_[:, 0:nq], self.QT[h, :, t0:t0 + nq], reads=[self.QT], writes=[q_])

            load_head(0)
            load_q(0)
            for h in range(8):
                k_ = kt[h % 2]; v_ = vt[h % 2]
                if h + 1 < 8:
                    load_head(h + 1)
                allsteps = []
                for (t0, nq, key0, nkeys) in chunks:
                    q_ = qt[ci % 3]

                    def pre(ci=ci):
                        if ci + 1 < 8 * len(chunks):
                            load_q(ci + 1)
                    steps = []
                    nkt = nkeys // 128
                    for i in range(nkt):
                        kk = key0 + i * 128
                        for hf in range(2):
                            steps.append(dict(kT=k_[:, hf, kk:kk + 128], q=q_[:, 0:nq], nk=128, nq=nq,
                                              v=v_[:, kk // 128, :], o=B[hf * 2][:, 0:nq], otk=B[hf * 2],
                                              den=B[hf * 2 + 1][:, 0:nq], dtk=B[hf * 2 + 1],
                                              first=(i == 0), last=(i == nkt - 1), rtk=[k_, q_], vtk=v_))

                    def post(ci=ci, nq=nq, h=h, t0=t0):
                        acc = accs[ci % 2]
                        if self._pend is not None:
                            self._pend()
                        for j in range(4):
                            kb.op(kb.dve, lambda j=j: nc.vector.tensor_copy(out=acc[j][:, 0:nq], in_=B[j][:, 0:nq]), reads=[B[j]], writes=[acc[j]])

                        def fin():
                            for hf in range(2):
                                kb.op(kb.dve, lambda hf=hf: nc.vector.reciprocal(out=ra[hf][:, 0:nq], in_=acc[hf * 2 + 1][:, 0:nq]), reads=[acc[hf * 2 + 1]], writes=[ra[hf]])
                                kb.op(kb.dve, lambda hf=hf: nc.vector.tensor_tensor(out=fa[hf][:, 0:nq], in0=acc[hf * 2][:, 0:nq], in1=ra[hf][:, 0:nq], op=ALU.mult),
                                      reads=[acc[hf * 2], ra[hf]], writes=[fa[hf]])
                            kb.op(kb.dve, lambda: nc.vector.scalar_tensor_tensor(out=fa[0][:, 0:nq], in0=fa[1][:, 0:nq], scalar=nlam[:, 0:1], in1=fa[0][:, 0:nq],
                                                                                 op0=ALU.mult, op1=ALU.add), reads=[fa[1], nlam, fa[0]], writes=[fa[0]])
                            kb.op(kb.pool, lambda: nc.gpsimd.tensor_tensor(out=sq[:, 0:nq], in0=fa[0][:, 0:nq], in1=fa[0][:, 0:nq], op=ALU.mult), reads=[fa[0]], writes=[sq])
                            kb.op(kb.pe, lambda: nc.tensor.matmul(B[7][:, 0:nq], self.ones_f[:], sq[:, 0:nq], start=True, stop=True),
                                  reads=[sq, self.ones_f], writes=[B[7]])
                            kb.op(kb.act, lambda: nc.scalar.activation(out=rstd[:, 0:nq], in_=B[7][:, 0:nq], func=AF.Ln, bias=self.eps5[:, 0:1], scale=1.0 / 128),
                                  reads=[B[7], self.eps5], writes=[rstd])
                            kb.op(kb.act, lambda: nc.scalar.activation(out=rstd[:, 0:nq], in_=rstd[:, 0:nq], func=AF.Exp, scale=-0.5), reads=[rstd], writes=[rstd])
                            o_ = ot[ci % 2]
                            kb.op(kb.dve, lambda: nc.vector.scalar_tensor_tensor(out=o_[:, 0:nq], in0=fa[0][:, 0:nq], scalar=subw[:, 0:1], in1=rstd[:, 0:nq],
                                                                                 op0=ALU.mult, op1=ALU.mult), reads=[fa[0], subw, rstd], writes=[o_])
                            kb.dma(kb.sp, self.OT[h, :, t0:t0 + nq], o_[:, 0:nq], reads=[o_], writes=[self.OT])
                        self._pend = fin
                    steps[0]["pre"] = pre
                    steps[-1]["post"] = post
                    allsteps.extend(steps)
                    ci += 1
                self.attn_steps(allsteps, scale, 512, sb, pts)
            if self._pend is not None:
                self._pend()
                self._pend = None
            kb.end_phase()

    def finalize_aug(self, ps, nq, rb, o_, dst_ap, extra_den=None, act_recip=False):
        nc, kb = self.nc, self.kb
        if extra_den is not None:
            tk, ap = extra_den
            if act_recip:
                kb.op(kb.act, lambda: nc.scalar.activation(out=rb[64:128, 0:nq], in_=ps[64:128, 0:nq], func=AF.Ln, bias=ap, scale=1.0),
                      reads=[ps, tk], writes=[rb])
                kb.op(kb.act, lambda: nc.scalar.activation(out=rb[64:128, 0:nq], in_=rb[64:128, 0:nq], func=AF.Exp, scale=-1.0), reads=[rb], writes=[rb])
            else:
                kb.op(kb.dve, lambda: nc.vector.tensor_scalar_add(out=rb[64:128, 0:nq], in0=ps[64:128, 0:nq], scalar1=ap),
                      reads=[ps, tk], writes=[rb])
                kb.op(kb.dve, lambda: nc.vector.reciprocal(out=rb[64:128, 0:nq], in_=rb[64:128, 0:nq]), reads=[rb], writes=[rb])
        elif act_recip:
            kb.op(kb.act, lambda: nc.scalar.activation(out=rb[64:128, 0:nq], in_=ps[64:128, 0:nq], func=AF.Ln), reads=[ps], writes=[rb])
            kb.op(kb.act, lambda: nc.scalar.activation(out=rb[64:128, 0:nq], in_=rb[64:128, 0:nq], func=AF.Exp, scale=-1.0), reads=[rb], writes=[rb])
        else:
            kb.op(kb.dve, lambda: nc.vector.reciprocal(out=rb[64:128, 0:nq], in_=ps[64:128, 0:nq]), reads=[ps], writes=[rb])
        kb.op(kb.dve, lambda: nc.vector.tensor_tensor(out=o_[0:64, 0:nq], in0=ps[0:64, 0:nq], in1=rb[64:128, 0:nq], op=ALU.mult),
              reads=[ps, rb], writes=[o_])
        kb.dma(kb.sp, dst_ap, o_[0:64, 0:nq], reads=[o_], writes=[self.OT])

    def attn_na(self, l):
        import os
        AR = os.environ.get("AR", "1") == "1"
        nc, kb, I = self.nc, self.kb, self.I
        with ExitStack() as es:
            kt = [kb.tile(es, "kt%d" % i, [128, NKEY], BF16) for i in range(2)]
            kb.op(kb.pool, lambda: nc.gpsimd.memset(kt[0][64:128, :], 0.0), writes=[kt[0]])
            kb.op(kb.pool, lambda: nc.gpsimd.memset(kt[1][0:64, :], 0.0), writes=[kt[1]])
            vt = [kb.tile(es, "vt%d" % i, [128, NKEY // 128, 128], BF16) for i in range(2)]
            qt = [kb.tile(es, "qt%d" % i, [128, 512], BF16) for i in range(3)]
            pts = [kb.tile(es, "pt%d" % i, [128, 512], BF16) for i in range(6)]
            ot = [kb.tile(es, "ot%d" % i, [64, 512], BF16) for i in range(3)]
            rb = [kb.tile(es, "rb%d" % i, [128, 512], F32) for i in range(2)]
            Mt = kb.tile(es, "Mt", [128, 2, 1024], F32)
            Et = [kb.tile(es, "Et%d" % i, [128, 1024], F32) for i in range(2)]
            tabs = [kb.tile(es, "tab%d" % i, [128, 2, 1024], BF16) for i in range(2)]
            kb.dma(kb.sp, Mt[:], I["na_M"][:, :, :].rearrange("v p n -> p v n"), writes=[Mt])
            for v_ in vt:
                kb.op(kb.pool, lambda v_=v_: nc.gpsimd.memset(v_[:, :, 64:128], 1.0), writes=[v_])
            B = self.bank
            sb = [B[4], B[5], B[6], B[7]]
            ob = [B[0], B[1], B[2], B[3]]
            scale = 64 ** -0.5
            ci = 0
            chunks = [(sq_ * 256, 256) for sq_ in range(4)] + [(NPR + c_ * 512, 512) for c_ in range(8)]

            def load_head(h):
                k_ = kt[h % 2]; v_ = vt[h % 2]
                base = (h % 2) * 64
                kb.dma(kb.sp, k_[base:base + 64, :], self.KT[h // 2, base:base + 64, :], reads=[self.KT], writes=[k_], nowaw=False)
                for j4 in range(4):
                    kb.dma(kb.sp, v_[:, j4 * 11:(j4 + 1) * 11, 0:64], self.VS[j4 * 1408:(j4 + 1) * 1408, h * 64:(h + 1) * 64].rearrange("(t p) d -> p t d", p=128),
                           reads=[self.VS], writes=[v_], nowaw=(j4 > 0))

            def load_q(idx):
                h, c = divmod(idx, len(chunks))
                (t0, nq) = chunks[c]
                kb.dma(kb.sp, qt[idx % 3][:, 0:nq], self.QT[h // 2, :, t0:t0 + nq], reads=[self.QT], writes=[qt[idx % 3]])

            load_head(0)
            load_q(0)
            allsteps = []

            def add(steps, nq_default, pre, post):
                for st in steps:
                    st.setdefault("nq", nq_default)
                if pre is not None:
                    steps[0]["pre"] = pre
                if post is not None:
                    steps[-1]["post"] = post
                allsteps.extend(steps)

            for h in range(16):
                k_ = kt[h % 2]; v_ = vt[h % 2]; E_ = Et[h % 2]; tb = tabs[h % 2]
                tl, base = h // 2, (h % 2) * 64
                def head_setup(h=h, E_=E_, tb=tb):
                    kb.dma(kb.sp, E_[:], I["na_E"][h, :, :], writes=[E_])
                    kb.op(kb.act, lambda: nc.scalar.activation(out=E_[:], in_=E_[:], func=AF.Exp), reads=[E_], writes=[E_])
                    for vv in range(2):
                        kb.op(kb.dve, lambda vv=vv: nc.vector.tensor_tensor(out=tb[:, vv, :], in0=E_[:], in1=Mt[:, vv, :], op=ALU.mult),
                              reads=[E_, Mt], writes=[tb])
                for s in range(4):
                    t0 = s * 256
                    q_ = qt[ci % 3]

                    def pre(ci=ci, s=s, hs=head_setup, h=h):
                        if s == 0:
                            hs()
                        if s == 2 and h + 1 < 16:
                            load_head(h + 1)
                        if ci + 1 < 16 * 12:
                            load_q(ci + 1)
                    o_b = ob[ci % 4]
                    steps = []
                    for i in range(2):
                        kk = t0 + i * 128
                        steps.append(dict(kT=k_[:, kk:kk + 128], q=q_[:, 0:256], nk=128, v=v_[:, kk // 128, :], o=o_b[:, 0:256], otk=o_b,
                                          first=(i == 0), last=(i == 1), rtk=[k_, q_], vtk=v_))

                    def post(o_b=o_b, ci=ci, tl=tl, base=base, t0=t0):
                        self.finalize_aug(o_b, 256, rb[ci % 2], ot[ci % 3], self.OT[tl, base:base + 64, t0:t0 + 256], act_recip=AR)
                    add(steps, 256, pre, post)
                    ci += 1
                for c in range(8):
                    t0 = NPR + c * 512
                    q_ = qt[ci % 3]

                    def pre(ci=ci):
                        if ci + 1 < 16 * 12:
                            load_q(ci + 1)
                    o_b = ob[ci % 4]
                    cst = []
                    if 1 <= c <= 6:
                        steps = []
                        for i in range(4):
                            kk = LKEY0 + i * 128
                            steps.append(dict(kT=k_[:, kk:kk + 128], q=q_[:, 0:512], nk=128, nq=512, v=v_[:, kk // 128, :], o=o_b[:, 0:512], otk=o_b,
                                              first=(i == 0), last=False, rtk=[k_, q_], vtk=v_))
                        rks = list(range(8 * c - 4, 8 * c + 12, 2))
                        for idx, Rk in enumerate(rks):
                            lo = max(Rk - 4, 8 * c); hi = min(Rk + 4, 8 * c + 6)
                            q0 = (lo - 8 * c) // 2 * 128; q1 = ((hi - 8 * c) // 2 + 1) * 128
                            w0 = 7 + lo - Rk
                            kk = LNEW0 + Rk * 64
                            steps.append(dict(kT=k_[:, kk:kk + 128], q=q_[:, q0:q1], nk=128, nq=q1 - q0, v=v_[:, kk // 128, :], o=o_b[:, q0:q1], otk=o_b,
                                              first=False, last=(idx == len(rks) - 1), rtk=[k_, q_], vtk=v_,
                                              mask=tb[:, 0, w0 * 64:w0 * 64 + (q1 - q0)], mtk=tb))
                        cst.extend(steps)
                    else:
                        steps = []
                        for i in range(4):
                            kk = LKEY0 + i * 128
                            steps.append(dict(kT=k_[:, kk:kk + 128], q=q_[:, 0:512], nk=128, nq=512, v=v_[:, kk // 128, :], o=o_b[:, 0:512], otk=o_b,
                                              first=(i == 0), last=False, rtk=[k_, q_], vtk=v_))
                        cst.extend(steps)
                        for sub in range(4):
                            mrow = c * 4 + sub
                            Rq = 2 * mrow
                            if mrow in (0, 1):
                                rks, var = [0, 2, 4, 6], 1
                            elif mrow in (30, 31):
                                rks, var = [56, 58, 60, 62], 1
                            else:
                                rks, var = [Rq - 4 + 2 * t for t in range(5)], 0
                            qs = q_[:, sub * 128:(sub + 1) * 128]
                            oap = o_b[:, sub * 128:(sub + 1) * 128]
                            steps = []
                            for i, Rk in enumerate(rks):
                                kk = LNEW0 + Rk * 64
                                w0 = 7 + Rq - Rk
                                steps.append(dict(kT=k_[:, kk:kk + 128], q=qs, nk=128, nq=128, v=v_[:, kk // 128, :], o=oap, otk=o_b,
                                                  first=False, last=(i == len(rks) - 1), rtk=[k_, q_], vtk=v_,
                                                  mask=tb[:, var, w0 * 64:(w0 + 2) * 64], mtk=tb))
                            cst.extend(steps)

                    def post(o_b=o_b, ci=ci, tl=tl, base=base, t0=t0):
                        self.finalize_aug(o_b, 512, rb[ci % 2], ot[ci % 3], self.OT[tl, base:base + 64, t0:t0 + 512], act_recip=AR)
                    add(cst, 512, pre, post)
                    ci += 1
            self.attn_steps(allsteps, scale, 512, sb, pts)
            kb.end_phase()

    def attn_swa(self, l):
        import os
        AR = os.environ.get("AR", "1") == "1"
        nc, kb, I = self.nc, self.kb, self.I
        with ExitStack() as es:
            kt = [kb.tile(es, "kt%d" % i, [128, 2, NKEY], BF16) for i in range(2)]
            for k_ in kt:
                kb.op(kb.pool, lambda k_=k_: nc.gpsimd.memset(k_[64:128, 0, :], 0.0), writes=[k_])
                kb.op(kb.pool, lambda k_=k_: nc.gpsimd.memset(k_[0:64, 1, :], 0.0), writes=[k_])
            vt = [kb.tile(es, "vt%d" % i, [128, NKEY // 128, 128], BF16) for i in range(2)]
            qt = [kb.tile(es, "qt%d" % i, [128, 512], BF16) for i in range(3)]
            pts = [kb.tile(es, "pt%d" % i, [128, 512], BF16) for i in range(6)]
            ot = [kb.tile(es, "ot%d" % i, [64, 512], BF16) for i in range(3)]
            rb = [kb.tile(es, "rb%d" % i, [128, 512], F32) for i in range(2)]
            M3 = kb.tile(es, "M3", [128, 384], BF16)
            esink = kb.tile(es, "esink", [128, 16], F32)
            kb.op(kb.pool, lambda: nc.gpsimd.memset(M3[:, 128:256], 1.0), writes=[M3])
            kb.dma(kb.pool, M3[:, 0:128], I["tri"][1, :, :], writes=[M3], nowaw=False)
            kb.dma(kb.pool, M3[:, 256:384], I["tri"][0, :, :], writes=[M3])
            kb.dma(kb.sp, esink[:], I["l3_sink_bc"][:, :], writes=[esink])
            kb.op(kb.act, lambda: nc.scalar.activation(out=esink[:], in_=esink[:], func=AF.Exp), reads=[esink], writes=[esink])
            for v_ in vt:
                kb.op(kb.pool, lambda v_=v_: nc.gpsimd.memset(v_[:, :, 64:128], 1.0), writes=[v_])
            B = self.bank
            sb = [B[4], B[5], B[6], B[7]]
            ob = [B[0], B[1], B[2], B[3]]
            scale = 64 ** -0.5
            ci = 0
            chunks = [(sq_ * 256, 256) for sq_ in range(4)] + [(NPR + c_ * 512, 512) for c_ in range(8)]

            def load_group(g):
                k2 = kt[g % 2]; v_ = vt[g % 2]
                src = self.KT[g // 2, (g % 2) * 64:(g % 2) * 64 + 64, :]
                kb.dma(kb.sp, k2[0:64, 0, :], src, reads=[self.KT], writes=[k2], nowaw=False)
                kb.dma(kb.sp, k2[64:128, 1, :], src, reads=[self.KT], writes=[k2])
                for j4 in range(4):
                    kb.dma(kb.sp, v_[:, j4 * 11:(j4 + 1) * 11, 0:64], self.VS[j4 * 1408:(j4 + 1) * 1408, g * 64:(g + 1) * 64].rearrange("(t p) d -> p t d", p=128),
                           reads=[self.VS], writes=[v_], nowaw=(j4 > 0))

            def load_q(idx):
                h, c = divmod(idx, len(chunks))
                (t0, nq) = chunks[c]
                kb.dma(kb.sp, qt[idx % 3][:, 0:nq], self.QT[h // 2, :, t0:t0 + nq], reads=[self.QT], writes=[qt[idx % 3]])

            load_group(0)
            load_q(0)
            allsteps = []

            def add(steps, nq_default, pre, post):
                for st in steps:
                    st.setdefault("nq", nq_default)
                if pre is not None:
                    steps[0]["pre"] = pre
                if post is not None:
                    steps[-1]["post"] = post
                allsteps.extend(steps)

            for g in range(4):
                k2 = kt[g % 2]; v_ = vt[g % 2]
                for hh in range(4):
                    h = g * 4 + hh
                    tl, base = h // 2, (h % 2) * 64
                    kv = h % 2
                    sk = (esink, esink[64:128, h:h + 1])
                    for s in range(4):
                        t0 = s * 256
                        q_ = qt[ci % 3]

                        def pre(ci=ci, first=(hh == 0 and s == 2), g=g):
                            if first and g + 1 < 4:
                                load_group(g + 1)
                            if ci + 1 < 16 * 12:
                                load_q(ci + 1)
                        o_b = ob[ci % 4]
                        steps = []
                        for i in range(2):
                            kk = t0 + i * 128
                            steps.append(dict(kT=k2[:, kv, kk:kk + 128], q=q_[:, 0:256], nk=128, v=v_[:, kk // 128, :], o=o_b[:, 0:256], otk=o_b,
                                              first=(i == 0), last=(i == 1), rtk=[k2, q_], vtk=v_))

                        def post(o_b=o_b, ci=ci, tl=tl, base=base, t0=t0, sk=sk):
                            self.finalize_aug(o_b, 256, rb[ci % 2], ot[ci % 3], self.OT[tl, base:base + 64, t0:t0 + 256], extra_den=sk, act_recip=AR)
                        add(steps, 256, pre, post)
                        ci += 1
                    for c in range(8):
                        t0 = NPR + c * 512
                        q_ = qt[ci % 3]

                        def pre(ci=ci):
                            if ci + 1 < 16 * 12:
                                load_q(ci + 1)
                        o_b = ob[ci % 4]
                        steps = []
                        for i in range(4):
                            kk = LKEY0 + i * 128
                            steps.append(dict(kT=k2[:, kv, kk:kk + 128], q=q_[:, 0:512], nk=128, nq=512, v=v_[:, kk // 128, :], o=o_b[:, 0:512], otk=o_b,
                                              first=(i == 0), last=False, rtk=[k2, q_], vtk=v_))
                        band = []
                        for j in range(4 * c - 1, 4 * c + 5):
                            if j < 0 or j > 31:
                                continue
                            blo = max(j - 1, 4 * c); bhi = min(j + 1, 4 * c + 3)
                            if blo > bhi:
                                continue
                            band.append((j, blo, bhi))
                        for idx, (j, blo, bhi) in enumerate(band):
                            q0 = (blo - 4 * c) * 128; q1 = (bhi - 4 * c + 1) * 128
                            kk = LNEW0 + j * 128
                            steps.append(dict(kT=k2[:, kv, kk:kk + 128], q=q_[:, q0:q1], nk=128, nq=q1 - q0, v=v_[:, kk // 128, :], o=o_b[:, q0:q1], otk=o_b,
                                              first=False, last=(idx == len(band) - 1), rtk=[k2, q_], vtk=v_,
                                              mask=M3[:, (blo - j + 1) * 128:(bhi - j + 2) * 128], mtk=M3))
                        def post(o_b=o_b, ci=ci, tl=tl, base=base, t0=t0, sk=sk):
                            self.finalize_aug(o_b, 512, rb[ci % 2], ot[ci % 3], self.OT[tl, base:base + 64, t0:t0 + 512], extra_den=sk, act_recip=AR)
                        add(steps, 512, pre, post)
                        ci += 1
            self.attn_steps(allsteps, scale, 512, sb, pts)
            kb.end_phase()

    def attn_mla(self, l):
        nc, kb, I = self.nc, self.kb, self.I
        with ExitStack() as es:
            ckv = kb.tile(es, "ckvT", [128, 2, NKEY], BF16)
            kt = [kb.tile(es, "kt%d" % i, [96, NKEY], BF16) for i in range(2)]
            vt = [kb.tile(es, "vt%d" % i, [128, NKEY // 128, 128], BF16) for i in range(2)]
            qt = [kb.tile(es, "qt%d" % i, [96, 512], BF16) for i in range(3)]
            pts = [kb.tile(es, "pt%d" % i, [128, 512], BF16) for i in range(6)]
            ot = [kb.tile(es, "ot%d" % i, [64, 512], BF16) for i in range(3)]
            rb = [kb.tile(es, "rb%d" % i, [128, 512], F32) for i in range(2)]
            Wukv = kb.tile(es, "Wukv", [128, 2, 2048], BF16)
            self.load_w(Wukv, I["l2_w_ukv"], 2)
            kb.dma(kb.sp, ckv[:], self.KT[0:2, :, :].rearrange("k p n -> p k n"), reads=[self.KT], writes=[ckv])
            for k_ in kt:
                kb.dma(kb.sp, k_[64:96, :], self.KT[2, 0:32, :], reads=[self.KT], writes=[k_])
            for v_ in vt:
                kb.op(kb.pool, lambda v_=v_: nc.gpsimd.memset(v_[:, :, 64:128], 1.0), writes=[v_])
            B = self.bank
            sb = [B[4], B[5], B[6]]
            ob = [B[0], B[1]]
            xb = [B[2], B[3], B[7]]
            scale = 96 ** -0.5
            ci = 0
            xi = 0
            chunks = self.q_chunks()

            def load_q(idx):
                h, c = divmod(idx, len(chunks))
                (t0, nq, key0, nkeys) = chunks[c]
                kb.dma(kb.sp, qt[idx % 3][:, 0:nq], self.QT[h, 0:96, t0:t0 + nq], reads=[self.QT], writes=[qt[idx % 3]])

            load_q(0)
            for h in range(16):
                k_ = kt[h % 2]; v_ = vt[h % 2]
                tl, base = h // 2, (h % 2) * 64
                for kc in range(NKEY // 512):
                    ps = xb[xi % 3]; xi += 1
                    for k in range(2):
                        kb.op(kb.pe, lambda ps=ps, k=k, kc=kc: nc.tensor.matmul(ps[0:64, :], Wukv[:, k, h * 128:h * 128 + 64], ckv[:, k, kc * 512:(kc + 1) * 512],
                                                                               start=(k == 0), stop=(k == 1)), reads=[Wukv, ckv], writes=[ps])
                    kb.op(kb.dve, lambda ps=ps, kc=kc, k_=k_: nc.vector.tensor_copy(out=k_[0:64, kc * 512:(kc + 1) * 512], in_=ps[0:64, :]),
                          reads=[ps], writes=[k_])
                ntl = NKEY // 128
                for tg in range(0, ntl, 8):
                    ps = xb[xi % 3]; xi += 1
                    nt = min(8, ntl - tg)
                    for t in range(nt):
                        for k in range(2):
                            kb.op(kb.pe, lambda ps=ps, k=k, t=t, tg=tg: nc.tensor.matmul(ps[:, t * 64:(t + 1) * 64], ckv[:, k, (tg + t) * 128:(tg + t + 1) * 128],
                                                                                        Wukv[:, k, h * 128 + 64:h * 128 + 128], start=(k == 0), stop=(k == 1)),
                                  reads=[Wukv, ckv], writes=[ps])
                    kb.op(kb.dve, lambda ps=ps, tg=tg, nt=nt, v_=v_: nc.vector.tensor_copy(
                        out=v_[:, tg:tg + nt, 0:64], in_=ps[:, 0:nt * 64].rearrange("p (t d) -> p t d", d=64)),
                        reads=[ps], writes=[v_])
                allsteps = []
                for (t0, nq, key0, nkeys) in chunks:
                    q_ = qt[ci % 3]

                    def pre(ci=ci):
                        if ci + 1 < 16 * len(chunks):
                            load_q(ci + 1)
                    o_b = ob[ci % 2]
                    steps = []
                    nkt = nkeys // 128
                    for i in range(nkt):
                        kk = key0 + i * 128
                        steps.append(dict(kT=k_[:, kk:kk + 128], q=q_[:, 0:nq], nk=128, nq=nq, v=v_[:, kk // 128, :], o=o_b[:, 0:nq], otk=o_b,
                                          first=(i == 0), last=(i == nkt - 1), rtk=[k_, q_], vtk=v_))

                    def post(o_b=o_b, ci=ci, tl=tl, base=base, t0=t0, nq=nq):
                        self.finalize_aug(o_b, nq, rb[ci % 2], ot[ci % 3], self.OT[tl, base:base + 64, t0:t0 + nq])
                    steps[0]["pre"] = pre
                    steps[-1]["post"] = post
                    allsteps.extend(steps)
                    ci += 1
                self.attn_steps(allsteps, scale, 512, sb, pts)
            kb.end_phase()

    def phase_C(self, l):
        nc, kb, I = self.nc, self.kb, self.I
        m = l % 4
        N = 256
        N1 = 512
        nchunk = NTOK // N
        B = self.bank
        with ExitStack() as es:
            W1 = kb.tile(es, "W1", [128, 8, 4096], BF16)
            W2a = kb.tile(es, "W2a", [128, 16, 1024], BF16)
            with ExitStack() as es1:
                Wo = kb.tile(es1, "Wo", [128, 8, 1024], BF16)
                self.load_w(Wo, I["l%d_w_o" % m], 8, parts=2)
                self.load_w(W1, I["w_mlp1"][l], 8, parts=8)
                self.load_w(W2a, I["w_mlp2"][l][0:2048, :], 16, parts=4)
                NB1 = 3
                xT = [kb.tile(es1, "xT%d" % i, [128, 8, N1], F32) for i in range(NB1)]
                xS = [kb.subs(x_, 8) for x_ in xT]
                oT = [kb.tile(es1, "oT%d" % i, [128, 8, N1], BF16) for i in range(2)]
                sq = [kb.tile(es1, "sq%d" % i, [128, 512], F32) for i in range(2)]
                mean = kb.tile(es1, "mean", [128, 512], F32)
                var = kb.tile(es1, "var", [128, 512], F32)
                nch1 = NTOK // N1

                def load1(c):
                    kb.dma(kb.sp, oT[c % 2][:], self.OT[:, :, c * N1:(c + 1) * N1].rearrange("k p n -> p k n"), reads=[self.OT], writes=[oT[c % 2]])
                    kb.dma(kb.sp, xT[c % NB1][:], self.XA[:, :, c * N1:(c + 1) * N1].rearrange("k p n -> p k n"), reads=[self.XA], writes=[xT[c % NB1]])
                    KB.loaded(xT[c % NB1], xS[c % NB1])

                def sA(c):
                    cond = 0 if c * N1 < NPR else 1
                    x = xT[c % NB1]; o = oT[c % 2]; xs = xS[c % NB1]
                    for n in range(8):
                        ps = B[n % 4]
                        for k in range(8):
                            kb.op(kb.pe, lambda ps=ps, k=k, n=n: nc.tensor.matmul(ps[:, 0:N1], Wo[:, k, n * 128:(n + 1) * 128], o[:, k, :],
                                                                                 start=(k == 0), stop=(k == 7)), reads=[Wo, o], writes=[ps])
                        kb.op(kb.dve, lambda ps=ps, n=n: nc.vector.scalar_tensor_tensor(out=x[:, n, :], in0=ps[:, 0:N1], scalar=self.mvc(l, 2, n, cond), in1=x[:, n, :],
                                                                                       op0=ALU.mult, op1=ALU.add), reads=[ps, self.mv, xs[n]], writes=[xs[n]])

                def sB(c):
                    self.ln_epilogue((sq, mean, var, B[4 + (c % 2) * 2], B[5 + (c % 2) * 2]), xT[c % NB1], N1, l, 0, self.XB, c * N1, sub=xS[c % NB1])

                load1(0)
                load1(1)
                sA(0)
                for c in range(nch1):
                    if c + 2 < nch1:
                        load1(c + 2)
                    if c + 1 < nch1:
                        sA(c + 1)
                    sB(c)
                kb.end_phase()

            W2b = kb.tile(es, "W2b", [128, 16, 1024], BF16)
            self.load_w(W2b, I["w_mlp2"][l][2048:4096, :], 16, parts=4)
            xT = [kb.tile(es, "xT%d" % i, [128, 8, N], F32) for i in range(2)]
            xS = [kb.subs(x_, 8) for x_ in xT]
            hT = [kb.tile(es, "hT%d" % i, [128, 8, N], BF16) for i in range(2)]
            uT = kb.tile(es, "uT", [128, 32, N], BF16)
            rl = [kb.tile(es, "rl%d" % i, [128, 2 * N], F32) for i in range(2)]
            sq = [kb.tile(es, "sq%d" % i, [128, 512], F32) for i in range(2)]
            mean = kb.tile(es, "mean", [128, 512], F32)
            var = kb.tile(es, "var", [128, 512], F32)

            def load(c):
                x = xT[c % 2]
                kb.dma(kb.sp, x[:], self.XB[:, :, c * N:(c + 1) * N].rearrange("k p n -> p k n"), reads=[self.XB], writes=[x])
                KB.loaded(x, xS[c % 2])

            def s1(c):
                cond = 0 if c * N < NPR else 1
                x = xT[c % 2]; h = hT[c % 2]; xs = xS[c % 2]
                for k in range(8):
                    kb.op(kb.act, lambda k=k: nc.scalar.activation(out=h[:, k, :], in_=x[:, k, :], func=AF.Identity,
                                                                   bias=self.mvc(l, 3, k, cond), scale=self.mvc(l, 4, k, cond)),
                          reads=[xs[k], self.mv], writes=[h])
                for fp in range(16):
                    ps = B[fp % 4]
                    for hf in range(2):
                        f = fp * 2 + hf
                        for k in range(8):
                            kb.op(kb.pe, lambda ps=ps, k=k, f=f, hf=hf: nc.tensor.matmul(ps[:, hf * N:(hf + 1) * N], W1[:, k, f * 128:(f + 1) * 128], h[:, k, :],
                                                                                        start=(k == 0), stop=(k == 7)), reads=[W1, h], writes=[ps])
                    r = rl[fp % 2]
                    kb.op(kb.act, lambda ps=ps, r=r: nc.scalar.activation(out=r[:], in_=ps[:, :], func=AF.Relu), reads=[ps], writes=[r])
                    kb.op(kb.pool, lambda r=r, fp=fp: nc.gpsimd.tensor_tensor(out=uT[:, 2 * fp:2 * fp + 2, :], in0=r[:].rearrange("p (a n) -> p a n", a=2),
                                                                            in1=r[:].rearrange("p (a n) -> p a n", a=2), op=ALU.mult),
                          reads=[r], writes=[uT])

            def s2(c):
                cond = 0 if c * N < NPR else 1
                x = xT[c % 2]; xs = xS[c % 2]
                for n in range(8):
                    ps = B[4 + n % 2]
                    for f in range(32):
                        W2x = W2a if f < 16 else W2b
                        kb.op(kb.pe, lambda ps=ps, f=f, n=n, W2x=W2x: nc.tensor.matmul(ps[:, 0:N], W2x[:, f % 16, n * 128:(n + 1) * 128], uT[:, f, :],
                                                                                      start=(f == 0), stop=(f == 31)), reads=[W2x, uT], writes=[ps])
                    kb.op(kb.dve, lambda ps=ps, n=n: nc.vector.scalar_tensor_tensor(out=x[:, n, :], in0=ps[:, 0:N], scalar=self.mvc(l, 5, n, cond), in1=x[:, n, :],
                                                                                   op0=ALU.mult, op1=ALU.add), reads=[ps, self.mv, xs[n]], writes=[xs[n]])

            def s3(c):
                self.ln_epilogue((sq, mean, var, B[6], B[7]), xT[c % 2], N, l, 1, self.XA, c * N, sub=xS[c % 2])

            load(0)
            s1(0)
            for c in range(nchunk):
                if c + 1 < nchunk:
                    load(c + 1)
                s2(c)
                if c + 1 < nchunk:
                    s1(c + 1)
                s3(c)
            kb.end_phase()

    def phase_out(self):
        nc, kb = self.nc, self.kb
        with ExitStack() as es:
            xT = [kb.tile(es, "xT%d" % i, [128, 8, 512], F32) for i in range(2)]
            yo = [kb.tile(es, "yo%d" % i, [128, 4, 1024], F32) for i in range(2)]
            for c in range(NTOK // 512):
                x = xT[c % 2]; y = yo[c % 2]
                t0 = c * 512
                kb.dma(kb.sp, x[:], self.XA[:, :, t0:t0 + 512].rearrange("k p n -> p k n"), reads=[self.XA], writes=[x])
                for j in range(4):
                    for kh in range(2):
                        bk = self.bank[(j * 2 + kh) % 4]
                        for kk in range(4):
                            k = kh * 4 + kk
                            kb.op(kb.pe, lambda bk=bk, j=j, k=k, kk=kk: nc.tensor.transpose(
                                bk[:, kk * 128:(kk + 1) * 128], x[:, k, j * 128:(j + 1) * 128], self.ident[:]),
                                reads=[x, self.ident], writes=[bk])
                        if kh == 0:
                            kb.op(kb.dve, lambda bk=bk, j=j, kh=kh: nc.vector.tensor_copy(out=y[:, j, kh * 512:(kh + 1) * 512], in_=bk[:, :]),
                                  reads=[bk], writes=[y])
                        else:
                            kb.op(kb.act, lambda bk=bk, j=j, kh=kh: nc.scalar.copy(out=y[:, j, kh * 512:(kh + 1) * 512], in_=bk[:, :]),
                                  reads=[bk], writes=[y])
                dst = self.O["yp"] if c < 2 else self.O["ys"]
                r0 = c * 512 if c < 2 else (c - 2) * 512
                kb.dma(kb.sp, dst[r0:r0 + 512, :].rearrange("(j p) f -> p j f", p=128), y[:], reads=[y], writes=[dst])
            kb.end_phase()


_PROG = {}


def _colT(v, ntile):
    return np.ascontiguousarray(np.asarray(v, np.float32).reshape(ntile, 128).T)


def make_in_maps(inp, ncores=8, nl=NLAYERS, names=None):
    f = lambda a: np.ascontiguousarray(np.asarray(a, dtype=np.float32))
    shared = {}
    nlw = max(nl, 1)
    shared["w_mod"] = f(inp["w_mod"][:nlw])
    bm = f(inp["b_mod"]).reshape(4, 6, 8, 128)
    shared["bmodT"] = np.ascontiguousarray(bm.transpose(0, 3, 1, 2).reshape(4, 128, 48))
    g = f(inp["ln_g"]).reshape(4, 2, 8, 128)
    b = f(inp["ln_b"]).reshape(4, 2, 8, 128)
    shared["lngT"] = np.ascontiguousarray(g.transpose(3, 0, 1, 2).reshape(128, 64))
    shared["lnbT"] = np.ascontiguousarray(b.transpose(3, 0, 1, 2).reshape(128, 64))
    shared["w_mlp1"] = f(inp["w_mlp1"][:nlw]); shared["w_mlp2"] = f(inp["w_mlp2"][:nlw])
    shared["l0_w_qkv"] = f(inp["l0_w_qkv"])
    shared["l0_lam_bc"] = np.ascontiguousarray(np.broadcast_to(f(inp["l0_lam"]).reshape(1, 256), (128, 256)))
    shared["l0_subln"] = f(inp["l0_subln"]).reshape(128, 1)
    shared["l0_w_o"] = f(inp["l0_w_o"])
    shared["l1_w_qkv"] = f(inp["l1_w_qkv"])
    E, M = _na_tables(f(inp["l1_rpb"]))
    shared["na_E"] = E; shared["na_M"] = M
    shared["l1_w_o"] = f(inp["l1_w_o"])
    shared["l2_w_a"] = f(inp["l2_w_a"])
    shared["l2_qnormT"] = _colT(inp["l2_q_norm"], 4)
    shared["l2_kvnormT"] = _colT(inp["l2_kv_norm"], 2)
    shared["l2_kvnorm_bc"] = np.ascontiguousarray(np.broadcast_to(f(inp["l2_kv_norm"]).reshape(1, 256), (128, 256)))
    shared["l2_w_uq"] = f(inp["l2_w_uq"]); shared["l2_w_ukv"] = f(inp["l2_w_ukv"]); shared["l2_w_o"] = f(inp["l2_w_o"])
    shared["l3_w_qkv"] = f(inp["l3_w_qkv"])
    shared["l3_sink_bc"] = np.ascontiguousarray(np.broadcast_to(f(inp["l3_sink"]).reshape(1, 16), (128, 16)))
    shared["l3_w_o"] = f(inp["l3_w_o"])
    shared["ident"] = np.eye(128, dtype=np.float32)
    shared["rt64"] = _rot_lhsT(64, [0, 64], 128)
    shared["rt32"] = _rot_lhsT(32, [0, 64], 128)
    c64, s64 = _rope_tables(64, [0, 64])
    c32, s32 = _rope_tables(32, [0, 64])
    shared["cos64"] = c64; shared["sin64"] = s64; shared["cos32"] = c32; shared["sin32"] = s32
    kk = np.arange(128)[:, None]; qq = np.arange(128)[None, :]
    shared["tri"] = np.stack([(qq <= kk), (kk <= qq)]).astype(np.float32)
    maps = []
    cctx = f(inp["c_ctx"])
    if names is not None:
        shared = {k: v for k, v in shared.items() if k in names}
    for i in range(ncores):
        d = dict(shared)
        d["xp"] = f(inp["x_prompt"][4 * i:4 * i + 4]).reshape(NPR, D)
        d["xs"] = f(inp["x_sample"][i]).reshape(NLAT, D)
        d["ck0"] = f(inp["cache_l0_k"][i]).reshape(PAST, 1024); d["cv0"] = f(inp["cache_l0_v"][i]).reshape(PAST, 1024)
        d["ck1"] = f(inp["cache_l1_k"][i]).reshape(PAST, 1024); d["cv1"] = f(inp["cache_l1_v"][i]).reshape(PAST, 1024)
        d["cckv"] = f(inp["cache_l2_ckv"][i]).reshape(PAST, 256); d["ckpe"] = f(inp["cache_l2_kpe"][i]).reshape(PAST, 32)
        d["ck3"] = f(inp["cache_l3_k"][i]).reshape(PAST, 256); d["cv3"] = f(inp["cache_l3_v"][i]).reshape(PAST, 256)
        d["condT"] = np.ascontiguousarray(np.stack([_colT(cctx, 8), _colT(f(inp["c"][i]), 8)], axis=-1))
        if names is not None:
            d = {k: v for k, v in d.items() if k in names}
        maps.append(d)
    return maps


def assemble(results, ncores=8):
    cat = lambda k: np.concatenate([np.asarray(r[k], np.float32) for r in results], axis=0)
    yp = cat("yp").reshape(4 * ncores, 256, D)
    ys = cat("ys").reshape(ncores, NLAT, D)
    k0 = cat("k0").reshape(4 * ncores, 256, 8, 128); v0 = cat("v0").reshape(4 * ncores, 256, 8, 128)
    k1 = cat("k1").reshape(4 * ncores, 256, 16, 64); v1 = cat("v1").reshape(4 * ncores, 256, 16, 64)
    ckv2 = cat("ckv2").reshape(4 * ncores, 256, 256); kpe2 = cat("kpe2").reshape(4 * ncores, 256, 32)
    k3 = cat("k3").reshape(4 * ncores, 256, 4, 64); v3 = cat("v3").reshape(4 * ncores, 256, 4, 64)
    return (yp, ys, k0, v0, k1, v1, ckv2, kpe2, k3, v3)


def kernel(**inputs):
    if "p" not in _PROG:
        _PROG["p"] = Prog(NLAYERS)
    prog = _PROG["p"]
    maps = make_in_maps(inputs, 8, NLAYERS, set(prog.I.keys()))
    res = run_bass_kernel_spmd(prog.nc, maps, core_ids=list(range(8)))
    return assemble(res.results, 8)
```
